# Optimizing a Trainium2 kernel written in Bass

```python
import math
import jax, jax.numpy as jnp
from jax import lax
import numpy as np

D_MODEL = 1024
BATCH = 8
SEQ = 4096
DEPTH = 2

MLA_HEADS = 8
MLA_Q_LORA = 256
MLA_KV_LORA = 256
MLA_NOPE = 64
MLA_ROPE = 32
MLA_V = 64
MLA_SCALE = (MLA_NOPE + MLA_ROPE) ** -0.5
ROPE_BASE = 10000.0
Q_BLOCK = 128
MAX_POS_OFFSET = 1024

SGU_GROUPS = 4
SGU_GROUP_DIM = 128
SGU_DIM = SGU_GROUPS * SGU_GROUP_DIM
SGU_CHUNK = 128

EVEN_IN = MLA_Q_LORA + MLA_KV_LORA + MLA_ROPE + 2 * SGU_DIM
EVEN_SPLITS = (MLA_Q_LORA,
               MLA_Q_LORA + MLA_KV_LORA,
               MLA_Q_LORA + MLA_KV_LORA + MLA_ROPE,
               MLA_Q_LORA + MLA_KV_LORA + MLA_ROPE + SGU_DIM)
EVEN_MIX = MLA_HEADS * MLA_V + SGU_DIM

HG_HEADS = 8
HG_DK = 128
HG_DV = D_MODEL // HG_HEADS
HG_KEY_DIM = HG_HEADS * HG_DK
HG_VAL_DIM = HG_HEADS * HG_DV
HG_CHUNK = 64
ODD_IN = 2 * HG_KEY_DIM + 2 * HG_VAL_DIM
ODD_SPLITS = (HG_KEY_DIM, 2 * HG_KEY_DIM, 2 * HG_KEY_DIM + HG_VAL_DIM)

D_FF = 4 * D_MODEL

N_EVEN = (DEPTH + 1) // 2
N_ODD = DEPTH // 2
DN_ALPHA = (2 * DEPTH) ** 0.25
DN_BETA = (8 * DEPTH) ** -0.25
NORM_EPS = 1e-5

kernel_name = 'hybrid_mla_sgu_hgrn2_deepnorm'


def layer_norm(x, g, b):
    xf = x.astype(jnp.float32)
    mu = jnp.mean(xf, -1, keepdims=True)
    var = jnp.mean(jnp.square(xf - mu), -1, keepdims=True)
    y = (xf - mu) * lax.rsqrt(var + NORM_EPS)
    return (y * g.astype(jnp.float32) + b.astype(jnp.float32)).astype(x.dtype)


def rms_norm(x, g):
    xf = x.astype(jnp.float32)
    y = xf * lax.rsqrt(jnp.mean(jnp.square(xf), -1, keepdims=True) + NORM_EPS)
    return (y * g.astype(jnp.float32)).astype(x.dtype)


def apply_rope(x, cos, sin):
    half = MLA_ROPE // 2
    xf = x.astype(jnp.float32)
    x1, x2 = xf[..., :half], xf[..., half:]
    out = jnp.concatenate([x1 * cos - x2 * sin, x2 * cos + x1 * sin], axis=-1)
    return out.astype(x.dtype)


def mla(c_q, c_kv, k_rope, positions, g_q, g_kv, w_qb, w_kvb):
    B, S, _ = c_q.shape
    q = (rms_norm(c_q, g_q) @ w_qb).reshape(B, S, MLA_HEADS, MLA_NOPE + MLA_ROPE)
    q_nope, q_rope = q[..., :MLA_NOPE], q[..., MLA_NOPE:]
    kv = (rms_norm(c_kv, g_kv) @ w_kvb).reshape(B, S, MLA_HEADS, MLA_NOPE + MLA_V)
    k_nope, v = kv[..., :MLA_NOPE], kv[..., MLA_NOPE:]
    half = MLA_ROPE // 2
    inv_freq = ROPE_BASE ** (-jnp.arange(half, dtype=jnp.float32) / half)
    ang = positions.astype(jnp.float32)[..., None] * inv_freq
    cos, sin = jnp.cos(ang), jnp.sin(ang)
    q_rope = apply_rope(q_rope, cos[:, :, None, :], sin[:, :, None, :])
    k_rope = apply_rope(k_rope, cos, sin)
    nb = S // Q_BLOCK
    qn_b = q_nope.reshape(B, nb, Q_BLOCK, MLA_HEADS, MLA_NOPE).transpose(1, 0, 2, 3, 4)
    qr_b = q_rope.reshape(B, nb, Q_BLOCK, MLA_HEADS, MLA_ROPE).transpose(1, 0, 2, 3, 4)
    key_idx = jnp.arange(S)

    def block(args):
        qn, qr, bi = args
        s = (jnp.einsum('bqhd,bkhd->bhqk', qn, k_nope)
             + jnp.einsum('bqhr,bkr->bhqk', qr, k_rope)).astype(jnp.float32) * MLA_SCALE
        q_idx = bi * Q_BLOCK + jnp.arange(Q_BLOCK)
        mask = key_idx[None, :] <= q_idx[:, None]
        s = jnp.where(mask[None, None], s, -jnp.inf)
        p = jax.nn.softmax(s, axis=-1).astype(v.dtype)
        return jnp.einsum('bhqk,bkhd->bqhd', p, v)

    out = lax.map(block, (qn_b, qr_b, jnp.arange(nb)))
    return out.transpose(1, 0, 2, 3, 4).reshape(B, S, MLA_HEADS * MLA_V)


def sgu(u, v, ln_g, ln_b, w_s, b_s):
    B, S, _ = u.shape
    u = jax.nn.gelu(u)
    v = layer_norm(jax.nn.gelu(v), ln_g, ln_b)
    nc = S // SGU_CHUNK
    vc = v.reshape(B, nc, SGU_CHUNK, SGU_GROUPS, SGU_GROUP_DIM)
    causal = jnp.tril(jnp.ones((SGU_CHUNK, SGU_CHUNK), dtype=bool))
    w = jnp.where(causal[None], w_s, jnp.zeros_like(w_s))
    mixed = jnp.einsum('gts,bnsgc->bntgc', w, vc) + b_s.T[:, :, None]
    return u * mixed.reshape(B, S, SGU_DIM)


def hgrn2(q, f, i, g, lb, g_norm):
    B, S, _ = q.shape
    nc = S // HG_CHUNK
    qf = jax.nn.silu(q.astype(jnp.float32))
    gate = lb + (1.0 - lb) * jax.nn.sigmoid(f.astype(jnp.float32))
    k = 1.0 - gate
    log_g = jnp.log(gate)
    vf = i.astype(jnp.float32)

    def chunks(t, d):
        return t.reshape(B, nc, HG_CHUNK, HG_HEADS, d).transpose(1, 0, 3, 2, 4)

    xs = (chunks(qf, HG_DK), chunks(k, HG_DK), chunks(vf, HG_DV), chunks(log_g, HG_DK))
    tri = jnp.tril(jnp.ones((HG_CHUNK, HG_CHUNK), dtype=bool))[:, :, None]

    def step(state, inp):
        qc, kc, vc, lg = inp
        bcum = jnp.cumsum(lg, axis=2)
        diff = bcum[:, :, :, None, :] - bcum[:, :, None, :, :]
        decay = jnp.exp(jnp.where(tri, diff, -jnp.inf))
        attn = jnp.einsum('bhtd,bhsd,bhtsd->bhts', qc, kc, decay)
        o = (jnp.einsum('bhts,bhsv->bhtv', attn, vc)
             + jnp.einsum('bhtd,bhdv->bhtv', qc * jnp.exp(bcum), state))
        b_last = bcum[:, :, -1:, :]
        k_dec = kc * jnp.exp(b_last - bcum)
        new_state = (jnp.exp(b_last[:, :, 0, :])[..., None] * state
                     + jnp.einsum('bhsd,bhsv->bhdv', k_dec, vc))
        return new_state, o

    state0 = jnp.zeros((B, HG_HEADS, HG_DK, HG_DV), jnp.float32)
    _, o = lax.scan(step, state0, xs)
    o = o.transpose(1, 0, 3, 2, 4).reshape(B, S, HG_HEADS, HG_DV)
    o = o * lax.rsqrt(jnp.mean(jnp.square(o), -1, keepdims=True) + NORM_EPS)
    o = o * g_norm.astype(jnp.float32).reshape(HG_HEADS, HG_DV)
    o = o * jax.nn.silu(g.astype(jnp.float32).reshape(B, S, HG_HEADS, HG_DV))
    return o.reshape(B, S, HG_VAL_DIM).astype(i.dtype)


def setup_inputs(seed: int = 0) -> dict:
    key = jax.random.key(seed)
    ks = jax.random.split(key, 24)

    def nrm(k, shape, scale):
        return jax.random.normal(k, shape, jnp.float32) * scale

    def gain(k, shape):
        return 1.0 + 0.01 * jax.random.normal(k, shape, jnp.float32)

    x = jax.random.normal(ks[0], (BATCH, SEQ, D_MODEL), jnp.float32)
    offs = jax.random.randint(ks[1], (BATCH, 1), 0, MAX_POS_OFFSET, dtype=jnp.int32)
    positions = (offs + jnp.arange(SEQ, dtype=jnp.int32)[None, :]).astype(jnp.int32)
    return {
        'x': x,
        'positions': positions,
        'w_in_e': nrm(ks[2], (N_EVEN, D_MODEL, EVEN_IN), D_MODEL ** -0.5),
        'mla_gq': gain(ks[3], (N_EVEN, MLA_Q_LORA)),
        'mla_gkv': gain(ks[4], (N_EVEN, MLA_KV_LORA)),
        'w_qb': nrm(ks[5], (N_EVEN, MLA_Q_LORA, MLA_HEADS * (MLA_NOPE + MLA_ROPE)), MLA_Q_LORA ** -0.5),
        'w_kvb': nrm(ks[6], (N_EVEN, MLA_KV_LORA, MLA_HEADS * (MLA_NOPE + MLA_V)), MLA_KV_LORA ** -0.5),
        'sgu_ln_g': gain(ks[7], (N_EVEN, SGU_DIM)),
        'sgu_ln_b': nrm(ks[8], (N_EVEN, SGU_DIM), 0.01),
        'sgu_w': nrm(ks[9], (N_EVEN, SGU_GROUPS, SGU_CHUNK, SGU_CHUNK), SGU_CHUNK ** -0.5),
        'sgu_b': gain(ks[10], (N_EVEN, SGU_GROUPS, SGU_CHUNK)),
        'w_out_e': nrm(ks[11], (N_EVEN, EVEN_MIX, D_MODEL), DN_BETA * EVEN_MIX ** -0.5),
        'w_in_o': nrm(ks[12], (N_ODD, D_MODEL, ODD_IN), D_MODEL ** -0.5),
        'hg_lb': nrm(ks[13], (DEPTH, HG_KEY_DIM), 0.1),
        'hg_gnorm': gain(ks[14], (N_ODD, HG_VAL_DIM)),
        'w_out_o': nrm(ks[15], (N_ODD, HG_VAL_DIM, D_MODEL), DN_BETA * HG_VAL_DIM ** -0.5),
        'ln1_g': gain(ks[16], (DEPTH, D_MODEL)),
        'ln1_b': nrm(ks[17], (DEPTH, D_MODEL), 0.01),
        'w_ff1': nrm(ks[18], (DEPTH, D_MODEL, D_FF), DN_BETA * D_MODEL ** -0.5),
        'w_ff2': nrm(ks[19], (DEPTH, D_FF, D_MODEL), DN_BETA * D_FF ** -0.5),
        'ln2_g': gain(ks[20], (DEPTH, D_MODEL)),
        'ln2_b': nrm(ks[21], (DEPTH, D_MODEL), 0.01),
    }


def reference(x, positions, w_in_e, mla_gq, mla_gkv, w_qb, w_kvb, sgu_ln_g, sgu_ln_b,
              sgu_w, sgu_b, w_out_e, w_in_o, hg_lb, hg_gnorm, w_out_o,
              ln1_g, ln1_b, w_ff1, w_ff2, ln2_g, ln2_b):
    lb_sm = jax.nn.softmax(hg_lb.astype(jnp.float32), axis=0)
    lb_all = jnp.cumsum(lb_sm, axis=0) - lb_sm[0:1]
    h = x
    for l in range(DEPTH):
        if l % 2 == 0:
            e = l // 2
            z = h @ w_in_e[e]
            c_q, c_kv, k_r, u, v = jnp.split(z, EVEN_SPLITS, axis=-1)
            a_out = mla(c_q, c_kv, k_r, positions, mla_gq[e], mla_gkv[e], w_qb[e], w_kvb[e])
            b_out = sgu(u, v, sgu_ln_g[e], sgu_ln_b[e], sgu_w[e], sgu_b[e])
            mix = jnp.concatenate([a_out, b_out], axis=-1) @ w_out_e[e]
        else:
            o = l // 2
            z = h @ w_in_o[o]
            q, f, i, g = jnp.split(z, ODD_SPLITS, axis=-1)
            mix = hgrn2(q, f, i, g, lb_all[l], hg_gnorm[o]) @ w_out_o[o]
        h = layer_norm(DN_ALPHA * h + mix, ln1_g[l], ln1_b[l])
        ff = jnp.square(jax.nn.relu(h @ w_ff1[l])) @ w_ff2[l]
        h = layer_norm(DN_ALPHA * h + ff, ln2_g[l], ln2_b[l])
    return h
```

```python
import bisect
import math
from contextlib import ExitStack

import numpy as np
import ml_dtypes
import concourse.bass as bass
import concourse.mybir as mybir
from concourse.bass_utils import run_bass_kernel_spmd

F32 = mybir.dt.float32
BF16 = mybir.dt.bfloat16
I32 = mybir.dt.int32
AF = mybir.ActivationFunctionType
ALU = mybir.AluOpType

D = 1024
DFF = 4096
DEPTH = 2
ALPHA = float((2 * DEPTH) ** 0.25)
EPS = 1e-5
MLA_SCALE = float(96 ** -0.5)
EVEN_IN = 1568
TG = 512
PI = math.pi
import os
VAR = int(os.environ.get('VAR', '0'))


class Tok:
    __slots__ = ("sem", "val", "step", "hv", "hc", "name")

    def __init__(self, nc, name, step):
        self.sem = nc.alloc_semaphore(name)
        self.val = 0
        self.step = step
        self.hv = []
        self.hc = []
        self.name = name


class Res:
    __slots__ = ("w", "r", "name", "excl")

    def __init__(self, name="", excl=False):
        self.w = None
        self.r = {}
        self.name = name
        self.excl = excl


class Eng:
    def __init__(self, nc, e, name, raw):
        self.e = e
        self.name = name
        self.raw = raw
        self.tok = Tok(nc, "t_" + name, 1)
        self.clock = {}
        self.dirty = False
        self.ring = []
        self.ri = 0


class Sched:
    def __init__(self, nc, nring=10):
        self.nc = nc
        self.engs = {
            "pe": Eng(nc, nc.tensor, "pe", False),
            "act": Eng(nc, nc.scalar, "act", True),
            "dve": Eng(nc, nc.vector, "dve", True),
            "pool": Eng(nc, nc.gpsimd, "pool", True),
            "sp": Eng(nc, nc.sync, "sp", False),
        }
        self.dtoks = []
        for q in ("sp", "pool"):
            for i in range(nring):
                t = Tok(nc, f"d_{q}{i}", 16)
                self.engs[q].ring.append(t)
                self.dtoks.append(t)
        self.nins = 0
        self.nwait = 0

    def _deps(self, E, reads, writes):
        deps = {}
        for r in reads:
            if r.w is not None:
                t, v = r.w
                if v > deps.get(t, 0):
                    deps[t] = v
            if r.excl:
                for t, v in r.r.items():
                    if t is not E.tok and v > deps.get(t, 0):
                        deps[t] = v
        for w in writes:
            if w.w is not None:
                t, v = w.w
                if v > deps.get(t, 0):
                    deps[t] = v
            for t, v in w.r.items():
                if v > deps.get(t, 0):
                    deps[t] = v
        waits = []
        own = None
        for t, v in deps.items():
            if t is E.tok:
                if E.raw and v > E.clock.get(t, 0):
                    own = (t, v)
                continue
            if E.clock.get(t, 0) >= v:
                continue
            waits.append((t, v))
        if own is not None:
            waits.append(own)
        return waits

    def _note(self, E, t, v):
        if E.clock.get(t, 0) < v:
            E.clock[t] = v
        i = bisect.bisect_right(t.hv, v) - 1
        if i >= 0:
            for t2, v2 in t.hc[i].items():
                if E.clock.get(t2, 0) < v2:
                    E.clock[t2] = v2
        E.dirty = True

    def _emit_waits(self, E, waits):
        last = None
        if waits:
            for t, v in waits[:-1]:
                E.e.wait_ge(t.sem, v)
                self.nwait += 1
            last = waits[-1]
            for t, v in waits:
                self._note(E, t, v)
        return last

    def op(self, eng, fn, reads=(), writes=(), inc=True):
        E = self.engs[eng]
        waits = self._deps(E, reads, writes)
        last = self._emit_waits(E, waits)
        ins = fn(E.e)
        self.nins += 1
        if last is not None:
            ins._wait_ge(last[0].sem, last[1])
        tok = E.tok
        if inc:
            ins.then_inc(tok.sem, 1)
            tok.val += 1
            cv = tok.val
        else:
            cv = tok.val + 1
        if E.dirty:
            tok.hv.append(cv)
            tok.hc.append(dict(E.clock))
            E.dirty = False
        for r in reads:
            if r.r.get(tok, 0) < cv:
                r.r[tok] = cv
        for w in writes:
            w.w = (tok, cv)
            w.r = {}
        return ins

    def dma(self, q, out, in_, reads=(), writes=()):
        E = self.engs[q]
        tok = E.ring[E.ri]
        E.ri = (E.ri + 1) % len(E.ring)
        waits = self._deps(E, reads, writes)
        if tok.val and E.clock.get(tok, 0) < tok.val:
            waits = [(t, v) for (t, v) in waits if t is not tok] + [(tok, tok.val)]
        last = self._emit_waits(E, waits)
        ins = E.e.dma_start(out=out, in_=in_)
        self.nins += 1
        if last is not None:
            ins._wait_ge(last[0].sem, last[1])
        ins.then_inc(tok.sem, 16)
        tok.val += 16
        cv = tok.val
        tok.hv.append(cv)
        tok.hc.append(dict(E.clock))
        for r in reads:
            if r.r.get(tok, 0) < cv:
                r.r[tok] = cv
        for w in writes:
            w.w = (tok, cv)
            w.r = {}
        return ins

    def barrier(self):
        toks = [E.tok for E in self.engs.values()] + self.dtoks
        for E in self.engs.values():
            for t in toks:
                if t is E.tok or t.val == 0:
                    continue
                if E.clock.get(t, 0) < t.val:
                    E.e.wait_ge(t.sem, t.val)
                    self._note(E, t, t.val)

    def finish(self):
        E = self.engs["sp"]
        for t in self.dtoks:
            if t.val and E.clock.get(t, 0) < t.val:
                E.e.wait_ge(t.sem, t.val)


class FreePool:
    def __init__(self, items):
        self.items = list(items)

    def avail(self, n=1):
        return len(self.items) >= n

    def get(self):
        return self.items.pop(0)

    def put(self, it):
        self.items.append(it)


class Ring:
    def __init__(self, items):
        self.items = items
        self.i = 0

    def next(self):
        it = self.items[self.i]
        self.i = (self.i + 1) % len(self.items)
        return it


def build(S_len=4096, stop_after=99, debug=False, cut=99):
    S = S_len
    NG = S // TG
    NT = S // 128
    nc = bass.Bass("TRN2", target_bir_lowering=False)
    sc = Sched(nc)

    def din(name, shape, dt=F32):
        return nc.dram_tensor(name, list(shape), dt, kind="ExternalInput").ap()

    xT = din("xT", [D, S])
    pos = din("pos", [1, S], I32)
    w_in_e = din("w_in_e", [D, EVEN_IN])
    w_qb = din("w_qb", [256, 768])
    w_kvb = din("w_kvb", [256, 1024])
    w_out_e = din("w_out_e", [D, D])
    sgu_wT = din("sgu_wT", [128, 4, 128])
    sgu_b = din("sgu_b", [1, 512])
    sgu_g = din("sgu_g", [1, 512])
    sgu_bb = din("sgu_bb", [1, 512])
    gq_in = din("gq", [128, 2])
    gkv_in = din("gkv", [128, 2])
    w_in_o = din("w_in_o", [D, 4096])
    w_out_o = din("w_out_o", [D, D])
    lb_in = din("hg_lb", [128, 2, 8])
    gn_in = din("hg_gn", [128, 8])
    lnp_in = din("lnp", [128, 4, 2, 8])
    w_ff1 = din("w_ff1", [2, D, DFF])
    w_ff2 = din("w_ff2", [2, DFF, D])
    invf_in = din("invf", [128, 1])
    mask_in = din("mask_ge", [128, 128])
    ident_in = din("ident", [128, 128])
    out = nc.dram_tensor("out", [D, S], F32, kind="ExternalOutput").ap()
    ikind = "ExternalOutput" if debug else "Internal"
    mixin = nc.dram_tensor("mixin", [D, S], BF16, kind=ikind).ap()
    hA = nc.dram_tensor("hA", [D, S], F32, kind=ikind).ap()
    hB = nc.dram_tensor("hB", [D, S], F32, kind=ikind).ap()
    hC = nc.dram_tensor("hC", [D, S], F32, kind=ikind).ap()
    r_mix = [Res(f"mix{g}") for g in range(NG)]
    r_hA = [Res(f"hA{g}") for g in range(NG)]
    r_hB = [Res(f"hB{g}") for g in range(NG)]
    r_hC = [Res(f"hC{g}") for g in range(NG)]

    def fm(ap):
        return ap.rearrange("(k p) s -> p k s", p=128)

    def gsl(g):
        return slice(g * TG, (g + 1) * TG)

    top = ExitStack()

    cnt = [0]

    def sb(es, name, shape, dt):
        cnt[0] += 1
        return es.enter_context(nc.sbuf_tensor(f"s{cnt[0]}_{name}", list(shape), dt))

    psum = [top.enter_context(nc.psum_tensor(f"ps{i}", [128, 512], F32)) for i in range(8)]
    rps = [Res(f"ps{i}", excl=True) for i in range(8)]

    def psring(idx):
        return Ring([(psum[i], rps[i]) for i in idx])

    ones_bf = sb(top, "ones_bf", [128, 128], BF16)
    ones_f = sb(top, "ones_f", [128, 128], F32)
    mask_f = sb(top, "mask_f", [128, 128], F32)
    mask_b = sb(top, "mask_b", [128, 128], BF16)
    ident_b = sb(top, "ident_b", [128, 128], BF16)
    lnp = sb(top, "lnp", [128, 4, 2, 8], F32)
    r_const = Res("const")
    sc.op("pool", lambda e: e.memset(ones_bf[:], 1.0), writes=[r_const])
    sc.op("pool", lambda e: e.memset(ones_f[:], 1.0), writes=[r_const])
    sc.dma("sp", mask_f[:], mask_in, writes=[r_const])
    sc.dma("pool", mask_b[:], mask_in, writes=[r_const])
    sc.dma("pool", ident_b[:], ident_in, writes=[r_const])
    sc.dma("sp", lnp[:], lnp_in, writes=[r_const])

    def layer_norm(es_tmp, buf, rbuf, layer, which, psr, tmps):
        for _ in layer_norm_gen(es_tmp, buf, rbuf, layer, which, psr, tmps):
            pass

    def layer_norm_gen(es_tmp, buf, rbuf, layer, which, psr, tmps):
        (rb_ring, rs_ring, mean, msq, rstd, mr) = tmps
        gi, bi = (0, 1) if which == 1 else (2, 3)
        p_sum, r_sum = psr.next()
        p_sq, r_sq = psr.next()
        def stats_mm(c, rb, rrb, rs, rrs):
            sc.op("pe", lambda e: e.matmul(p_sum[:, :], lhsT=ones_bf[:], rhs=rb[:], start=(c == 0), stop=(c == 7)),
                  reads=[rrb, r_const], writes=[r_sum], inc=True)
            sc.op("pe", lambda e: e.matmul(p_sq[:, :], lhsT=ones_bf[:], rhs=rs[:], start=(c == 0), stop=(c == 7)),
                  reads=[rrs, r_const], writes=[r_sq], inc=True)

        prev = None
        for c in range(8):
            if prev is not None:
                stats_mm(*prev)
            rb, rrb = rb_ring.next()
            rs, rrs = rs_ring.next()
            sc.op("act", lambda e: e.copy(out=rb[:], in_=buf[:, c, :]), reads=[rbuf], writes=[rrb])
            sc.op("act", lambda e: e.activation(out=rs[:], in_=buf[:, c, :], func=AF.Square), reads=[rbuf], writes=[rrs])
            prev = (c, rb, rrb, rs, rrs)
            yield
            yield
        stats_mm(*prev)
        yield
        (mean_t, r_mean), (msq_t, r_msq), (rstd_t, r_rstd), (mr_t, r_mr) = mean, msq, rstd, mr
        sc.op("act", lambda e: e.activation(out=mean_t[:], in_=p_sum[:, :], func=AF.Copy, scale=1.0 / D), reads=[r_sum], writes=[r_mean])
        sc.op("dve", lambda e: e.tensor_tensor(out=msq_t[:], in0=mean_t[:], in1=mean_t[:], op=ALU.mult), reads=[r_mean], writes=[r_msq])
        sc.op("dve", lambda e: e.scalar_tensor_tensor(out=msq_t[:], in0=p_sq[:, :], scalar=1.0 / D, in1=msq_t[:], op0=ALU.mult, op1=ALU.subtract),
              reads=[r_sq, r_msq], writes=[r_msq])
        sc.op("dve", lambda e: e.tensor_scalar(out=msq_t[:], in0=msq_t[:], scalar1=EPS, scalar2=None, op0=ALU.add), reads=[r_msq], writes=[r_msq])
        sc.op("act", lambda e: e.activation(out=rstd_t[:], in_=msq_t[:], func=AF.Ln), reads=[r_msq], writes=[r_rstd])
        sc.op("act", lambda e: e.activation(out=rstd_t[:], in_=rstd_t[:], func=AF.Exp, scale=-0.5), reads=[r_rstd], writes=[r_rstd])
        mr_t, r_mr = mean_t, r_mean
        sc.op("dve", lambda e: e.tensor_tensor(out=mr_t[:], in0=mean_t[:], in1=rstd_t[:], op=ALU.mult), reads=[r_mean, r_rstd], writes=[r_mr])
        yield
        for c in range(8):
            sc.op("dve", lambda e: e.tensor_tensor(out=buf[:, c, :], in0=buf[:, c, :], in1=rstd_t[:], op=ALU.mult), reads=[rbuf, r_rstd], writes=[rbuf])
            sc.op("dve", lambda e: e.tensor_tensor(out=buf[:, c, :], in0=buf[:, c, :], in1=mr_t[:], op=ALU.subtract), reads=[rbuf, r_mr], writes=[rbuf])
            sc.op("act", lambda e: e.activation(out=buf[:, c, :], in_=buf[:, c, :], func=AF.Identity,
                                                scale=lnp[:, gi, layer, c:c + 1], bias=lnp[:, bi, layer, c:c + 1]),
                  reads=[rbuf, r_const], writes=[rbuf])
            yield

    def ln_tmps(es):
        rb_ring = Ring([(sb(es, f"ln_rb{i}", [128, TG], BF16), Res(f"ln_rb{i}")) for i in range(2)])
        rs_ring = Ring([(sb(es, f"ln_rs{i}", [128, TG], BF16), Res(f"ln_rs{i}")) for i in range(2)])
        t = [(sb(es, f"ln_t{i}", [128, TG], F32), Res(f"ln_t{i}")) for i in range(3)]
        return (rb_ring, rs_ring, t[0], t[1], t[2], t[0])

    def load_w(dst, src_ap, rlist, nparts, axis_len, q="pool"):
        step = axis_len // nparts
        for i in range(nparts):
            sc.dma(q, dst[:, :, i * step:(i + 1) * step], src_ap[:, :, i * step:(i + 1) * step], writes=[rlist[i]])

    es01 = ExitStack()
    cqn = sb(es01, "cqn", [128, 2, S], BF16)
    ckvn = sb(es01, "ckvn", [128, 2, S], BF16)
    krot = sb(es01, "krot", [96, S], BF16)
    cosT = sb(es01, "cosT", [96, S], F32)
    sinT = sb(es01, "sinT", [96, S], F32)
    r_cqn = [Res(f"cqn{g}") for g in range(NG)]
    r_ckvn = [Res(f"ckvn{g}") for g in range(NG)]
    r_krot = [Res(f"krot{g}") for g in range(NG)]
    r_trig = Res("trig")

    es = ExitStack()
    w_in = sb(es, "w_in", [128, 8, EVEN_IN], BF16)
    w_kr = sb(es, "w_kr", [128, 8, 2, 96], BF16)
    wsg = sb(es, "wsg", [128, 4, 128], BF16)
    wsg_f = sb(es, "wsg_f", [128, 4, 128], F32)
    lnG = sb(es, "lnG", [128, 512], F32)
    lnB = sb(es, "lnB", [128, 512], F32)
    bsg = sb(es, "bsg", [1, 512], BF16)
    gq = sb(es, "gq", [128, 2], F32)
    gkv = sb(es, "gkv", [128, 2], F32)
    invf = sb(es, "invf", [128, 1], F32)
    r_win = [Res(f"w_in{i}") for i in range(2)]
    r_p0c = Res("p0c")
    w_in_v = w_in_e.rearrange("(k p) n -> p k n", p=128)
    sc.dma("pool", w_in[:, :, 0:544], w_in_v[:, :, 0:544], writes=[r_win[0]])
    sc.dma("pool", w_in[:, :, 544:EVEN_IN], w_in_v[:, :, 544:EVEN_IN], writes=[r_win[1]])
    sc.dma("sp", wsg_f[:], sgu_wT, writes=[r_p0c])
    sc.dma("sp", lnG[:], bass.AP(sgu_g.tensor, 0, [[0, 128], [1, 512]]), writes=[r_p0c])
    sc.dma("sp", lnB[:], bass.AP(sgu_bb.tensor, 0, [[0, 128], [1, 512]]), writes=[r_p0c])
    sc.dma("pool", bsg[:], sgu_b, writes=[r_p0c])
    sc.dma("sp", gq[:], gq_in, writes=[r_p0c])
    sc.dma("sp", gkv[:], gkv_in, writes=[r_p0c])
    sc.dma("sp", invf[:], invf_in, writes=[r_p0c])
    for g4 in range(4):
        sc.op("dve", lambda e: e.tensor_tensor(out=wsg[:, g4, :], in0=wsg_f[:, g4, :], in1=mask_f[:], op=ALU.mult), reads=[r_p0c, r_const], writes=[r_p0c])
    r_wkr = Res("w_kr")
    sc.op("pool", lambda e: e.memset(w_kr[:], 0.0), writes=[r_wkr])
    sc.op("pool", lambda e: e.tensor_copy(out=w_kr[:, :, 0, 64:96], in_=w_in[:, :, 512:544]), reads=[r_win[0]], writes=[r_wkr])
    sc.op("pool", lambda e: e.tensor_scalar(out=w_kr[:, :, 1, 64:80], in0=w_in[:, :, 528:544], scalar1=-1.0, scalar2=None, op0=ALU.mult), reads=[r_win[0]], writes=[r_wkr])
    sc.op("pool", lambda e: e.tensor_copy(out=w_kr[:, :, 1, 80:96], in_=w_in[:, :, 512:528]), reads=[r_win[0]], writes=[r_wkr])

    if stop_after == -3:
        sc.barrier()
        es.close()
        es01.close()
        return _finish(nc, sc, top)
    es_trig = ExitStack()
    posi = sb(es_trig, "posi", [96, S], I32)
    ang = sb(es_trig, "ang", [96, S], F32)
    kf = sb(es_trig, "kf", [96, S], F32)
    r_tr = Res("trtmp")
    P = slice(64, 96)
    sc.dma("sp", posi[P, :], bass.AP(pos.tensor, 0, [[0, 32], [1, S]]), writes=[r_tr])
    sc.op("dve", lambda e: e.tensor_copy(out=ang[P, :], in_=posi[P, :]), reads=[r_tr], writes=[r_tr])
    sc.op("dve", lambda e: e.tensor_scalar(out=ang[P, :], in0=ang[P, :], scalar1=invf[P, 0:1], scalar2=None, op0=ALU.mult), reads=[r_tr, r_p0c], writes=[r_tr])
    sc.op("dve", lambda e: e.tensor_scalar(out=kf[P, :], in0=ang[P, :], scalar1=float(1.0 / (2 * PI)), scalar2=None, op0=ALU.mult), reads=[r_tr], writes=[r_tr])
    sc.op("dve", lambda e: e.tensor_copy(out=posi[P, :], in_=kf[P, :]), reads=[r_tr], writes=[r_tr])
    sc.op("dve", lambda e: e.tensor_copy(out=kf[P, :], in_=posi[P, :]), reads=[r_tr], writes=[r_tr])
    C1 = 6.28125
    C2 = float(2 * PI - C1)
    sc.op("dve", lambda e: e.scalar_tensor_tensor(out=ang[P, :], in0=kf[P, :], scalar=-C1, in1=ang[P, :], op0=ALU.mult, op1=ALU.add), reads=[r_tr], writes=[r_tr])
    sc.op("dve", lambda e: e.scalar_tensor_tensor(out=ang[P, :], in0=kf[P, :], scalar=-C2, in1=ang[P, :], op0=ALU.mult, op1=ALU.add), reads=[r_tr], writes=[r_tr])

    def wrap(t):
        sc.op("dve", lambda e: e.tensor_scalar(out=kf[P, :], in0=t[P, :], scalar1=float(PI), scalar2=float(2 * PI), op0=ALU.is_gt, op1=ALU.mult), reads=[r_tr], writes=[r_tr])
        sc.op("dve", lambda e: e.tensor_tensor(out=t[P, :], in0=t[P, :], in1=kf[P, :], op=ALU.subtract), reads=[r_tr], writes=[r_tr])
        sc.op("dve", lambda e: e.tensor_scalar(out=kf[P, :], in0=t[P, :], scalar1=float(-PI), scalar2=float(2 * PI), op0=ALU.is_lt, op1=ALU.mult), reads=[r_tr], writes=[r_tr])
        sc.op("dve", lambda e: e.tensor_tensor(out=t[P, :], in0=t[P, :], in1=kf[P, :], op=ALU.add), reads=[r_tr], writes=[r_tr])

    wrap(ang)
    sc.op("act", lambda e: e.activation(out=sinT[P, :], in_=ang[P, :], func=AF.Sin), reads=[r_tr], writes=[r_trig])
    sc.op("dve", lambda e: e.tensor_scalar(out=ang[P, :], in0=ang[P, :], scalar1=float(PI / 2), scalar2=None, op0=ALU.add), reads=[r_tr], writes=[r_tr])
    wrap(ang)
    sc.op("act", lambda e: e.activation(out=cosT[P, :], in_=ang[P, :], func=AF.Sin), reads=[r_tr], writes=[r_trig])
    sc.barrier()
    es_trig.close()

    if stop_after == -2:
        sc.dma("sp", out[0:32, :], sinT[P, :], reads=[r_trig])
        sc.dma("sp", out[32:64, :], cosT[P, :], reads=[r_trig])
        sc.barrier()
        es.close()
        es01.close()
        return _finish(nc, sc, top)
    xb_ring = Ring([(sb(es, f"xb{i}", [128, 8, TG], BF16), Res(f"xb{i}")) for i in range(2)])
    cq_ring = Ring([(sb(es, f"cq{i}", [128, 2, TG], F32), Res(f"cq{i}")) for i in range(2)])
    sq_ring = Ring([(sb(es, f"sq{i}", [128, 2, TG], BF16), Res(f"sq{i}")) for i in range(2)])
    f_ring = Ring([(sb(es, f"ft{i}", [128, TG], F32), Res(f"ft{i}")) for i in range(12)])
    gu_ring = Ring([(sb(es, f"gu{i}", [128, 4, TG], F32), [Res(f"gu{i}_{j}") for j in range(4)]) for i in range(2)])
    bo_ring = Ring([(sb(es, f"bo{i}", [128, 4, TG], BF16), Res(f"bo{i}")) for i in range(2)])
    vnb_ring = Ring([(sb(es, f"vnb{i}", [128, TG], BF16), Res(f"vnb{i}")) for i in range(2)])
    st_ring = Ring([(sb(es, f"bst{i}", [128, 8], F32), Res(f"bst{i}")) for i in range(2)])
    psA = psring([0, 1, 2, 3])
    psB = psring([4, 5])
    psC = psring([6, 7])
    GC = float(math.sqrt(0.044715))
    GS = float(2.0 * math.sqrt(2.0 / PI))

    def gelu_from_psum(pt, rpt, dst_ap, rdst, rows=slice(0, 128)):
        t1, rt1 = f_ring.next()
        sc.op("act", lambda e: e.activation(out=t1[:], in_=pt[:, :], func=AF.Square, scale=GC), reads=[rpt], writes=[rt1])
        sc.op("dve", lambda e: e.scalar_tensor_tensor(out=t1[:], in0=t1[:], scalar=1.0, in1=pt[:, :], op0=ALU.add, op1=ALU.mult), reads=[rt1, rpt], writes=[rt1])
        sc.op("act", lambda e: e.activation(out=t1[:], in_=t1[:], func=AF.Exp, scale=-GS), reads=[rt1], writes=[rt1])
        sc.op("act", lambda e: e.activation(out=t1[:], in_=t1[:], func=AF.Ln, bias=1.0), reads=[rt1], writes=[rt1])
        sc.op("act", lambda e: e.activation(out=t1[:], in_=t1[:], func=AF.Exp, scale=-1.0), reads=[rt1], writes=[rt1])
        sc.op("dve", lambda e: e.tensor_tensor(out=dst_ap, in0=t1[:], in1=pt[:, :], op=ALU.mult), reads=[rt1, rpt], writes=[rdst])

    def rms_feat(cq_t, rcq, sq_t, rsq, gvec, dst, rdst, gcols):
        pss, rpss = psB.next()
        for j in range(2):
            sc.op("pe", lambda e: e.matmul(pss[:, :], lhsT=ones_bf[:], rhs=sq_t[:, j, :], start=(j == 0), stop=(j == 1)), reads=[rsq, r_const], writes=[rpss])
        t1, rt1 = f_ring.next()
        sc.op("dve", lambda e: e.tensor_scalar(out=t1[:], in0=pss[:, :], scalar1=1.0 / 256, scalar2=EPS, op0=ALU.mult, op1=ALU.add), reads=[rpss], writes=[rt1])
        sc.op("act", lambda e: e.activation(out=t1[:], in_=t1[:], func=AF.Ln), reads=[rt1], writes=[rt1])
        sc.op("act", lambda e: e.activation(out=t1[:], in_=t1[:], func=AF.Exp, scale=-0.5), reads=[rt1], writes=[rt1])
        for j in range(2):
            sc.op("dve", lambda e: e.scalar_tensor_tensor(out=dst[:, j, gcols], in0=cq_t[:, j, :], scalar=gvec[:, j:j + 1], in1=t1[:], op0=ALU.mult, op1=ALU.mult),
                  reads=[rcq, rt1, r_p0c], writes=[rdst])

    xT_v = fm(xT)
    mix_v = fm(mixin)
    gctx = {}
    fpool = FreePool(f_ring.items)
    cqpool = FreePool(cq_ring.items)
    sqpool = FreePool(sq_ring.items)
    stpool = FreePool(st_ring.items)
    vnbpool = FreePool(vnb_ring.items)
    pA = FreePool(psA.items)
    pB = FreePool(psB.items)
    pC = FreePool(psC.items)

    def g_setup(g):
        xb, rxb = xb_ring.next()
        sc.dma("pool", xb[:], xT_v[:, :, gsl(g)], writes=[rxb])
        gu, rgu = gu_ring.next()
        bo, rbo = bo_ring.next()
        gctx[g] = dict(xb=xb, rxb=rxb, gu=gu, rgu=rgu, bo=bo, rbo=rbo, u_done=0, v_done=0, done=0)

    def gelu_gen(pt, rpt, dst_ap, rdst):
        while not fpool.avail():
            yield
        t1, rt1 = it = fpool.get()
        sc.op("act", lambda e: e.activation(out=t1[:], in_=pt[:, :], func=AF.Square, scale=GC), reads=[rpt], writes=[rt1])
        yield
        sc.op("dve", lambda e: e.scalar_tensor_tensor(out=t1[:], in0=t1[:], scalar=1.0, in1=pt[:, :], op0=ALU.add, op1=ALU.mult), reads=[rt1, rpt], writes=[rt1])
        yield
        sc.op("act", lambda e: e.activation(out=t1[:], in_=t1[:], func=AF.Exp, scale=-GS), reads=[rt1], writes=[rt1])
        yield
        sc.op("act", lambda e: e.activation(out=t1[:], in_=t1[:], func=AF.Ln, bias=1.0), reads=[rt1], writes=[rt1])
        yield
        sc.op("act", lambda e: e.activation(out=t1[:], in_=t1[:], func=AF.Exp, scale=-1.0), reads=[rt1], writes=[rt1])
        yield
        sc.op("dve", lambda e: e.tensor_tensor(out=dst_ap, in0=t1[:], in1=pt[:, :], op=ALU.mult), reads=[rt1, rpt], writes=[rdst])
        fpool.put(it)
        yield

    def gen_cq(g, which):
        c = gctx[g]
        xb, rxb = c["xb"], c["rxb"]
        gc = gsl(g)
        while not (cqpool.avail() and sqpool.avail()):
            yield
        cq_t, rcq = icq = cqpool.get()
        sq_t, rsq = isq = sqpool.get()
        for j in range(2):
            col = (which * 2 + j) * 128
            while not pA.avail():
                yield
            pt, rpt = ipt = pA.get()
            for k in range(8):
                sc.op("pe", lambda e: e.matmul(pt[:, :], lhsT=w_in[:, k, col:col + 128], rhs=xb[:, k, :], start=(k == 0), stop=(k == 7)),
                      reads=[r_win[0], rxb], writes=[rpt], inc=(k == 7))
            yield
            sc.op("act", lambda e: e.activation(out=sq_t[:, j, :], in_=pt[:, :], func=AF.Square), reads=[rpt], writes=[rsq])
            yield
            sc.op("dve", lambda e: e.tensor_copy(out=cq_t[:, j, :], in_=pt[:, :]), reads=[rpt], writes=[rcq])
            pA.put(ipt)
            yield
        gvec, dst, rdst = (gq, cqn, r_cqn[g]) if which == 0 else (gkv, ckvn, r_ckvn[g])
        while not (pB.avail() and fpool.avail()):
            yield
        pss, rpss = ipss = pB.get()
        t1, rt1 = it1 = fpool.get()
        for j in range(2):
            sc.op("pe", lambda e: e.matmul(pss[:, :], lhsT=ones_bf[:], rhs=sq_t[:, j, :], start=(j == 0), stop=(j == 1)), reads=[rsq, r_const], writes=[rpss])
        sqpool.put(isq)
        yield
        sc.op("dve", lambda e: e.tensor_scalar(out=t1[:], in0=pss[:, :], scalar1=1.0 / 256, scalar2=EPS, op0=ALU.mult, op1=ALU.add), reads=[rpss], writes=[rt1])
        pB.put(ipss)
        yield
        sc.op("act", lambda e: e.activation(out=t1[:], in_=t1[:], func=AF.Ln), reads=[rt1], writes=[rt1])
        yield
        sc.op("act", lambda e: e.activation(out=t1[:], in_=t1[:], func=AF.Exp, scale=-0.5), reads=[rt1], writes=[rt1])
        yield
        for j in range(2):
            sc.op("dve", lambda e: e.scalar_tensor_tensor(out=dst[:, j, gc], in0=cq_t[:, j, :], scalar=gvec[:, j:j + 1], in1=t1[:], op0=ALU.mult, op1=ALU.mult),
                  reads=[rcq, rt1, r_p0c], writes=[rdst])
            yield
        cqpool.put(icq)
        fpool.put(it1)

    def gen_krope(g):
        c = gctx[g]
        xb, rxb = c["xb"], c["rxb"]
        gc = gsl(g)
        while not (pA.avail(2) and fpool.avail(2)):
            yield
        pa, rpa = ipa = pA.get()
        pb, rpb = ipb = pA.get()
        t1, rt1 = it1 = fpool.get()
        t2, rt2 = it2 = fpool.get()
        for (pp, rpp, v) in ((pa, rpa, 0), (pb, rpb, 1)):
            for k in range(8):
                sc.op("pe", lambda e: e.matmul(pp[0:96, :], lhsT=w_kr[:, k, v, :], rhs=xb[:, k, :], start=(k == 0), stop=(k == 7)),
                      reads=[r_wkr, rxb], writes=[rpp], inc=(k == 7))
            yield
        sc.op("dve", lambda e: e.tensor_tensor(out=t1[P, :], in0=pa[P, :], in1=cosT[P, gc], op=ALU.mult), reads=[rpa, r_trig], writes=[rt1])
        pA.put(ipa)
        yield
        sc.op("dve", lambda e: e.tensor_tensor(out=t2[P, :], in0=pb[P, :], in1=sinT[P, gc], op=ALU.mult), reads=[rpb, r_trig], writes=[rt2])
        pA.put(ipb)
        yield
        sc.op("dve", lambda e: e.tensor_tensor(out=krot[P, gc], in0=t1[P, :], in1=t2[P, :], op=ALU.add), reads=[rt1, rt2], writes=[r_krot[g]])
        fpool.put(it1)
        fpool.put(it2)
        yield

    def gen_u(g, j):
        c = gctx[g]
        xb, rxb, gu, rgu = c["xb"], c["rxb"], c["gu"], c["rgu"]
        col = 544 + j * 128
        while not pA.avail():
            yield
        pt, rpt = ipt = pA.get()
        for k in range(8):
            sc.op("pe", lambda e: e.matmul(pt[:, :], lhsT=w_in[:, k, col:col + 128], rhs=xb[:, k, :], start=(k == 0), stop=(k == 7)),
                  reads=[r_win[1], rxb], writes=[rpt], inc=(k == 7))
        yield
        yield from gelu_gen(pt, rpt, gu[:, j, :], rgu[j])
        pA.put(ipt)
        c["u_done"] += 1

    def gen_v(g, tt):
        c = gctx[g]
        xb, rxb, gu, rgu, bo, rbo = c["xb"], c["rxb"], c["gu"], c["rgu"], c["bo"], c["rbo"]
        while not (pA.avail() and fpool.avail(2)):
            yield
        pv, rpv = ipv = pA.get()
        gv, rgv = igv = fpool.get()
        for k in range(8):
            sc.op("pe", lambda e: e.matmul(pv[:, :], lhsT=xb[:, k, tt * 128:(tt + 1) * 128], rhs=w_in[:, k, 1056:1568], start=(k == 0), stop=(k == 7)),
                  reads=[r_win[1], rxb], writes=[rpv], inc=(k == 7))
        yield
        yield from gelu_gen(pv, rpv, gv[:], rgv)
        pA.put(ipv)
        while not stpool.avail():
            yield
        st, rst = ist = stpool.get()
        sc.op("dve", lambda e: e.bn_stats(out=st[:, 0:6], in_=gv[:]), reads=[rgv], writes=[rst])
        yield
        sc.op("dve", lambda e: e.bn_aggr(out=st[:, 6:8], in_=st[:, 0:6]), reads=[rst], writes=[rst])
        yield
        sc.op("dve", lambda e: e.tensor_scalar(out=st[:, 7:8], in0=st[:, 7:8], scalar1=EPS, scalar2=None, op0=ALU.add), reads=[rst], writes=[rst])
        yield
        sc.op("act", lambda e: e.activation(out=st[:, 7:8], in_=st[:, 7:8], func=AF.Ln), reads=[rst], writes=[rst])
        yield
        sc.op("act", lambda e: e.activation(out=st[:, 7:8], in_=st[:, 7:8], func=AF.Exp, scale=-0.5), reads=[rst], writes=[rst])
        yield
        sc.op("dve", lambda e: e.tensor_scalar(out=gv[:], in0=gv[:], scalar1=st[:, 6:7], scalar2=st[:, 7:8], op0=ALU.subtract, op1=ALU.mult), reads=[rgv, rst], writes=[rgv])
        stpool.put(ist)
        yield
        sc.op("dve", lambda e: e.tensor_tensor(out=gv[:], in0=gv[:], in1=lnG[:], op=ALU.mult), reads=[rgv, r_p0c], writes=[rgv])
        yield
        while not (vnbpool.avail() and pC.avail()):
            yield
        vnb, rvnb = ivnb = vnbpool.get()
        pm, rpm = ipm = pC.get()
        sc.op("dve", lambda e: e.tensor_tensor(out=vnb[:], in0=gv[:], in1=lnB[:], op=ALU.add), reads=[rgv, r_p0c], writes=[rvnb])
        fpool.put(igv)
        yield
        for g4 in range(4):
            cs = slice(g4 * 128, (g4 + 1) * 128)
            sc.op("pe", lambda e: e.matmul(pm[:, cs], lhsT=vnb[:, cs], rhs=wsg[:, g4, :], start=True, stop=False), reads=[rvnb, r_p0c], writes=[rpm], inc=False)
            sc.op("pe", lambda e: e.matmul(pm[:, cs], lhsT=ones_bf[0:1, 0:128], rhs=bsg[0:1, cs], start=False, stop=True), reads=[r_const, r_p0c], writes=[rpm], inc=(g4 == 3))
        vnbpool.put(ivnb)
        yield
        while c["u_done"] < 4:
            yield
        sc.op("dve", lambda e: e.tensor_tensor(out=bo[:, :, tt * 128:(tt + 1) * 128], in0=pm[:, :].rearrange("p (a b) -> p a b", a=4), in1=gu[:, :, tt * 128:(tt + 1) * 128], op=ALU.mult),
              reads=[rpm] + rgu, writes=[rbo])
        pC.put(ipm)
        c["v_done"] += 1
        if c["v_done"] == 4:
            sc.dma("sp", mix_v[:, 4:8, gsl(g)], bo[:], reads=[rbo], writes=[r_mix[g]])
        yield

    tasks = []
    for g in range(NG):
        tasks.append(("setup", g))
        tasks += [(gen_cq, g, 0), (gen_cq, g, 1), (gen_krope, g)]
        tasks += [(gen_u, g, j) for j in range(4)]
        tasks += [(gen_v, g, tt) for tt in range(4)]
    NW0 = 4
    slots0 = [None] * NW0
    while tasks or any(sl is not None for sl in slots0):
        for i in range(NW0):
            if slots0[i] is None and tasks:
                t = tasks.pop(0)
                if t[0] == "setup":
                    g_setup(t[1])
                    t = tasks.pop(0)
                slots0[i] = t[0](*t[1:])
            if slots0[i] is not None:
                try:
                    next(slots0[i])
                except StopIteration:
                    slots0[i] = None
    sc.barrier()
    es.close()
    if stop_after <= 0:
        es01.close()
        return _finish(nc, sc, top)

    es = ExitStack()
    wq = sb(es, "wq", [128, 2, 8 * 96 + 32], BF16)
    wqs = sb(es, "wqs", [128, 2, 8 * 96 + 32], BF16)
    wkv = sb(es, "wkv", [128, 2, 8, 128], BF16)
    vaug = sb(es, "vaug", [128, NT, 8, 128], BF16)
    r_wq = Res("wq")
    r_wqs = Res("wqs")
    r_wkv = Res("wkv")
    r_vaug = [Res(f"vaug{t}") for t in range(NT)]
    r_vones = Res("vones")
    sc.op("pool", lambda e: e.memset(wq[:, :, 768:800], 0.0), writes=[r_wq])
    sc.dma("pool", wq[:, :, 0:768], w_qb.rearrange("(k p) n -> p k n", p=128), writes=[r_wq])
    sc.dma("pool", wkv[:], w_kvb.rearrange("(k p) (h d) -> p k h d", p=128, h=8), writes=[r_wkv])
    sc.op("pool", lambda e: e.memset(wqs[:], 0.0), writes=[r_wqs])
    for k in range(2):
        wq4 = wq[:, k, 0:768].rearrange("p (h d) -> p h d", h=8)
        wqs4 = wqs[:, k, 0:768].rearrange("p (h d) -> p h d", h=8)
        sc.op("pool", lambda e: e.tensor_scalar(out=wqs4[:, :, 64:80], in0=wq4[:, :, 80:96], scalar1=-1.0, scalar2=None, op0=ALU.mult), reads=[r_wq], writes=[r_wqs])
        sc.op("pool", lambda e: e.tensor_copy(out=wqs4[:, :, 80:96], in_=wq4[:, :, 64:80]), reads=[r_wq], writes=[r_wqs])
    sc.op("pool", lambda e: e.memset(vaug[:, :, :, 64:128], 1.0), writes=[r_vones])
    psP = psring([0, 1])
    psS = psring([2, 3, 4, 5])
    psO = psring([6, 7])
    psBC = psP
    negm = sb(es, "negm", [128, 128], BF16)
    r_negm = Res("negm")
    sc.op("dve", lambda e: e.tensor_scalar(out=negm[:], in0=mask_f[:], scalar1=-1.0, scalar2=30000.0, op0=ALU.add, op1=ALU.mult), reads=[r_const], writes=[r_negm])
    for tt in range(NT):
        pv, rpv = psP.next()
        ts_ = slice(tt * 128, (tt + 1) * 128)
        for k in range(2):
            sc.op("pe", lambda e: e.matmul(pv[:, :].rearrange("p (h d) -> p h d", h=8), lhsT=ckvn[:, k, ts_], rhs=wkv[:, k, :, 64:128], start=(k == 0), stop=(k == 1)),
                  reads=[r_ckvn[tt // 4], r_wkv], writes=[rpv], inc=(k == 1))
        eng = "act" if tt % 2 == 0 else "dve"
        if eng == "act":
            sc.op("act", lambda e: e.copy(out=vaug[:, tt, :, 0:64], in_=pv[:, :].rearrange("p (h d) -> p h d", h=8)), reads=[rpv], writes=[r_vaug[tt]])
        else:
            sc.op("dve", lambda e: e.tensor_copy(out=vaug[:, tt, :, 0:64], in_=pv[:, :].rearrange("p (h d) -> p h d", h=8)), reads=[rpv], writes=[r_vaug[tt]])

    qT_ring = Ring([(sb(es, f"qT{i}", [128, S], BF16), [Res(f"qT{i}_{g}") for g in range(NG)]) for i in range(2)])
    kT_ring = Ring([(sb(es, f"kT{i}", [128, S], BF16), [Res(f"kT{i}_{g}") for g in range(NG)]) for i in range(2)])
    for (t_, rl_) in qT_ring.items + kT_ring.items:
        for g in range(NG):
            sc.op("pool", lambda e: e.memset(t_[96:128, gsl(g)], 0.0), writes=[rl_[g]])
    pT_ring = Ring([(sb(es, f"pT{i}", [128, TG], BF16), Res(f"pT{i}")) for i in range(6)])
    rt_ring = Ring([(sb(es, f"rt{i}", [96, TG], F32), Res(f"rt{i}")) for i in range(4)])
    on_ring = Ring([(sb(es, f"onum{i}", [64, TG], F32), Res(f"onum{i}")) for i in range(2)])
    rd_ring = Ring([(sb(es, f"rden{i}", [65, TG], F32), Res(f"rden{i}")) for i in range(2)])
    ao_ring = Ring([(sb(es, f"ao{i}", [64, TG], BF16), Res(f"ao{i}")) for i in range(2)])

    def proj_head(h):
        qT, rq = qT_ring.next()
        kT, rk = kT_ring.next()
        for g in range(NG):
            gc = gsl(g)
            pa, rpa = psP.next()
            pb, rpb = psP.next()
            for k in range(2):
                sc.op("pe", lambda e: e.matmul(pa[:, :], lhsT=wq[:, k, h * 96:h * 96 + 128], rhs=cqn[:, k, gc], start=(k == 0), stop=(k == 1)), reads=[r_wq, r_cqn[g]], writes=[rpa], inc=(k == 1))
            for k in range(2):
                sc.op("pe", lambda e: e.matmul(pb[:, :], lhsT=wqs[:, k, h * 96:h * 96 + 128], rhs=cqn[:, k, gc], start=(k == 0), stop=(k == 1)), reads=[r_wqs, r_cqn[g]], writes=[rpb], inc=(k == 1))
            sc.op("dve", lambda e: e.tensor_copy(out=qT[0:64, gc], in_=pa[0:64, :]), reads=[rpa], writes=[rq[g]])
            t1, rt1 = rt_ring.next()
            t2, rt2 = rt_ring.next()
            sc.op("dve", lambda e: e.tensor_tensor(out=t1[P, :], in0=pa[P, :], in1=cosT[P, gc], op=ALU.mult), reads=[rpa, r_trig], writes=[rt1])
            sc.op("dve", lambda e: e.tensor_tensor(out=t2[P, :], in0=pb[P, :], in1=sinT[P, gc], op=ALU.mult), reads=[rpb, r_trig], writes=[rt2])
            sc.op("pool", lambda e: e.tensor_tensor(out=qT[P, gc], in0=t1[P, :], in1=t2[P, :], op=ALU.add), reads=[rt1, rt2], writes=[rq[g]])
            pk, rpk = psP.next()
            for k in range(2):
                sc.op("pe", lambda e: e.matmul(pk[:, :], lhsT=wkv[:, k, h, :], rhs=ckvn[:, k, gc], start=(k == 0), stop=(k == 1)), reads=[r_wkv, r_ckvn[g]], writes=[rpk], inc=(k == 1))
            sc.op("dve", lambda e: e.tensor_copy(out=kT[0:64, gc], in_=pk[0:64, :]), reads=[rpk], writes=[rk[g]])
            sc.op("pool", lambda e: e.tensor_copy(out=kT[P, gc], in_=krot[P, gc]), reads=[r_krot[g]], writes=[rk[g]])
        return (qT, rq, kT, rk)

    fin_pending = []

    def attn_head(h, qT, rq, kT, rk):
        for gq_ in range(NG):
            po, rpo = psO.next()
            nj = 4 * gq_ + 4
            pend = []

            def pv_mm(item, po=po, rpo=rpo, nj=nj):
                j, pT, rpT, c0, ncols = item
                sc.op("pe", lambda e: e.matmul(po[:, c0:TG], lhsT=vaug[:, j, h, :], rhs=pT[:, 0:ncols], start=(j == 0), stop=(j == nj - 1), skip_group_check=True),
                      reads=[r_vaug[j], r_vones, rpT], writes=[rpo], inc=True)

            for j in range(nj):
                c0 = 0 if j < 4 * gq_ else (j - 4 * gq_) * 128
                ncols = TG - c0
                ps_, rps_ = psS.next()
                q0 = gq_ * TG + c0
                diag = j >= 4 * gq_
                sc.op("pe", lambda e: e.matmul(ps_[:, 0:ncols], lhsT=kT[:, j * 128:(j + 1) * 128], rhs=qT[:, q0:q0 + ncols], start=True, stop=not diag, skip_group_check=True),
                      reads=[rk[j // 4], rq[gq_]], writes=[rps_], inc=not diag)
                if diag:
                    sc.op("pe", lambda e: e.matmul(ps_[:, 0:128], lhsT=ident_b[:], rhs=negm[:], start=False, stop=True, skip_group_check=True),
                          reads=[r_const, r_negm], writes=[rps_], inc=True)
                pT, rpT = pT_ring.next()
                sc.op("act", lambda e: e.activation(out=pT[:, 0:ncols], in_=ps_[:, 0:ncols], func=AF.Exp, scale=MLA_SCALE), reads=[rps_], writes=[rpT])
                pend.append((j, pT, rpT, c0, ncols))
                if len(pend) > 3:
                    pv_mm(pend.pop(0))
            while pend:
                pv_mm(pend.pop(0))
            rd, rrd = rd_ring.next()
            sc.op("dve", lambda e: e.reciprocal(out=rd[0:64, :], in_=po[64:128, :]), reads=[rpo], writes=[rrd])
            ao, rao = ao_ring.next()
            sc.op("dve", lambda e: e.tensor_tensor(out=ao[:, :], in0=po[0:64, :], in1=rd[0:64, :], op=ALU.mult), reads=[rpo, rrd], writes=[rao])
            sc.dma("sp", mixin[h * 64:(h + 1) * 64, gsl(gq_)], ao[:, :], reads=[rao], writes=[r_mix[gq_]])

    cur = proj_head(0)
    for h in range(8):
        nxt = proj_head(h + 1) if h + 1 < 8 else None
        attn_head(h, *cur)
        cur = nxt
    while fin_pending:
        fin_pending.pop(0)()
    sc.barrier()
    es.close()
    es01.close()
    if stop_after <= 1:
        return _finish(nc, sc, top)

    def outproj_ln_phase(es, wo, r_wo, src_rhs_loader, resid_src_v, r_resid_src, dst_v, r_dst, layer):
        pass

    es_sh = ExitStack()
    wsh = sb(es_sh, "wsh", [128, 8, 4096], BF16)
    r_wsh = [Res(f"wsh{i}") for i in range(4)]
    es_w0 = ExitStack()
    es = ExitStack()
    wo_holder = []

    def _p2a_first():
        wo_ = sb(es, "wo", [128, 8, D], BF16)
        r_wo_ = [Res("wo0"), Res("wo1")]
        load_w(wo_, w_out_e.rearrange("(k p) n -> p k n", p=128), r_wo_, 2, D)
        wo_holder.append((wo_, r_wo_))

    _p2a_first()
    wo, r_wo = wo_holder[0]
    w1_v0 = w_ff1[0].rearrange("(k p) n -> p k n", p=128)
    for i in range(4):
        sc.dma("pool", wsh[:, :, i * 1024:(i + 1) * 1024], w1_v0[:, :, i * 1024:(i + 1) * 1024], writes=[r_wsh[i]])
    pre0 = (wsh, None, r_wsh, None)
    mi_ring = Ring([(sb(es, f"mi{i}", [128, 8, TG], BF16), Res(f"mi{i}")) for i in range(4)])
    xr_ring = Ring([(sb(es, f"xr{i}", [128, 8, TG], F32), Res(f"xr{i}")) for i in range(4)])
    tms = [ln_tmps(es), ln_tmps(es)]
    psA = psring([0, 1, 2, 3])
    psLs = [psring([4, 5]), psring([6, 7])]
    hA_v = fm(hA)
    def run_rr(gens):
        gens = list(gens)
        while gens:
            for gen in list(gens):
                try:
                    next(gen)
                except StopIteration:
                    gens.remove(gen)

    p2a_bufs = {}
    p2a_in = {}

    def p2a_load(g):
        if g >= NG or g in p2a_in:
            return
        mi, rmi = mi_ring.next()
        xr, rxr = xr_ring.next()
        p2a_in[g] = (mi, rmi, xr, rxr)
        sc.dma("sp", mi[:], mix_v[:, :, gsl(g)], reads=[r_mix[g]], writes=[rmi])
        sc.dma("sp", xr[:], xT_v[:, :, gsl(g)], writes=[rxr])

    def p2a_main(g):
        gc = gsl(g)
        p2a_load(g)
        mi, rmi, xr, rxr = p2a_in[g]
        p2a_bufs[g] = (xr, rxr)
        for oc in range(8):
            pt, rpt = psA.next()
            for k in range(8):
                sc.op("pe", lambda e: e.matmul(pt[:, :], lhsT=wo[:, k, oc * 128:(oc + 1) * 128], rhs=mi[:, k, :], start=(k == 0), stop=(k == 7)),
                      reads=[r_wo[oc // 4], rmi], writes=[rpt], inc=(k == 7))
            sc.op("dve", lambda e: e.scalar_tensor_tensor(out=xr[:, oc, :], in0=xr[:, oc, :], scalar=ALPHA, in1=pt[:, :], op0=ALU.mult, op1=ALU.add),
                  reads=[rxr, rpt], writes=[rxr])
            yield
            yield
            yield

    def p2a_ln(g, sl):
        xr, rxr = p2a_bufs[g]
        yield from layer_norm_gen(es, xr, rxr, 0, 1, psLs[sl], tms[sl])
        sc.dma("pool", hA_v[:, :, gsl(g)], xr[:], reads=[rxr], writes=[r_hA[g]])
        p2a_load(g + 4)
        yield

    for g in range(min(4, NG)):
        p2a_load(g)
    ln_gens = {}

    def step_lns():
        for sl_ in list(ln_gens):
            try:
                next(ln_gens[sl_])
            except StopIteration:
                del ln_gens[sl_]

    for g in range(NG):
        m = p2a_main(g)
        while True:
            try:
                next(m)
            except StopIteration:
                break
            step_lns()
        sl = g % 2
        while sl in ln_gens:
            step_lns()
        ln_gens[sl] = p2a_ln(g, sl)
    while ln_gens:
        step_lns()
    sc.barrier()
    es.close()
    if stop_after <= 2:
        es_w0.close()
        es_sh.close()
        return _finish(nc, sc, top)

    def ffn_weights(es, layer, first=None):
        w1 = sb(es, "w1", [128, 8, DFF], BF16)
        w2 = sb(es, "w2", [128, 32, D], BF16)
        r_w1 = [Res(f"w1_{i}") for i in range(4)]
        r_w2 = [Res(f"w2_{i}") for i in range(4)]
        w1_v = w_ff1[layer].rearrange("(k p) n -> p k n", p=128)
        w2_v = w_ff2[layer].rearrange("(k p) n -> p k n", p=128)
        sc.dma("pool", w1[:, :, 0:1024], w1_v[:, :, 0:1024], writes=[r_w1[0]])
        if first is not None:
            first()
        for i in range(1, 4):
            sc.dma("pool", w1[:, :, i * 1024:(i + 1) * 1024], w1_v[:, :, i * 1024:(i + 1) * 1024], writes=[r_w1[i]])
        for i in range(4):
            sc.dma("pool", w2[:, i * 8:(i + 1) * 8, :], w2_v[:, i * 8:(i + 1) * 8, :], writes=[r_w2[i]])
        return (w1, w2, r_w1, r_w2)

    def ffn_phase(layer, src_v, r_src, dst_v, r_dst, pre=None, after_last_ffn1=None):
        es = ExitStack()
        hb = hr = None
        rhb = Res("hb")
        rhr = Res("hr")

        def alloc_io():
            nonlocal hb, hr
            hb = sb(es, "hb", [128, 8, TG], BF16)
            hr = sb(es, "hr", [128, 8, TG], F32)

        def load_first():
            sc.dma("pool", hb[:], src_v[:, :, gsl(0)], reads=[r_src[0]], writes=[rhb])
            sc.dma("sp", hr[:], src_v[:, :, gsl(0)], reads=[r_src[0]], writes=[rhr])

        if pre is None:
            w1 = sb(es, "w1", [128, 8, DFF], BF16)
            w2 = sb(es, "w2", [128, 32, D], BF16)
            alloc_io()
            r_w1 = [Res(f"w1_{i}") for i in range(4)]
            r_w2 = [Res(f"w2_{i}") for i in range(4)]
            w1_v = w_ff1[layer].rearrange("(k p) n -> p k n", p=128)
            w2_v = w_ff2[layer].rearrange("(k p) n -> p k n", p=128)
            sc.dma("pool", w1[:, :, 0:1024], w1_v[:, :, 0:1024], writes=[r_w1[0]])
            load_first()
            for i in range(1, 4):
                sc.dma("pool", w1[:, :, i * 1024:(i + 1) * 1024], w1_v[:, :, i * 1024:(i + 1) * 1024], writes=[r_w1[i]])
            for i in range(4):
                sc.dma("pool", w2[:, i * 8:(i + 1) * 8, :], w2_v[:, i * 8:(i + 1) * 8, :], writes=[r_w2[i]])
        else:
            (w1, w2, r_w1, r_w2) = pre
            if w2 is None:
                w2 = sb(es, "w2", [128, 32, D], BF16)
                r_w2 = [Res(f"w2_{i}") for i in range(4)]
                alloc_io()
                load_first()
                w2_v = w_ff2[layer].rearrange("(k p) n -> p k n", p=128)
                for i in range(4):
                    sc.dma("pool", w2[:, i * 8:(i + 1) * 8, :], w2_v[:, i * 8:(i + 1) * 8, :], writes=[r_w2[i]])
            else:
                alloc_io()
                load_first()
        a = sb(es, "a_act", [128, 32, TG], BF16)
        ra = [Res(f"a{i}") for i in range(32)]
        sq_ring = Ring([(sb(es, f"fsq{i}", [128, TG], F32), Res(f"fsq{i}")) for i in range(2)])
        tm = ln_tmps(es)
        psA = psring([0, 1, 2, 3])
        psB = psring([4, 5])
        psL = psring([6, 7])
        pending_ln = [None]

        def step_ln(drain=False):
            while pending_ln[0] is not None:
                try:
                    next(pending_ln[0])
                except StopIteration:
                    pending_ln[0] = None
                if not drain:
                    break

        def ln_store_gen_ffn(g):
            yield from layer_norm_gen(es, hr, rhr, layer, 2, psL, tm)
            sc.dma("sp", dst_v[:, :, gsl(g)], hr[:], reads=[rhr], writes=[r_dst[g]])
            if g + 1 < NG:
                sc.dma("sp", hr[:], src_v[:, :, gsl(g + 1)], reads=[r_src[g + 1]], writes=[rhr])
            yield

        for g in range(NG):
            gc = gsl(g)
            for fc in range(32):
                pt, rpt = psA.next()
                for k in range(8):
                    sc.op("pe", lambda e: e.matmul(pt[:, :], lhsT=w1[:, k, fc * 128:(fc + 1) * 128], rhs=hb[:, k, :], start=(k == 0), stop=(k == 7)),
                          reads=[r_w1[fc // 8], rhb], writes=[rpt], inc=(k == 7))
                sq, rsq = sq_ring.next()
                sc.op("act", lambda e: e.activation(out=sq[:], in_=pt[:, :], func=AF.Square), reads=[rpt], writes=[rsq])
                sc.op("dve", lambda e: e.scalar_tensor_tensor(out=a[:, fc, :], in0=pt[:, :], scalar=0.0, in1=sq[:], op0=ALU.is_gt, op1=ALU.mult),
                      reads=[rpt, rsq], writes=[ra[fc]])
                if fc >= 1:
                    step_ln()
            step_ln(drain=True)
            if g + 1 < NG:
                sc.dma("pool", hb[:], src_v[:, :, gsl(g + 1)], reads=[r_src[g + 1]], writes=[rhb])
            elif after_last_ffn1 is not None:
                after_last_ffn1()
            for oc in range(8):
                pt, rpt = psB.next()
                for fc in range(32):
                    sc.op("pe", lambda e: e.matmul(pt[:, :], lhsT=w2[:, fc, oc * 128:(oc + 1) * 128], rhs=a[:, fc, :], start=(fc == 0), stop=(fc == 31)),
                          reads=[r_w2[fc // 8], ra[fc]], writes=[rpt], inc=(fc == 31))
                sc.op("dve", lambda e: e.scalar_tensor_tensor(out=hr[:, oc, :], in0=hr[:, oc, :], scalar=ALPHA, in1=pt[:, :], op0=ALU.mult, op1=ALU.add),
                      reads=[rhr, rpt], writes=[rhr])
            pending_ln[0] = ln_store_gen_ffn(g)
        step_ln(drain=True)
        sc.barrier()
        es.close()

    hB_v = fm(hB)
    wi_v = w_in_o.rearrange("(k p) n -> p k n", p=128)

    def _load_wi():
        for i in (0, 2, 3, 1):
            sc.dma("pool", wsh[:, :, i * 1024:(i + 1) * 1024], wi_v[:, :, i * 1024:(i + 1) * 1024], writes=[r_wsh[i]])

    ffn_phase(0, hA_v, r_hA, hB_v, r_hB, pre=pre0, after_last_ffn1=_load_wi)
    es_w0.close()
    if stop_after <= 3:
        es_sh.close()
        return _finish(nc, sc, top)

    es = ExitStack()
    wi = wsh
    r_wi = r_wsh
    hb_ring = Ring([(sb(es, f"hb{i}", [128, 8, TG], BF16), Res(f"hb{i}")) for i in range(2)])
    hr = sb(es, "hr3", [128, 8, TG], F32)
    rhr = Res("hr3")
    hbs = {}
    hbs[0] = hb_ring.next()
    sc.dma("pool", hbs[0][0][:], hB_v[:, :, gsl(0)], reads=[r_hB[0]], writes=[hbs[0][1]])
    sc.dma("sp", hr[:], hB_v[:, :, gsl(0)], reads=[r_hB[0]], writes=[rhr])
    wo2 = sb(es, "wo2", [128, 8, D], BF16)
    r_wo2 = [Res("wo2_0"), Res("wo2_1")]
    load_w(wo2, w_out_o.rearrange("(k p) n -> p k n", p=128), r_wo2, 2, D)
    lbt = sb(es, "lbt", [128, 2, 8], F32)
    gn = sb(es, "gn", [128, 8], F32)
    oml = sb(es, "oml", [128, 8], F32)
    noml = sb(es, "noml", [128, 8], F32)
    r_hc = Res("hgc")
    sc.dma("sp", lbt[:], lb_in, writes=[r_hc])
    sc.dma("sp", gn[:], gn_in, writes=[r_hc])
    sc.op("dve", lambda e: e.tensor_tensor(out=oml[:], in0=lbt[:, 1, :], in1=lbt[:, 0, :], op=ALU.subtract), reads=[r_hc], writes=[r_hc])
    sc.op("act", lambda e: e.activation(out=oml[:], in_=oml[:], func=AF.Exp), reads=[r_hc], writes=[r_hc])
    sc.op("act", lambda e: e.activation(out=oml[:], in_=oml[:], func=AF.Ln, bias=1.0), reads=[r_hc], writes=[r_hc])
    sc.op("act", lambda e: e.activation(out=oml[:], in_=oml[:], func=AF.Exp, scale=-1.0), reads=[r_hc], writes=[r_hc])
    sc.op("dve", lambda e: e.tensor_scalar(out=noml[:], in0=oml[:], scalar1=-1.0, scalar2=None, op0=ALU.mult), reads=[r_hc], writes=[r_hc])
    st = sb(es, "hst", [128, 8, 128], F32)
    stb4 = sb(es, "hstb", [128, 8, 4, 128], BF16)
    r_st = [Res(f"st{h}") for h in range(8)]
    r_stb = [[Res(f"stb{h}_{i}") for i in range(4)] for h in range(8)]
    for h in range(8):
        sc.op("pool", lambda e: e.memset(st[:, h, :], 0.0), writes=[r_st[h]])
        sc.op("pool", lambda e: e.memset(stb4[:, h, 0, :], 0.0), writes=[r_stb[h][0]])
    ones_t = ones_f
    mask4 = sb(es, "mask4", [128, 4, 128], BF16)
    for i in range(4):
        sc.op("pool", lambda e: e.tensor_copy(out=mask4[:, i, :], in_=mask_b[:]), reads=[r_const], writes=[r_hc])
    NWAY = 3
    cbuf = []
    for i in range(NWAY):
        d = {}
        for nm in ("s1", "lg", "bc"):
            d[nm] = (sb(es, f"c{i}_{nm}", [128, TG], F32), Res(f"c{i}_{nm}"))
        d["eb"] = d["lg"]
        for nm in ("kt", "qt", "kd", "at4"):
            d[nm] = (sb(es, f"c{i}_{nm}", [128, TG], BF16), Res(f"c{i}_{nm}"))
        d["ktok4"] = d["kd"]
        d["sqo"] = d["at4"]
        d["banks"] = ((psum[2 * i], rps[2 * i]), (psum[2 * i + 1], rps[2 * i + 1]))
        cbuf.append(d)
    q_free = [True] * 8
    g_free = [True] * 8
    vtok_bufs = [(sb(es, f"vtok{i}", [128, 4, D], BF16), [Res(f"vtok{i}_{t}") for t in range(4)]) for i in range(2)]
    siluq = sb(es, "siluq", [128, 8, TG], BF16)
    r_sq = [Res(f"siluq{h}") for h in range(8)]
    sg = sb(es, "sg", [128, 8, TG], BF16)
    r_sg = [Res(f"sg{h}") for h in range(8)]
    onb = sb(es, "onb", [128, 8, TG], BF16)
    r_onb = [Res(f"onb{h}") for h in range(8)]
    tm = ln_tmps(es)
    psA = psring([6, 7])
    hC_v = fm(hC)

    def load_hb(g):
        if g not in hbs:
            hbs[g] = hb_ring.next()
            sc.dma("pool", hbs[g][0][:], hB_v[:, :, gsl(g)], reads=[r_hB[g]], writes=[hbs[g][1]])

    def session(g, col0, dst, rdst):
        hb, rhb = hbs[g]
        for h in range(8):
            pp, rpp = psA.next()
            c0 = col0 + h * 128
            for k in range(8):
                sc.op("pe", lambda e: e.matmul(pp[:, :], lhsT=wi[:, k, c0:c0 + 128], rhs=hb[:, k, :], start=(k == 0), stop=(k == 7)),
                      reads=[r_wi[col0 // 1024], rhb], writes=[rpp], inc=(k == 7))
            sc.op("act", lambda e: e.activation(out=dst[:, h, :], in_=pp[:, :], func=AF.Silu), reads=[rpp], writes=[rdst[h]])

    def vproj_gen(g):
        hb, rhb = hbs[g]
        vtok, rvt = vtok_bufs[g % 2]
        for tt in range(4):
            for half in range(2):
                pv, rpv = psA.next()
                c0 = 2048 + half * 512
                for k in range(8):
                    sc.op("pe", lambda e: e.matmul(pv[:, :], lhsT=hb[:, k, tt * 128:(tt + 1) * 128], rhs=wi[:, k, c0:c0 + 512], start=(k == 0), stop=(k == 7)),
                          reads=[r_wi[2], rhb], writes=[rpv], inc=(k == 7))
                if half == 0:
                    sc.op("act", lambda e: e.copy(out=vtok[:, tt, 0:512], in_=pv[:, :]), reads=[rpv], writes=[rvt[tt]])
                else:
                    sc.op("dve", lambda e: e.tensor_copy(out=vtok[:, tt, 512:1024], in_=pv[:, :]), reads=[rpv], writes=[rvt[tt]])
                yield

    def vproj(g):
        for _ in vproj_gen(g):
            pass

    def chain(g, h, slot):
        B = cbuf[slot]
        hb, rhb = hbs[g]
        vtok, rvt = vtok_bufs[g % 2]
        (b0, rb0), (b1, rb1) = B["banks"]
        (s1, rs1), (lg, rlg), (bc, rbc), (eb, reb) = B["s1"], B["lg"], B["bc"], B["eb"]
        (kt, rkt), (qt, rqt), (kd, rkd) = B["kt"], B["qt"], B["kd"]
        (ktok4, rk4), (at4, rat), (sqo, rsqo) = B["ktok4"], B["at4"], B["sqo"]
        hs = slice(h * 128, (h + 1) * 128)
        TS = [slice(tt * 128, (tt + 1) * 128) for tt in range(4)]
        for k in range(8):
            sc.op("pe", lambda e: e.matmul(b0[:, :], lhsT=wi[:, k, 1024 + h * 128:1024 + (h + 1) * 128], rhs=hb[:, k, :], start=(k == 0), stop=(k == 7)),
                  reads=[r_wi[1], rhb], writes=[rb0], inc=(k == 7))
        yield
        sc.op("act", lambda e: e.activation(out=s1[:], in_=b0[:, :], func=AF.Exp), reads=[rb0], writes=[rs1])
        yield
        sc.op("act", lambda e: e.activation(out=s1[:], in_=s1[:], func=AF.Ln, bias=1.0), reads=[rs1], writes=[rs1])
        yield
        sc.op("act", lambda e: e.activation(out=s1[:], in_=s1[:], func=AF.Exp, scale=-1.0), reads=[rs1], writes=[rs1])
        yield
        sc.op("act", lambda e: e.activation(out=lg[:], in_=s1[:], func=AF.Ln, scale=noml[:, h:h + 1], bias=1.0), reads=[rs1, r_hc], writes=[rlg])
        yield
        for tt in range(4):
            sc.op("dve", lambda e: e.tensor_tensor_scan(out=bc[:, TS[tt]], data0=ones_t[:], data1=lg[:, TS[tt]], initial=0.0, op0=ALU.mult, op1=ALU.add),
                  reads=[rlg, r_hc], writes=[rbc])
        yield
        sc.op("act", lambda e: e.activation(out=eb[:], in_=bc[:], func=AF.Exp), reads=[rbc], writes=[reb])
        yield
        sc.op("act", lambda e: e.activation(out=bc[:], in_=bc[:], func=AF.Exp, scale=-1.0), reads=[rbc], writes=[rbc])
        yield
        sc.op("dve", lambda e: e.scalar_tensor_tensor(out=kt[:], in0=s1[:], scalar=oml[:, h:h + 1], in1=bc[:], op0=ALU.mult, op1=ALU.mult), reads=[rs1, rbc, r_hc], writes=[rkt])
        yield
        sc.op("dve", lambda e: e.tensor_tensor(out=qt[:], in0=siluq[:, h, :], in1=eb[:], op=ALU.mult), reads=[r_sq[h], reb], writes=[rqt])
        q_free[h] = True
        yield
        eb_last = bass.AP(eb[:].tensor, 127, [[TG, 128], [128, 4], [0, 128]])
        sc.op("dve", lambda e: e.tensor_tensor(out=kd[:].rearrange("p (a b) -> p a b", a=4), in0=kt[:].rearrange("p (a b) -> p a b", a=4), in1=eb_last, op=ALU.mult),
              reads=[rkt, reb], writes=[rkd])
        yield
        for tt in range(4):
            sc.op("pe", lambda e: e.matmul(b0[:, TS[tt]], lhsT=kd[:, TS[tt]], rhs=ident_b[:], start=True, stop=True), reads=[rkd, r_const], writes=[rb0], inc=(tt == 3))
        yield
        sc.op("act", lambda e: e.copy(out=ktok4[:], in_=b0[:, :]), reads=[rb0], writes=[rk4])
        yield
        for tt in range(4):
            sc.op("pe", lambda e: e.matmul(b1[:, TS[tt]], lhsT=kt[:, TS[tt]], rhs=qt[:, TS[tt]], start=True, stop=True), reads=[rkt, rqt], writes=[rb1], inc=(tt == 3))
        yield
        sc.op("dve", lambda e: e.tensor_tensor(out=at4[:], in0=b1[:, :], in1=mask4[:].rearrange("p a b -> p (a b)"), op=ALU.mult), reads=[rb1, r_hc], writes=[rat])
        yield
        for tt in range(4):
            sc.op("pe", lambda e: e.matmul(b0[:, TS[tt]], lhsT=ktok4[:, TS[tt]], rhs=vtok[:, tt, hs], start=True, stop=True), reads=[rk4, rvt[tt]], writes=[rb0], inc=(tt == 3))
        yield
        for tt in range(4):
            last = tt * 128 + 127
            sc.op("dve", lambda e: e.scalar_tensor_tensor(out=st[:, h, :], in0=st[:, h, :], scalar=eb[:, last:last + 1], in1=b0[:, TS[tt]], op0=ALU.mult, op1=ALU.add),
                  reads=[r_st[h], reb, rb0], writes=[r_st[h]])
            yield
            if tt < 3:
                sc.op("dve", lambda e: e.tensor_copy(out=stb4[:, h, tt + 1, :], in_=st[:, h, :]), reads=[r_st[h]], writes=[r_stb[h][tt + 1]])
                yield
        for tt in range(4):
            sc.op("pe", lambda e: e.matmul(b1[:, TS[tt]], lhsT=vtok[:, tt, hs], rhs=at4[:, TS[tt]], start=True, stop=False), reads=[rvt[tt], rat], writes=[rb1], inc=False)
            sc.op("pe", lambda e: e.matmul(b1[:, TS[tt]], lhsT=stb4[:, h, tt, :], rhs=qt[:, TS[tt]], start=False, stop=True), reads=[r_stb[h][tt], rqt], writes=[rb1], inc=True)
        yield
        sc.op("dve", lambda e: e.tensor_copy(out=stb4[:, h, 0, :], in_=st[:, h, :]), reads=[r_st[h]], writes=[r_stb[h][0]])
        sc.op("act", lambda e: e.activation(out=sqo[:], in_=b1[:, :], func=AF.Square), reads=[rb1], writes=[rsqo])
        yield
        sc.op("pe", lambda e: e.matmul(b0[:, :], lhsT=ones_bf[:], rhs=sqo[:], start=True, stop=True), reads=[rsqo, r_const], writes=[rb0], inc=True)
        yield
        rn, rrn = s1, rs1
        sc.op("dve", lambda e: e.tensor_scalar(out=rn[:], in0=b0[:, :], scalar1=1.0 / 128, scalar2=EPS, op0=ALU.mult, op1=ALU.add), reads=[rb0], writes=[rrn])
        yield
        sc.op("act", lambda e: e.activation(out=rn[:], in_=rn[:], func=AF.Ln), reads=[rrn], writes=[rrn])
        yield
        sc.op("act", lambda e: e.activation(out=rn[:], in_=rn[:], func=AF.Exp, scale=-0.5), reads=[rrn], writes=[rrn])
        yield
        sc.op("dve", lambda e: e.scalar_tensor_tensor(out=rn[:], in0=b1[:, :], scalar=gn[:, h:h + 1], in1=rn[:], op0=ALU.mult, op1=ALU.mult), reads=[rb1, rrn, r_hc], writes=[rrn])
        yield
        sc.op("dve", lambda e: e.tensor_tensor(out=onb[:, h, :], in0=rn[:], in1=sg[:, h, :], op=ALU.mult), reads=[rrn, r_sg[h]], writes=[r_onb[h]])
        g_free[h] = True
        yield

    def session_item(g, col0, dst, rdst, h):
        hb, rhb = hbs[g]
        pp, rpp = psA.next()
        c0 = col0 + h * 128
        for k in range(8):
            sc.op("pe", lambda e: e.matmul(pp[:, :], lhsT=wi[:, k, c0:c0 + 128], rhs=hb[:, k, :], start=(k == 0), stop=(k == 7)),
                  reads=[r_wi[col0 // 1024], rhb], writes=[rpp], inc=(k == 7))
        sc.op("act", lambda e: e.activation(out=dst[:, h, :], in_=pp[:, :], func=AF.Silu), reads=[rpp], writes=[rdst[h]])

    def outproj_gen(g):
        for oc in range(8):
            pt, rpt = psA.next()
            for h in range(8):
                sc.op("pe", lambda e: e.matmul(pt[:, :], lhsT=wo2[:, h, oc * 128:(oc + 1) * 128], rhs=onb[:, h, :], start=(h == 0), stop=(h == 7)),
                      reads=[r_wo2[oc // 4], r_onb[h]], writes=[rpt], inc=(h == 7))
            sc.op("dve", lambda e: e.scalar_tensor_tensor(out=hr[:, oc, :], in0=hr[:, oc, :], scalar=ALPHA, in1=pt[:, :], op0=ALU.mult, op1=ALU.add),
                  reads=[rhr, rpt], writes=[rhr])
            yield

    def ln_store_gen(g):
        yield from layer_norm_gen(es, hr, rhr, 1, 1, psA, tm)
        sc.dma("sp", hC_v[:, :, gsl(g)], hr[:], reads=[rhr], writes=[r_hC[g]])
        if g + 1 < NG:
            sc.dma("sp", hr[:], hB_v[:, :, gsl(g + 1)], reads=[r_hB[g + 1]], writes=[rhr])
        yield

    state = {"chains_done": False}

    def bulk_gen(g):
        if g > 0:
            yield from outproj_gen(g - 1)
            yield from ln_store_gen(g - 1)
        if g + 1 < NG:
            yield from vproj_gen(g + 1)
            pending = [("q", h) for h in range(8)] + [("g", h) for h in range(8)]
            while pending:
                elig = [it for it in pending if (q_free[it[1]] if it[0] == "q" else g_free[it[1]])]
                if len(elig) >= 4 or (state["chains_done"] and elig):
                    for it in elig[:4]:
                        if it[0] == "q":
                            session_item(g + 1, 0, siluq, r_sq, it[1])
                        else:
                            session_item(g + 1, 3072, sg, r_sg, it[1])
                        pending.remove(it)
                        yield
                else:
                    yield

    for h in range(8):
        session_item(0, 0, siluq, r_sq, h)
    vproj(0)
    for h in range(8):
        session_item(0, 3072, sg, r_sg, h)
    for g in range(NG):
        if g + 1 < NG:
            load_hb(g + 1)
        for h in range(8):
            q_free[h] = False
            g_free[h] = False
        state["chains_done"] = False
        heads = list(range(8))
        slots = [None] * NWAY
        bulk = bulk_gen(g)
        bulk_alive = True
        rnd = 0
        while True:
            active = False
            for i in range(NWAY):
                if slots[i] is None and heads:
                    slots[i] = chain(g, heads.pop(0), i)
                if slots[i] is not None:
                    active = True
                    try:
                        next(slots[i])
                    except StopIteration:
                        slots[i] = None
            if not active and not heads:
                state["chains_done"] = True
            rnd += 1
            if bulk_alive and (rnd % 3 != 2 or state["chains_done"]):
                try:
                    next(bulk)
                except StopIteration:
                    bulk_alive = False
            if state["chains_done"] and not bulk_alive:
                break
    w1_v1 = w_ff1[1].rearrange("(k p) n -> p k n", p=128)
    for i in range(4):
        sc.dma("pool", wsh[:, :, i * 1024:(i + 1) * 1024], w1_v1[:, :, i * 1024:(i + 1) * 1024], writes=[r_wsh[i]])
    for _ in outproj_gen(NG - 1):
        pass
    for _ in ln_store_gen(NG - 1):
        pass
    sc.barrier()
    es.close()
    if stop_after <= 4:
        es_sh.close()
        return _finish(nc, sc, top)

    out_v = fm(out)
    r_out = [Res(f"out{g}") for g in range(NG)]
    ffn_phase(1, hC_v, r_hC, out_v, r_out, pre=(wsh, None, r_wsh, None))
    es_sh.close()
    return _finish(nc, sc, top)


def _finish(nc, sc, top):
    sc.finish()
    top.close()
    return nc


def _prep_shared(inputs):
    f = np.float32
    half = 16
    inv_freq = (10000.0 ** (-np.arange(half, dtype=np.float32) / half)).astype(f)
    invf = np.zeros((128, 1), f)
    invf[64:80, 0] = inv_freq
    invf[80:96, 0] = inv_freq
    pidx = np.arange(128)
    mask = (pidx[None, :] >= pidx[:, None]).astype(f)
    lnp = np.stack([inputs["ln1_g"], inputs["ln1_b"], inputs["ln2_g"], inputs["ln2_b"]], 0)
    lnp = np.ascontiguousarray(lnp.reshape(4, 2, 8, 128).transpose(3, 0, 1, 2)).astype(f)
    sh = {
        "w_in_e": np.ascontiguousarray(inputs["w_in_e"][0], f),
        "w_qb": np.ascontiguousarray(inputs["w_qb"][0], f),
        "w_kvb": np.ascontiguousarray(inputs["w_kvb"][0], f),
        "w_out_e": np.ascontiguousarray(inputs["w_out_e"][0], f),
        "sgu_wT": np.ascontiguousarray(np.transpose(inputs["sgu_w"][0], (2, 0, 1)), f),
        "sgu_b": np.ascontiguousarray(inputs["sgu_b"][0].reshape(1, 512), f),
        "sgu_g": np.ascontiguousarray(inputs["sgu_ln_g"][0].reshape(1, 512), f),
        "sgu_bb": np.ascontiguousarray(inputs["sgu_ln_b"][0].reshape(1, 512), f),
        "gq": np.ascontiguousarray(inputs["mla_gq"][0].reshape(2, 128).T, f),
        "gkv": np.ascontiguousarray(inputs["mla_gkv"][0].reshape(2, 128).T, f),
        "w_in_o": np.ascontiguousarray(inputs["w_in_o"][0], f),
        "w_out_o": np.ascontiguousarray(inputs["w_out_o"][0], f),
        "hg_lb": np.ascontiguousarray(inputs["hg_lb"].reshape(2, 8, 128).transpose(2, 0, 1), f),
        "hg_gn": np.ascontiguousarray(inputs["hg_gnorm"][0].reshape(8, 128).T, f),
        "lnp": lnp,
        "w_ff1": np.ascontiguousarray(inputs["w_ff1"], f),
        "w_ff2": np.ascontiguousarray(inputs["w_ff2"], f),
        "invf": invf,
        "mask_ge": mask,
        "ident": np.eye(128, dtype=f),
    }
    return sh


_NC_CACHE = {}


def kernel(**inputs):
    inputs = {k: np.asarray(v) for k, v in inputs.items()}
    x = inputs["x"]
    B, S, _ = x.shape
    if S not in _NC_CACHE:
        _NC_CACHE[S] = build(S)
    nc = _NC_CACHE[S]
    sh = _prep_shared(inputs)
    in_maps = []
    for b in range(B):
        m = dict(sh)
        m["xT"] = np.ascontiguousarray(x[b].T)
        m["pos"] = np.ascontiguousarray(inputs["positions"][b].reshape(1, S).astype(np.int32))
        in_maps.append(m)
    res = run_bass_kernel_spmd(nc, in_maps, core_ids=list(range(B)))
    outs = [np.asarray(r["out"]).T for r in res.results]
    return np.ascontiguousarray(np.stack(outs, 0).astype(np.float32))
```

```python
import bisect
import math
from contextlib import ExitStack

import numpy as np
import ml_dtypes
import concourse.bass as bass
import concourse.mybir as mybir
from concourse.bass_utils import run_bass_kernel_spmd

F32 = mybir.dt.float32
BF16 = mybir.dt.bfloat16
I32 = mybir.dt.int32
AF = mybir.ActivationFunctionType
ALU = mybir.AluOpType

D = 1024
DFF = 4096
DEPTH = 2
ALPHA = float((2 * DEPTH) ** 0.25)
EPS = 1e-5
MLA_SCALE = float(96 ** -0.5)
EVEN_IN = 1568
TG = 512
PI = math.pi
import os
VAR = int(os.environ.get('VAR', '0'))


class Tok:
    __slots__ = ("sem", "val", "step", "hv", "hc", "name")

    def __init__(self, nc, name, step):
        self.sem = nc.alloc_semaphore(name)
        self.val = 0
        self.step = step
        self.hv = []
        self.hc = []
        self.name = name


class Res:
    __slots__ = ("w", "r", "name", "excl")

    def __init__(self, name="", excl=False):
        self.w = None
        self.r = {}
        self.name = name
        self.excl = excl


class Eng:
    def __init__(self, nc, e, name, raw):
        self.e = e
        self.name = name
        self.raw = raw
        self.tok = Tok(nc, "t_" + name, 1)
        self.clock = {}
        self.dirty = False
        self.ring = []
        self.ri = 0


class Sched:
    def __init__(self, nc, nring=10):
        self.nc = nc
        self.engs = {
            "pe": Eng(nc, nc.tensor, "pe", False),
            "act": Eng(nc, nc.scalar, "act", True),
            "dve": Eng(nc, nc.vector, "dve", True),
            "pool": Eng(nc, nc.gpsimd, "pool", True),
            "sp": Eng(nc, nc.sync, "sp", False),
        }
        self.dtoks = []
        for q in ("sp", "pool"):
            for i in range(nring):
                t = Tok(nc, f"d_{q}{i}", 16)
                self.engs[q].ring.append(t)
                self.dtoks.append(t)
        self.nins = 0
        self.nwait = 0

    def _deps(self, E, reads, writes):
        deps = {}
        for r in reads:
            if r.w is not None:
                t, v = r.w
                if v > deps.get(t, 0):
                    deps[t] = v
            if r.excl:
                for t, v in r.r.items():
                    if t is not E.tok and v > deps.get(t, 0):
                        deps[t] = v
        for w in writes:
            if w.w is not None:
                t, v = w.w
                if v > deps.get(t, 0):
                    deps[t] = v
            for t, v in w.r.items():
                if v > deps.get(t, 0):
                    deps[t] = v
        waits = []
        own = None
        for t, v in deps.items():
            if t is E.tok:
                if E.raw and v > E.clock.get(t, 0):
                    own = (t, v)
                continue
            if E.clock.get(t, 0) >= v:
                continue
            waits.append((t, v))
        if own is not None:
            waits.append(own)
        return waits

    def _note(self, E, t, v):
        if E.clock.get(t, 0) < v:
            E.clock[t] = v
        i = bisect.bisect_right(t.hv, v) - 1
        if i >= 0:
            for t2, v2 in t.hc[i].items():
                if E.clock.get(t2, 0) < v2:
                    E.clock[t2] = v2
        E.dirty = True

    def _emit_waits(self, E, waits):
        last = None
        if waits:
            for t, v in waits[:-1]:
                E.e.wait_ge(t.sem, v)
                self.nwait += 1
            last = waits[-1]
            for t, v in waits:
                self._note(E, t, v)
        return last

    def op(self, eng, fn, reads=(), writes=(), inc=True):
        E = self.engs[eng]
        waits = self._deps(E, reads, writes)
        last = self._emit_waits(E, waits)
        ins = fn(E.e)
        self.nins += 1
        if last is not None:
            ins._wait_ge(last[0].sem, last[1])
        tok = E.tok
        if inc:
            ins.then_inc(tok.sem, 1)
            tok.val += 1
            cv = tok.val
        else:
            cv = tok.val + 1
        if E.dirty:
            tok.hv.append(cv)
            tok.hc.append(dict(E.clock))
            E.dirty = False
        for r in reads:
            if r.r.get(tok, 0) < cv:
                r.r[tok] = cv
        for w in writes:
            w.w = (tok, cv)
            w.r = {}
        return ins

    def dma(self, q, out, in_, reads=(), writes=()):
        E = self.engs[q]
        tok = E.ring[E.ri]
        E.ri = (E.ri + 1) % len(E.ring)
        waits = self._deps(E, reads, writes)
        if tok.val and E.clock.get(tok, 0) < tok.val:
            waits = [(t, v) for (t, v) in waits if t is not tok] + [(tok, tok.val)]
        last = self._emit_waits(E, waits)
        ins = E.e.dma_start(out=out, in_=in_)
        self.nins += 1
        if last is not None:
            ins._wait_ge(last[0].sem, last[1])
        ins.then_inc(tok.sem, 16)
        tok.val += 16
        cv = tok.val
        tok.hv.append(cv)
        tok.hc.append(dict(E.clock))
        for r in reads:
            if r.r.get(tok, 0) < cv:
                r.r[tok] = cv
        for w in writes:
            w.w = (tok, cv)
            w.r = {}
        return ins

    def barrier(self):
        toks = [E.tok for E in self.engs.values()] + self.dtoks
        for E in self.engs.values():
            for t in toks:
                if t is E.tok or t.val == 0:
                    continue
                if E.clock.get(t, 0) < t.val:
                    E.e.wait_ge(t.sem, t.val)
                    self._note(E, t, t.val)

    def finish(self):
        E = self.engs["sp"]
        for t in self.dtoks:
            if t.val and E.clock.get(t, 0) < t.val:
                E.e.wait_ge(t.sem, t.val)


class FreePool:
    def __init__(self, items):
        self.items = list(items)

    def avail(self, n=1):
        return len(self.items) >= n

    def get(self):
        return self.items.pop(0)

    def put(self, it):
        self.items.append(it)


class Ring:
    def __init__(self, items):
        self.items = items
        self.i = 0

    def next(self):
        it = self.items[self.i]
        self.i = (self.i + 1) % len(self.items)
        return it


def build(S_len=4096, stop_after=99, debug=False, cut=99):
    S = S_len
    NG = S // TG
    NT = S // 128
    nc = bass.Bass("TRN2", target_bir_lowering=False)
    sc = Sched(nc)

    def din(name, shape, dt=F32):
        return nc.dram_tensor(name, list(shape), dt, kind="ExternalInput").ap()

    xT = din("xT", [D, S])
    pos = din("pos", [1, S], I32)
    w_in_e = din("w_in_e", [D, EVEN_IN])
    w_qb = din("w_qb", [256, 768])
    w_kvb = din("w_kvb", [256, 1024])
    w_out_e = din("w_out_e", [D, D])
    sgu_wT = din("sgu_wT", [128, 4, 128])
    sgu_b = din("sgu_b", [1, 512])
    sgu_g = din("sgu_g", [1, 512])
    sgu_bb = din("sgu_bb", [1, 512])
    gq_in = din("gq", [128, 2])
    gkv_in = din("gkv", [128, 2])
    w_in_o = din("w_in_o", [D, 4096])
    w_out_o = din("w_out_o", [D, D])
    lb_in = din("hg_lb", [128, 2, 8])
    gn_in = din("hg_gn", [128, 8])
    lnp_in = din("lnp", [128, 4, 2, 8])
    w_ff1 = din("w_ff1", [2, D, DFF])
    w_ff2 = din("w_ff2", [2, DFF, D])
    invf_in = din("invf", [128, 1])
    mask_in = din("mask_ge", [128, 128])
    ident_in = din("ident", [128, 128])
    out = nc.dram_tensor("out", [D, S], F32, kind="ExternalOutput").ap()
    ikind = "ExternalOutput" if debug else "Internal"
    mixin = nc.dram_tensor("mixin", [D, S], BF16, kind=ikind).ap()
    hA = nc.dram_tensor("hA", [D, S], F32, kind=ikind).ap()
    hB = nc.dram_tensor("hB", [D, S], F32, kind=ikind).ap()
    hC = nc.dram_tensor("hC", [D, S], F32, kind=ikind).ap()
    r_mix = [Res(f"mix{g}") for g in range(NG)]
    r_hA = [Res(f"hA{g}") for g in range(NG)]
    r_hB = [Res(f"hB{g}") for g in range(NG)]
    r_hC = [Res(f"hC{g}") for g in range(NG)]

    def fm(ap):
        return ap.rearrange("(k p) s -> p k s", p=128)

    def gsl(g):
        return slice(g * TG, (g + 1) * TG)

    top = ExitStack()

    cnt = [0]

    def sb(es, name, shape, dt):
        cnt[0] += 1
        return es.enter_context(nc.sbuf_tensor(f"s{cnt[0]}_{name}", list(shape), dt))

    psum = [top.enter_context(nc.psum_tensor(f"ps{i}", [128, 512], F32)) for i in range(8)]
    rps = [Res(f"ps{i}", excl=True) for i in range(8)]

    def psring(idx):
        return Ring([(psum[i], rps[i]) for i in idx])

    ones_bf = sb(top, "ones_bf", [128, 128], BF16)
    ones_f = sb(top, "ones_f", [128, 128], F32)
    mask_f = sb(top, "mask_f", [128, 128], F32)
    mask_b = sb(top, "mask_b", [128, 128], BF16)
    ident_b = sb(top, "ident_b", [128, 128], BF16)
    lnp = sb(top, "lnp", [128, 4, 2, 8], F32)
    r_const = Res("const")
    sc.op("pool", lambda e: e.memset(ones_bf[:], 1.0), writes=[r_const])
    sc.op("pool", lambda e: e.memset(ones_f[:], 1.0), writes=[r_const])
    sc.dma("sp", mask_f[:], mask_in, writes=[r_const])
    sc.dma("pool", mask_b[:], mask_in, writes=[r_const])
    sc.dma("pool", ident_b[:], ident_in, writes=[r_const])
    sc.dma("sp", lnp[:], lnp_in, writes=[r_const])

    def layer_norm(es_tmp, buf, rbuf, layer, which, psr, tmps):
        for _ in layer_norm_gen(es_tmp, buf, rbuf, layer, which, psr, tmps):
            pass

    def layer_norm_gen(es_tmp, buf, rbuf, layer, which, psr, tmps):
        (rb_ring, rs_ring, mean, msq, rstd, mr) = tmps
        gi, bi = (0, 1) if which == 1 else (2, 3)
        p_sum, r_sum = psr.next()
        p_sq, r_sq = psr.next()
        def stats_mm(c, rb, rrb, rs, rrs):
            sc.op("pe", lambda e: e.matmul(p_sum[:, :], lhsT=ones_bf[:], rhs=rb[:], start=(c == 0), stop=(c == 7)),
                  reads=[rrb, r_const], writes=[r_sum], inc=True)
            sc.op("pe", lambda e: e.matmul(p_sq[:, :], lhsT=ones_bf[:], rhs=rs[:], start=(c == 0), stop=(c == 7)),
                  reads=[rrs, r_const], writes=[r_sq], inc=True)

        prev = None
        for c in range(8):
            if prev is not None:
                stats_mm(*prev)
            rb, rrb = rb_ring.next()
            rs, rrs = rs_ring.next()
            sc.op("act", lambda e: e.copy(out=rb[:], in_=buf[:, c, :]), reads=[rbuf], writes=[rrb])
            sc.op("act", lambda e: e.activation(out=rs[:], in_=buf[:, c, :], func=AF.Square), reads=[rbuf], writes=[rrs])
            prev = (c, rb, rrb, rs, rrs)
            yield
        stats_mm(*prev)
        yield
        (mean_t, r_mean), (msq_t, r_msq), (rstd_t, r_rstd), (mr_t, r_mr) = mean, msq, rstd, mr
        sc.op("act", lambda e: e.activation(out=mean_t[:], in_=p_sum[:, :], func=AF.Copy, scale=1.0 / D), reads=[r_sum], writes=[r_mean])
        sc.op("dve", lambda e: e.tensor_tensor(out=msq_t[:], in0=mean_t[:], in1=mean_t[:], op=ALU.mult), reads=[r_mean], writes=[r_msq])
        sc.op("dve", lambda e: e.scalar_tensor_tensor(out=msq_t[:], in0=p_sq[:, :], scalar=1.0 / D, in1=msq_t[:], op0=ALU.mult, op1=ALU.subtract),
              reads=[r_sq, r_msq], writes=[r_msq])
        sc.op("dve", lambda e: e.tensor_scalar(out=msq_t[:], in0=msq_t[:], scalar1=EPS, scalar2=None, op0=ALU.add), reads=[r_msq], writes=[r_msq])
        sc.op("act", lambda e: e.activation(out=rstd_t[:], in_=msq_t[:], func=AF.Ln), reads=[r_msq], writes=[r_rstd])
        sc.op("act", lambda e: e.activation(out=rstd_t[:], in_=rstd_t[:], func=AF.Exp, scale=-0.5), reads=[r_rstd], writes=[r_rstd])
        mr_t, r_mr = mean_t, r_mean
        sc.op("dve", lambda e: e.tensor_tensor(out=mr_t[:], in0=mean_t[:], in1=rstd_t[:], op=ALU.mult), reads=[r_mean, r_rstd], writes=[r_mr])
        yield
        for c in range(8):
            sc.op("dve", lambda e: e.tensor_tensor(out=buf[:, c, :], in0=buf[:, c, :], in1=rstd_t[:], op=ALU.mult), reads=[rbuf, r_rstd], writes=[rbuf])
            yield
            sc.op("dve", lambda e: e.tensor_tensor(out=buf[:, c, :], in0=buf[:, c, :], in1=mr_t[:], op=ALU.subtract), reads=[rbuf, r_mr], writes=[rbuf])
            sc.op("act", lambda e: e.activation(out=buf[:, c, :], in_=buf[:, c, :], func=AF.Identity,
                                                scale=lnp[:, gi, layer, c:c + 1], bias=lnp[:, bi, layer, c:c + 1]),
                  reads=[rbuf, r_const], writes=[rbuf])
            yield

    def ln_tmps(es):
        rb_ring = Ring([(sb(es, f"ln_rb{i}", [128, TG], BF16), Res(f"ln_rb{i}")) for i in range(2)])
        rs_ring = Ring([(sb(es, f"ln_rs{i}", [128, TG], BF16), Res(f"ln_rs{i}")) for i in range(2)])
        t = [(sb(es, f"ln_t{i}", [128, TG], F32), Res(f"ln_t{i}")) for i in range(3)]
        return (rb_ring, rs_ring, t[0], t[1], t[2], t[0])

    def load_w(dst, src_ap, rlist, nparts, axis_len, q="pool"):
        step = axis_len // nparts
        for i in range(nparts):
            sc.dma(q, dst[:, :, i * step:(i + 1) * step], src_ap[:, :, i * step:(i + 1) * step], writes=[rlist[i]])

    es01 = ExitStack()
    cqn = sb(es01, "cqn", [128, 2, S], BF16)
    ckvn = sb(es01, "ckvn", [128, 2, S], BF16)
    krot = sb(es01, "krot", [96, S], BF16)
    cosT = sb(es01, "cosT", [96, S], F32)
    sinT = sb(es01, "sinT", [96, S], F32)
    r_cqn = [Res(f"cqn{g}") for g in range(NG)]
    r_ckvn = [Res(f"ckvn{g}") for g in range(NG)]
    r_krot = [Res(f"krot{g}") for g in range(NG)]
    r_trig = Res("trig")

    es = ExitStack()
    w_in = sb(es, "w_in", [128, 8, EVEN_IN], BF16)
    w_kr = sb(es, "w_kr", [128, 8, 2, 96], BF16)
    wsg = sb(es, "wsg", [128, 4, 128], BF16)
    wsg_f = sb(es, "wsg_f", [128, 4, 128], F32)
    lnG = sb(es, "lnG", [128, 512], F32)
    lnB = sb(es, "lnB", [128, 512], F32)
    bsg = sb(es, "bsg", [1, 512], BF16)
    gq = sb(es, "gq", [128, 2], F32)
    gkv = sb(es, "gkv", [128, 2], F32)
    invf = sb(es, "invf", [128, 1], F32)
    r_win = [Res(f"w_in{i}") for i in range(2)]
    r_p0c = Res("p0c")
    w_in_v = w_in_e.rearrange("(k p) n -> p k n", p=128)
    sc.dma("pool", w_in[:, :, 0:544], w_in_v[:, :, 0:544], writes=[r_win[0]])
    sc.dma("pool", w_in[:, :, 544:EVEN_IN], w_in_v[:, :, 544:EVEN_IN], writes=[r_win[1]])
    sc.dma("sp", wsg_f[:], sgu_wT, writes=[r_p0c])
    sc.dma("sp", lnG[:], bass.AP(sgu_g.tensor, 0, [[0, 128], [1, 512]]), writes=[r_p0c])
    sc.dma("sp", lnB[:], bass.AP(sgu_bb.tensor, 0, [[0, 128], [1, 512]]), writes=[r_p0c])
    sc.dma("pool", bsg[:], sgu_b, writes=[r_p0c])
    sc.dma("sp", gq[:], gq_in, writes=[r_p0c])
    sc.dma("sp", gkv[:], gkv_in, writes=[r_p0c])
    sc.dma("sp", invf[:], invf_in, writes=[r_p0c])
    for g4 in range(4):
        sc.op("dve", lambda e: e.tensor_tensor(out=wsg[:, g4, :], in0=wsg_f[:, g4, :], in1=mask_f[:], op=ALU.mult), reads=[r_p0c, r_const], writes=[r_p0c])
    r_wkr = Res("w_kr")
    sc.op("pool", lambda e: e.memset(w_kr[:], 0.0), writes=[r_wkr])
    sc.op("pool", lambda e: e.tensor_copy(out=w_kr[:, :, 0, 64:96], in_=w_in[:, :, 512:544]), reads=[r_win[0]], writes=[r_wkr])
    sc.op("pool", lambda e: e.tensor_scalar(out=w_kr[:, :, 1, 64:80], in0=w_in[:, :, 528:544], scalar1=-1.0, scalar2=None, op0=ALU.mult), reads=[r_win[0]], writes=[r_wkr])
    sc.op("pool", lambda e: e.tensor_copy(out=w_kr[:, :, 1, 80:96], in_=w_in[:, :, 512:528]), reads=[r_win[0]], writes=[r_wkr])

    if stop_after == -3:
        sc.barrier()
        es.close()
        es01.close()
        return _finish(nc, sc, top)
    es_trig = ExitStack()
    posi = sb(es_trig, "posi", [96, S], I32)
    ang = sb(es_trig, "ang", [96, S], F32)
    kf = sb(es_trig, "kf", [96, S], F32)
    r_tr = Res("trtmp")
    P = slice(64, 96)
    sc.dma("sp", posi[P, :], bass.AP(pos.tensor, 0, [[0, 32], [1, S]]), writes=[r_tr])
    sc.op("dve", lambda e: e.tensor_copy(out=ang[P, :], in_=posi[P, :]), reads=[r_tr], writes=[r_tr])
    sc.op("dve", lambda e: e.tensor_scalar(out=ang[P, :], in0=ang[P, :], scalar1=invf[P, 0:1], scalar2=None, op0=ALU.mult), reads=[r_tr, r_p0c], writes=[r_tr])
    sc.op("dve", lambda e: e.tensor_scalar(out=kf[P, :], in0=ang[P, :], scalar1=float(1.0 / (2 * PI)), scalar2=None, op0=ALU.mult), reads=[r_tr], writes=[r_tr])
    sc.op("dve", lambda e: e.tensor_copy(out=posi[P, :], in_=kf[P, :]), reads=[r_tr], writes=[r_tr])
    sc.op("dve", lambda e: e.tensor_copy(out=kf[P, :], in_=posi[P, :]), reads=[r_tr], writes=[r_tr])
    C1 = 6.28125
    C2 = float(2 * PI - C1)
    sc.op("dve", lambda e: e.scalar_tensor_tensor(out=ang[P, :], in0=kf[P, :], scalar=-C1, in1=ang[P, :], op0=ALU.mult, op1=ALU.add), reads=[r_tr], writes=[r_tr])
    sc.op("dve", lambda e: e.scalar_tensor_tensor(out=ang[P, :], in0=kf[P, :], scalar=-C2, in1=ang[P, :], op0=ALU.mult, op1=ALU.add), reads=[r_tr], writes=[r_tr])

    def wrap(t):
        sc.op("dve", lambda e: e.tensor_scalar(out=kf[P, :], in0=t[P, :], scalar1=float(PI), scalar2=float(2 * PI), op0=ALU.is_gt, op1=ALU.mult), reads=[r_tr], writes=[r_tr])
        sc.op("dve", lambda e: e.tensor_tensor(out=t[P, :], in0=t[P, :], in1=kf[P, :], op=ALU.subtract), reads=[r_tr], writes=[r_tr])
        sc.op("dve", lambda e: e.tensor_scalar(out=kf[P, :], in0=t[P, :], scalar1=float(-PI), scalar2=float(2 * PI), op0=ALU.is_lt, op1=ALU.mult), reads=[r_tr], writes=[r_tr])
        sc.op("dve", lambda e: e.tensor_tensor(out=t[P, :], in0=t[P, :], in1=kf[P, :], op=ALU.add), reads=[r_tr], writes=[r_tr])

    wrap(ang)
    sc.op("act", lambda e: e.activation(out=sinT[P, :], in_=ang[P, :], func=AF.Sin), reads=[r_tr], writes=[r_trig])
    sc.op("dve", lambda e: e.tensor_scalar(out=ang[P, :], in0=ang[P, :], scalar1=float(PI / 2), scalar2=None, op0=ALU.add), reads=[r_tr], writes=[r_tr])
    wrap(ang)
    sc.op("act", lambda e: e.activation(out=cosT[P, :], in_=ang[P, :], func=AF.Sin), reads=[r_tr], writes=[r_trig])
    sc.barrier()
    es_trig.close()

    if stop_after == -2:
        sc.dma("sp", out[0:32, :], sinT[P, :], reads=[r_trig])
        sc.dma("sp", out[32:64, :], cosT[P, :], reads=[r_trig])
        sc.barrier()
        es.close()
        es01.close()
        return _finish(nc, sc, top)
    xb_ring = Ring([(sb(es, f"xb{i}", [128, 8, TG], BF16), Res(f"xb{i}")) for i in range(2)])
    cq_ring = Ring([(sb(es, f"cq{i}", [128, 2, TG], F32), Res(f"cq{i}")) for i in range(2)])
    sq_ring = Ring([(sb(es, f"sq{i}", [128, 2, TG], BF16), Res(f"sq{i}")) for i in range(2)])
    f_ring = Ring([(sb(es, f"ft{i}", [128, TG], F32), Res(f"ft{i}")) for i in range(12)])
    gu_ring = Ring([(sb(es, f"gu{i}", [128, 4, TG], F32), [Res(f"gu{i}_{j}") for j in range(4)]) for i in range(2)])
    bo_ring = Ring([(sb(es, f"bo{i}", [128, 4, TG], BF16), Res(f"bo{i}")) for i in range(2)])
    vnb_ring = Ring([(sb(es, f"vnb{i}", [128, TG], BF16), Res(f"vnb{i}")) for i in range(2)])
    st_ring = Ring([(sb(es, f"bst{i}", [128, 8], F32), Res(f"bst{i}")) for i in range(2)])
    psA = psring([0, 1, 2, 3])
    psB = psring([4, 5])
    psC = psring([6, 7])
    GC = float(math.sqrt(0.044715))
    GS = float(2.0 * math.sqrt(2.0 / PI))

    def gelu_from_psum(pt, rpt, dst_ap, rdst, rows=slice(0, 128)):
        t1, rt1 = f_ring.next()
        sc.op("act", lambda e: e.activation(out=t1[:], in_=pt[:, :], func=AF.Square, scale=GC), reads=[rpt], writes=[rt1])
        sc.op("dve", lambda e: e.scalar_tensor_tensor(out=t1[:], in0=t1[:], scalar=1.0, in1=pt[:, :], op0=ALU.add, op1=ALU.mult), reads=[rt1, rpt], writes=[rt1])
        sc.op("act", lambda e: e.activation(out=t1[:], in_=t1[:], func=AF.Exp, scale=-GS), reads=[rt1], writes=[rt1])
        sc.op("act", lambda e: e.activation(out=t1[:], in_=t1[:], func=AF.Ln, bias=1.0), reads=[rt1], writes=[rt1])
        sc.op("act", lambda e: e.activation(out=t1[:], in_=t1[:], func=AF.Exp, scale=-1.0), reads=[rt1], writes=[rt1])
        sc.op("dve", lambda e: e.tensor_tensor(out=dst_ap, in0=t1[:], in1=pt[:, :], op=ALU.mult), reads=[rt1, rpt], writes=[rdst])

    def rms_feat(cq_t, rcq, sq_t, rsq, gvec, dst, rdst, gcols):
        pss, rpss = psB.next()
        for j in range(2):
            sc.op("pe", lambda e: e.matmul(pss[:, :], lhsT=ones_bf[:], rhs=sq_t[:, j, :], start=(j == 0), stop=(j == 1)), reads=[rsq, r_const], writes=[rpss])
        t1, rt1 = f_ring.next()
        sc.op("dve", lambda e: e.tensor_scalar(out=t1[:], in0=pss[:, :], scalar1=1.0 / 256, scalar2=EPS, op0=ALU.mult, op1=ALU.add), reads=[rpss], writes=[rt1])
        sc.op("act", lambda e: e.activation(out=t1[:], in_=t1[:], func=AF.Ln), reads=[rt1], writes=[rt1])
        sc.op("act", lambda e: e.activation(out=t1[:], in_=t1[:], func=AF.Exp, scale=-0.5), reads=[rt1], writes=[rt1])
        for j in range(2):
            sc.op("dve", lambda e: e.scalar_tensor_tensor(out=dst[:, j, gcols], in0=cq_t[:, j, :], scalar=gvec[:, j:j + 1], in1=t1[:], op0=ALU.mult, op1=ALU.mult),
                  reads=[rcq, rt1, r_p0c], writes=[rdst])

    xT_v = fm(xT)
    mix_v = fm(mixin)
    gctx = {}
    fpool = FreePool(f_ring.items)
    cqpool = FreePool(cq_ring.items)
    sqpool = FreePool(sq_ring.items)
    stpool = FreePool(st_ring.items)
    vnbpool = FreePool(vnb_ring.items)
    pA = FreePool(psA.items)
    pB = FreePool(psB.items)
    pC = FreePool(psC.items)

    def g_setup(g):
        xb, rxb = xb_ring.next()
        sc.dma("pool", xb[:], xT_v[:, :, gsl(g)], writes=[rxb])
        gu, rgu = gu_ring.next()
        bo, rbo = bo_ring.next()
        gctx[g] = dict(xb=xb, rxb=rxb, gu=gu, rgu=rgu, bo=bo, rbo=rbo, u_done=0, v_done=0, done=0)

    def gelu_gen(pt, rpt, dst_ap, rdst):
        while not fpool.avail():
            yield
        t1, rt1 = it = fpool.get()
        sc.op("act", lambda e: e.activation(out=t1[:], in_=pt[:, :], func=AF.Square, scale=GC), reads=[rpt], writes=[rt1])
        yield
        sc.op("dve", lambda e: e.scalar_tensor_tensor(out=t1[:], in0=t1[:], scalar=1.0, in1=pt[:, :], op0=ALU.add, op1=ALU.mult), reads=[rt1, rpt], writes=[rt1])
        yield
        sc.op("act", lambda e: e.activation(out=t1[:], in_=t1[:], func=AF.Exp, scale=-GS), reads=[rt1], writes=[rt1])
        yield
        sc.op("act", lambda e: e.activation(out=t1[:], in_=t1[:], func=AF.Ln, bias=1.0), reads=[rt1], writes=[rt1])
        yield
        sc.op("act", lambda e: e.activation(out=t1[:], in_=t1[:], func=AF.Exp, scale=-1.0), reads=[rt1], writes=[rt1])
        yield
        sc.op("dve", lambda e: e.tensor_tensor(out=dst_ap, in0=t1[:], in1=pt[:, :], op=ALU.mult), reads=[rt1, rpt], writes=[rdst])
        fpool.put(it)
        yield

    def gen_cq(g, which):
        c = gctx[g]
        xb, rxb = c["xb"], c["rxb"]
        gc = gsl(g)
        while not (cqpool.avail() and sqpool.avail()):
            yield
        cq_t, rcq = icq = cqpool.get()
        sq_t, rsq = isq = sqpool.get()
        for j in range(2):
            col = (which * 2 + j) * 128
            while not pA.avail():
                yield
            pt, rpt = ipt = pA.get()
            for k in range(8):
                sc.op("pe", lambda e: e.matmul(pt[:, :], lhsT=w_in[:, k, col:col + 128], rhs=xb[:, k, :], start=(k == 0), stop=(k == 7)),
                      reads=[r_win[0], rxb], writes=[rpt], inc=(k == 7))
            yield
            sc.op("act", lambda e: e.activation(out=sq_t[:, j, :], in_=pt[:, :], func=AF.Square), reads=[rpt], writes=[rsq])
            yield
            sc.op("dve", lambda e: e.tensor_copy(out=cq_t[:, j, :], in_=pt[:, :]), reads=[rpt], writes=[rcq])
            pA.put(ipt)
            yield
        gvec, dst, rdst = (gq, cqn, r_cqn[g]) if which == 0 else (gkv, ckvn, r_ckvn[g])
        while not (pB.avail() and fpool.avail()):
            yield
        pss, rpss = ipss = pB.get()
        t1, rt1 = it1 = fpool.get()
        for j in range(2):
            sc.op("pe", lambda e: e.matmul(pss[:, :], lhsT=ones_bf[:], rhs=sq_t[:, j, :], start=(j == 0), stop=(j == 1)), reads=[rsq, r_const], writes=[rpss])
        sqpool.put(isq)
        yield
        sc.op("dve", lambda e: e.tensor_scalar(out=t1[:], in0=pss[:, :], scalar1=1.0 / 256, scalar2=EPS, op0=ALU.mult, op1=ALU.add), reads=[rpss], writes=[rt1])
        pB.put(ipss)
        yield
        sc.op("act", lambda e: e.activation(out=t1[:], in_=t1[:], func=AF.Ln), reads=[rt1], writes=[rt1])
        yield
        sc.op("act", lambda e: e.activation(out=t1[:], in_=t1[:], func=AF.Exp, scale=-0.5), reads=[rt1], writes=[rt1])
        yield
        for j in range(2):
            sc.op("dve", lambda e: e.scalar_tensor_tensor(out=dst[:, j, gc], in0=cq_t[:, j, :], scalar=gvec[:, j:j + 1], in1=t1[:], op0=ALU.mult, op1=ALU.mult),
                  reads=[rcq, rt1, r_p0c], writes=[rdst])
            yield
        cqpool.put(icq)
        fpool.put(it1)

    def gen_krope(g):
        c = gctx[g]
        xb, rxb = c["xb"], c["rxb"]
        gc = gsl(g)
        while not (pA.avail(2) and fpool.avail(2)):
            yield
        pa, rpa = ipa = pA.get()
        pb, rpb = ipb = pA.get()
        t1, rt1 = it1 = fpool.get()
        t2, rt2 = it2 = fpool.get()
        for (pp, rpp, v) in ((pa, rpa, 0), (pb, rpb, 1)):
            for k in range(8):
                sc.op("pe", lambda e: e.matmul(pp[0:96, :], lhsT=w_kr[:, k, v, :], rhs=xb[:, k, :], start=(k == 0), stop=(k == 7)),
                      reads=[r_wkr, rxb], writes=[rpp], inc=(k == 7))
            yield
        sc.op("dve", lambda e: e.tensor_tensor(out=t1[P, :], in0=pa[P, :], in1=cosT[P, gc], op=ALU.mult), reads=[rpa, r_trig], writes=[rt1])
        pA.put(ipa)
        yield
        sc.op("dve", lambda e: e.tensor_tensor(out=t2[P, :], in0=pb[P, :], in1=sinT[P, gc], op=ALU.mult), reads=[rpb, r_trig], writes=[rt2])
        pA.put(ipb)
        yield
        sc.op("dve", lambda e: e.tensor_tensor(out=krot[P, gc], in0=t1[P, :], in1=t2[P, :], op=ALU.add), reads=[rt1, rt2], writes=[r_krot[g]])
        fpool.put(it1)
        fpool.put(it2)
        yield

    def gen_u(g, j):
        c = gctx[g]
        xb, rxb, gu, rgu = c["xb"], c["rxb"], c["gu"], c["rgu"]
        col = 544 + j * 128
        while not pA.avail():
            yield
        pt, rpt = ipt = pA.get()
        for k in range(8):
            sc.op("pe", lambda e: e.matmul(pt[:, :], lhsT=w_in[:, k, col:col + 128], rhs=xb[:, k, :], start=(k == 0), stop=(k == 7)),
                  reads=[r_win[1], rxb], writes=[rpt], inc=(k == 7))
        yield
        yield from gelu_gen(pt, rpt, gu[:, j, :], rgu[j])
        pA.put(ipt)
        c["u_done"] += 1

    def gen_v(g, tt):
        c = gctx[g]
        xb, rxb, gu, rgu, bo, rbo = c["xb"], c["rxb"], c["gu"], c["rgu"], c["bo"], c["rbo"]
        while not (pA.avail() and fpool.avail(2)):
            yield
        pv, rpv = ipv = pA.get()
        gv, rgv = igv = fpool.get()
        for k in range(8):
            sc.op("pe", lambda e: e.matmul(pv[:, :], lhsT=xb[:, k, tt * 128:(tt + 1) * 128], rhs=w_in[:, k, 1056:1568], start=(k == 0), stop=(k == 7)),
                  reads=[r_win[1], rxb], writes=[rpv], inc=(k == 7))
        yield
        yield from gelu_gen(pv, rpv, gv[:], rgv)
        pA.put(ipv)
        while not stpool.avail():
            yield
        st, rst = ist = stpool.get()
        sc.op("dve", lambda e: e.bn_stats(out=st[:, 0:6], in_=gv[:]), reads=[rgv], writes=[rst])
        yield
        sc.op("dve", lambda e: e.bn_aggr(out=st[:, 6:8], in_=st[:, 0:6]), reads=[rst], writes=[rst])
        yield
        sc.op("dve", lambda e: e.tensor_scalar(out=st[:, 7:8], in0=st[:, 7:8], scalar1=EPS, scalar2=None, op0=ALU.add), reads=[rst], writes=[rst])
        yield
        sc.op("act", lambda e: e.activation(out=st[:, 7:8], in_=st[:, 7:8], func=AF.Ln), reads=[rst], writes=[rst])
        yield
        sc.op("act", lambda e: e.activation(out=st[:, 7:8], in_=st[:, 7:8], func=AF.Exp, scale=-0.5), reads=[rst], writes=[rst])
        yield
        sc.op("dve", lambda e: e.tensor_scalar(out=gv[:], in0=gv[:], scalar1=st[:, 6:7], scalar2=st[:, 7:8], op0=ALU.subtract, op1=ALU.mult), reads=[rgv, rst], writes=[rgv])
        stpool.put(ist)
        yield
        sc.op("dve", lambda e: e.tensor_tensor(out=gv[:], in0=gv[:], in1=lnG[:], op=ALU.mult), reads=[rgv, r_p0c], writes=[rgv])
        yield
        while not (vnbpool.avail() and pC.avail()):
            yield
        vnb, rvnb = ivnb = vnbpool.get()
        pm, rpm = ipm = pC.get()
        sc.op("dve", lambda e: e.tensor_tensor(out=vnb[:], in0=gv[:], in1=lnB[:], op=ALU.add), reads=[rgv, r_p0c], writes=[rvnb])
        fpool.put(igv)
        yield
        for g4 in range(4):
            cs = slice(g4 * 128, (g4 + 1) * 128)
            sc.op("pe", lambda e: e.matmul(pm[:, cs], lhsT=vnb[:, cs], rhs=wsg[:, g4, :], start=True, stop=False), reads=[rvnb, r_p0c], writes=[rpm], inc=False)
            sc.op("pe", lambda e: e.matmul(pm[:, cs], lhsT=ones_bf[0:1, 0:128], rhs=bsg[0:1, cs], start=False, stop=True), reads=[r_const, r_p0c], writes=[rpm], inc=(g4 == 3))
        vnbpool.put(ivnb)
        yield
        while c["u_done"] < 4:
            yield
        sc.op("dve", lambda e: e.tensor_tensor(out=bo[:, :, tt * 128:(tt + 1) * 128], in0=pm[:, :].rearrange("p (a b) -> p a b", a=4), in1=gu[:, :, tt * 128:(tt + 1) * 128], op=ALU.mult),
              reads=[rpm] + rgu, writes=[rbo])
        pC.put(ipm)
        c["v_done"] += 1
        if c["v_done"] == 4:
            sc.dma("sp", mix_v[:, 4:8, gsl(g)], bo[:], reads=[rbo], writes=[r_mix[g]])
        yield

    tasks = []
    for g in range(NG):
        tasks.append(("setup", g))
        tasks += [(gen_cq, g, 0), (gen_cq, g, 1), (gen_krope, g)]
        tasks += [(gen_u, g, j) for j in range(4)]
        tasks += [(gen_v, g, tt) for tt in range(4)]
    NW0 = 4
    slots0 = [None] * NW0
    while tasks or any(sl is not None for sl in slots0):
        for i in range(NW0):
            if slots0[i] is None and tasks:
                t = tasks.pop(0)
                if t[0] == "setup":
                    g_setup(t[1])
                    t = tasks.pop(0)
                slots0[i] = t[0](*t[1:])
            if slots0[i] is not None:
                try:
                    next(slots0[i])
                except StopIteration:
                    slots0[i] = None
    sc.barrier()
    es.close()
    if stop_after <= 0:
        es01.close()
        return _finish(nc, sc, top)

    es = ExitStack()
    wq = sb(es, "wq", [128, 2, 8 * 96 + 32], BF16)
    wqs = sb(es, "wqs", [128, 2, 8 * 96 + 32], BF16)
    wkv = sb(es, "wkv", [128, 2, 8, 128], BF16)
    vaug = sb(es, "vaug", [128, NT, 8, 128], BF16)
    r_wq = Res("wq")
    r_wqs = Res("wqs")
    r_wkv = Res("wkv")
    r_vaug = [Res(f"vaug{t}") for t in range(NT)]
    r_vones = Res("vones")
    sc.op("pool", lambda e: e.memset(wq[:, :, 768:800], 0.0), writes=[r_wq])
    sc.dma("pool", wq[:, :, 0:768], w_qb.rearrange("(k p) n -> p k n", p=128), writes=[r_wq])
    sc.dma("pool", wkv[:], w_kvb.rearrange("(k p) (h d) -> p k h d", p=128, h=8), writes=[r_wkv])
    sc.op("pool", lambda e: e.memset(wqs[:], 0.0), writes=[r_wqs])
    for k in range(2):
        wq4 = wq[:, k, 0:768].rearrange("p (h d) -> p h d", h=8)
        wqs4 = wqs[:, k, 0:768].rearrange("p (h d) -> p h d", h=8)
        sc.op("pool", lambda e: e.tensor_scalar(out=wqs4[:, :, 64:80], in0=wq4[:, :, 80:96], scalar1=-1.0, scalar2=None, op0=ALU.mult), reads=[r_wq], writes=[r_wqs])
        sc.op("pool", lambda e: e.tensor_copy(out=wqs4[:, :, 80:96], in_=wq4[:, :, 64:80]), reads=[r_wq], writes=[r_wqs])
    sc.op("pool", lambda e: e.memset(vaug[:, :, :, 64:128], 1.0), writes=[r_vones])
    psP = psring([0, 1])
    psS = psring([2, 3, 4, 5])
    psO = psring([6, 7])
    psBC = psP
    negm = sb(es, "negm", [128, 128], BF16)
    r_negm = Res("negm")
    sc.op("dve", lambda e: e.tensor_scalar(out=negm[:], in0=mask_f[:], scalar1=-1.0, scalar2=30000.0, op0=ALU.add, op1=ALU.mult), reads=[r_const], writes=[r_negm])
    for tt in range(NT):
        pv, rpv = psP.next()
        ts_ = slice(tt * 128, (tt + 1) * 128)
        for k in range(2):
            sc.op("pe", lambda e: e.matmul(pv[:, :].rearrange("p (h d) -> p h d", h=8), lhsT=ckvn[:, k, ts_], rhs=wkv[:, k, :, 64:128], start=(k == 0), stop=(k == 1)),
                  reads=[r_ckvn[tt // 4], r_wkv], writes=[rpv], inc=(k == 1))
        eng = "act" if tt % 2 == 0 else "dve"
        if eng == "act":
            sc.op("act", lambda e: e.copy(out=vaug[:, tt, :, 0:64], in_=pv[:, :].rearrange("p (h d) -> p h d", h=8)), reads=[rpv], writes=[r_vaug[tt]])
        else:
            sc.op("dve", lambda e: e.tensor_copy(out=vaug[:, tt, :, 0:64], in_=pv[:, :].rearrange("p (h d) -> p h d", h=8)), reads=[rpv], writes=[r_vaug[tt]])

    qT_ring = Ring([(sb(es, f"qT{i}", [128, S], BF16), [Res(f"qT{i}_{g}") for g in range(NG)]) for i in range(2)])
    kT_ring = Ring([(sb(es, f"kT{i}", [128, S], BF16), [Res(f"kT{i}_{g}") for g in range(NG)]) for i in range(2)])
    for (t_, rl_) in qT_ring.items + kT_ring.items:
        for g in range(NG):
            sc.op("pool", lambda e: e.memset(t_[96:128, gsl(g)], 0.0), writes=[rl_[g]])
    pT_ring = Ring([(sb(es, f"pT{i}", [128, TG], BF16), Res(f"pT{i}")) for i in range(6)])
    rt_ring = Ring([(sb(es, f"rt{i}", [96, TG], F32), Res(f"rt{i}")) for i in range(4)])
    on_ring = Ring([(sb(es, f"onum{i}", [64, TG], F32), Res(f"onum{i}")) for i in range(2)])
    rd_ring = Ring([(sb(es, f"rden{i}", [65, TG], F32), Res(f"rden{i}")) for i in range(2)])
    ao_ring = Ring([(sb(es, f"ao{i}", [64, TG], BF16), Res(f"ao{i}")) for i in range(2)])

    def proj_head(h):
        qT, rq = qT_ring.next()
        kT, rk = kT_ring.next()
        for g in range(NG):
            gc = gsl(g)
            pa, rpa = psP.next()
            pb, rpb = psP.next()
            for k in range(2):
                sc.op("pe", lambda e: e.matmul(pa[:, :], lhsT=wq[:, k, h * 96:h * 96 + 128], rhs=cqn[:, k, gc], start=(k == 0), stop=(k == 1)), reads=[r_wq, r_cqn[g]], writes=[rpa], inc=(k == 1))
            for k in range(2):
                sc.op("pe", lambda e: e.matmul(pb[:, :], lhsT=wqs[:, k, h * 96:h * 96 + 128], rhs=cqn[:, k, gc], start=(k == 0), stop=(k == 1)), reads=[r_wqs, r_cqn[g]], writes=[rpb], inc=(k == 1))
            sc.op("dve", lambda e: e.tensor_copy(out=qT[0:64, gc], in_=pa[0:64, :]), reads=[rpa], writes=[rq[g]])
            t1, rt1 = rt_ring.next()
            t2, rt2 = rt_ring.next()
            sc.op("dve", lambda e: e.tensor_tensor(out=t1[P, :], in0=pa[P, :], in1=cosT[P, gc], op=ALU.mult), reads=[rpa, r_trig], writes=[rt1])
            sc.op("dve", lambda e: e.tensor_tensor(out=t2[P, :], in0=pb[P, :], in1=sinT[P, gc], op=ALU.mult), reads=[rpb, r_trig], writes=[rt2])
            sc.op("pool", lambda e: e.tensor_tensor(out=qT[P, gc], in0=t1[P, :], in1=t2[P, :], op=ALU.add), reads=[rt1, rt2], writes=[rq[g]])
            pk, rpk = psP.next()
            for k in range(2):
                sc.op("pe", lambda e: e.matmul(pk[:, :], lhsT=wkv[:, k, h, :], rhs=ckvn[:, k, gc], start=(k == 0), stop=(k == 1)), reads=[r_wkv, r_ckvn[g]], writes=[rpk], inc=(k == 1))
            sc.op("dve", lambda e: e.tensor_copy(out=kT[0:64, gc], in_=pk[0:64, :]), reads=[rpk], writes=[rk[g]])
            sc.op("pool", lambda e: e.tensor_copy(out=kT[P, gc], in_=krot[P, gc]), reads=[r_krot[g]], writes=[rk[g]])
        return (qT, rq, kT, rk)

    fin_pending = []

    def attn_head(h, qT, rq, kT, rk):
        for gq_ in range(NG):
            po, rpo = psO.next()
            nj = 4 * gq_ + 4
            pend = []

            def pv_mm(item, po=po, rpo=rpo, nj=nj):
                j, pT, rpT, c0, ncols = item
                sc.op("pe", lambda e: e.matmul(po[:, c0:TG], lhsT=vaug[:, j, h, :], rhs=pT[:, 0:ncols], start=(j == 0), stop=(j == nj - 1), skip_group_check=True),
                      reads=[r_vaug[j], r_vones, rpT], writes=[rpo], inc=True)

            for j in range(nj):
                c0 = 0 if j < 4 * gq_ else (j - 4 * gq_) * 128
                ncols = TG - c0
                ps_, rps_ = psS.next()
                q0 = gq_ * TG + c0
                diag = j >= 4 * gq_
                sc.op("pe", lambda e: e.matmul(ps_[:, 0:ncols], lhsT=kT[:, j * 128:(j + 1) * 128], rhs=qT[:, q0:q0 + ncols], start=True, stop=not diag, skip_group_check=True),
                      reads=[rk[j // 4], rq[gq_]], writes=[rps_], inc=not diag)
                if diag:
                    sc.op("pe", lambda e: e.matmul(ps_[:, 0:128], lhsT=ident_b[:], rhs=negm[:], start=False, stop=True, skip_group_check=True),
                          reads=[r_const, r_negm], writes=[rps_], inc=True)
                pT, rpT = pT_ring.next()
                sc.op("act", lambda e: e.activation(out=pT[:, 0:ncols], in_=ps_[:, 0:ncols], func=AF.Exp, scale=MLA_SCALE), reads=[rps_], writes=[rpT])
                pend.append((j, pT, rpT, c0, ncols))
                if len(pend) > 3:
                    pv_mm(pend.pop(0))
            while pend:
                pv_mm(pend.pop(0))
            rd, rrd = rd_ring.next()
            sc.op("dve", lambda e: e.reciprocal(out=rd[0:64, :], in_=po[64:128, :]), reads=[rpo], writes=[rrd])
            ao, rao = ao_ring.next()
            sc.op("dve", lambda e: e.tensor_tensor(out=ao[:, :], in0=po[0:64, :], in1=rd[0:64, :], op=ALU.mult), reads=[rpo, rrd], writes=[rao])
            sc.dma("sp", mixin[h * 64:(h + 1) * 64, gsl(gq_)], ao[:, :], reads=[rao], writes=[r_mix[gq_]])

    cur = proj_head(0)
    for h in range(8):
        nxt = proj_head(h + 1) if h + 1 < 8 else None
        attn_head(h, *cur)
        cur = nxt
    while fin_pending:
        fin_pending.pop(0)()
    sc.barrier()
    es.close()
    es01.close()
    if stop_after <= 1:
        return _finish(nc, sc, top)

    def outproj_ln_phase(es, wo, r_wo, src_rhs_loader, resid_src_v, r_resid_src, dst_v, r_dst, layer):
        pass

    es_sh = ExitStack()
    wsh = sb(es_sh, "wsh", [128, 8, 4096], BF16)
    r_wsh = [Res(f"wsh{i}") for i in range(4)]
    es_w0 = ExitStack()
    es = ExitStack()
    wo_holder = []

    def _p2a_first():
        wo_ = sb(es, "wo", [128, 8, D], BF16)
        r_wo_ = [Res("wo0"), Res("wo1")]
        load_w(wo_, w_out_e.rearrange("(k p) n -> p k n", p=128), r_wo_, 2, D)
        wo_holder.append((wo_, r_wo_))

    _p2a_first()
    wo, r_wo = wo_holder[0]
    w1_v0 = w_ff1[0].rearrange("(k p) n -> p k n", p=128)
    for i in range(4):
        sc.dma("pool", wsh[:, :, i * 1024:(i + 1) * 1024], w1_v0[:, :, i * 1024:(i + 1) * 1024], writes=[r_wsh[i]])
    pre0 = (wsh, None, r_wsh, None)
    mi_ring = Ring([(sb(es, f"mi{i}", [128, 8, TG], BF16), Res(f"mi{i}")) for i in range(4)])
    xr_ring = Ring([(sb(es, f"xr{i}", [128, 8, TG], F32), Res(f"xr{i}")) for i in range(4)])
    tms = [ln_tmps(es), ln_tmps(es)]
    psA = psring([0, 1, 2, 3])
    psLs = [psring([4, 5]), psring([6, 7])]
    hA_v = fm(hA)
    def run_rr(gens):
        gens = list(gens)
        while gens:
            for gen in list(gens):
                try:
                    next(gen)
                except StopIteration:
                    gens.remove(gen)

    p2a_bufs = {}
    p2a_in = {}

    def p2a_load(g):
        if g >= NG or g in p2a_in:
            return
        mi, rmi = mi_ring.next()
        xr, rxr = xr_ring.next()
        p2a_in[g] = (mi, rmi, xr, rxr)
        sc.dma("sp", mi[:], mix_v[:, :, gsl(g)], reads=[r_mix[g]], writes=[rmi])
        sc.dma("sp", xr[:], xT_v[:, :, gsl(g)], writes=[rxr])

    def p2a_main(g):
        gc = gsl(g)
        p2a_load(g)
        mi, rmi, xr, rxr = p2a_in[g]
        p2a_bufs[g] = (xr, rxr)
        for oc in range(8):
            pt, rpt = psA.next()
            for k in range(8):
                sc.op("pe", lambda e: e.matmul(pt[:, :], lhsT=wo[:, k, oc * 128:(oc + 1) * 128], rhs=mi[:, k, :], start=(k == 0), stop=(k == 7)),
                      reads=[r_wo[oc // 4], rmi], writes=[rpt], inc=(k == 7))
            sc.op("dve", lambda e: e.scalar_tensor_tensor(out=xr[:, oc, :], in0=xr[:, oc, :], scalar=ALPHA, in1=pt[:, :], op0=ALU.mult, op1=ALU.add),
                  reads=[rxr, rpt], writes=[rxr])
            yield
            yield
            yield

    def p2a_ln(g, sl):
        xr, rxr = p2a_bufs[g]
        yield from layer_norm_gen(es, xr, rxr, 0, 1, psLs[sl], tms[sl])
        sc.dma("pool", hA_v[:, :, gsl(g)], xr[:], reads=[rxr], writes=[r_hA[g]])
        p2a_load(g + 4)
        yield

    for g in range(min(4, NG)):
        p2a_load(g)
    ln_gens = {}

    def step_lns():
        for sl_ in list(ln_gens):
            try:
                next(ln_gens[sl_])
            except StopIteration:
                del ln_gens[sl_]

    for g in range(NG):
        m = p2a_main(g)
        while True:
            try:
                next(m)
            except StopIteration:
                break
            step_lns()
        sl = g % 2
        while sl in ln_gens:
            step_lns()
        ln_gens[sl] = p2a_ln(g, sl)
    while ln_gens:
        step_lns()
    sc.barrier()
    es.close()
    if stop_after <= 2:
        es_w0.close()
        es_sh.close()
        return _finish(nc, sc, top)

    def ffn_weights(es, layer, first=None):
        w1 = sb(es, "w1", [128, 8, DFF], BF16)
        w2 = sb(es, "w2", [128, 32, D], BF16)
        r_w1 = [Res(f"w1_{i}") for i in range(4)]
        r_w2 = [Res(f"w2_{i}") for i in range(4)]
        w1_v = w_ff1[layer].rearrange("(k p) n -> p k n", p=128)
        w2_v = w_ff2[layer].rearrange("(k p) n -> p k n", p=128)
        sc.dma("pool", w1[:, :, 0:1024], w1_v[:, :, 0:1024], writes=[r_w1[0]])
        if first is not None:
            first()
        for i in range(1, 4):
            sc.dma("pool", w1[:, :, i * 1024:(i + 1) * 1024], w1_v[:, :, i * 1024:(i + 1) * 1024], writes=[r_w1[i]])
        for i in range(4):
            sc.dma("pool", w2[:, i * 8:(i + 1) * 8, :], w2_v[:, i * 8:(i + 1) * 8, :], writes=[r_w2[i]])
        return (w1, w2, r_w1, r_w2)

    def ffn_phase(layer, src_v, r_src, dst_v, r_dst, pre=None, after_last_ffn1=None):
        es = ExitStack()
        hb = hr = None
        rhb = Res("hb")
        rhr = Res("hr")

        def alloc_io():
            nonlocal hb, hr
            hb = sb(es, "hb", [128, 8, TG], BF16)
            hr = sb(es, "hr", [128, 8, TG], F32)

        def load_first():
            sc.dma("pool", hb[:], src_v[:, :, gsl(0)], reads=[r_src[0]], writes=[rhb])
            sc.dma("sp", hr[:], src_v[:, :, gsl(0)], reads=[r_src[0]], writes=[rhr])

        if pre is None:
            w1 = sb(es, "w1", [128, 8, DFF], BF16)
            w2 = sb(es, "w2", [128, 32, D], BF16)
            alloc_io()
            r_w1 = [Res(f"w1_{i}") for i in range(4)]
            r_w2 = [Res(f"w2_{i}") for i in range(4)]
            w1_v = w_ff1[layer].rearrange("(k p) n -> p k n", p=128)
            w2_v = w_ff2[layer].rearrange("(k p) n -> p k n", p=128)
            sc.dma("pool", w1[:, :, 0:1024], w1_v[:, :, 0:1024], writes=[r_w1[0]])
            load_first()
            for i in range(1, 4):
                sc.dma("pool", w1[:, :, i * 1024:(i + 1) * 1024], w1_v[:, :, i * 1024:(i + 1) * 1024], writes=[r_w1[i]])
            for i in range(4):
                sc.dma("pool", w2[:, i * 8:(i + 1) * 8, :], w2_v[:, i * 8:(i + 1) * 8, :], writes=[r_w2[i]])
        else:
            (w1, w2, r_w1, r_w2) = pre
            if w2 is None:
                w2 = sb(es, "w2", [128, 32, D], BF16)
                r_w2 = [Res(f"w2_{i}") for i in range(4)]
                alloc_io()
                load_first()
                w2_v = w_ff2[layer].rearrange("(k p) n -> p k n", p=128)
                for i in range(4):
                    sc.dma("pool", w2[:, i * 8:(i + 1) * 8, :], w2_v[:, i * 8:(i + 1) * 8, :], writes=[r_w2[i]])
            else:
                alloc_io()
                load_first()
        a = sb(es, "a_act", [128, 32, TG], BF16)
        ra = [Res(f"a{i}") for i in range(32)]
        sq_ring = Ring([(sb(es, f"fsq{i}", [128, TG], F32), Res(f"fsq{i}")) for i in range(2)])
        tm = ln_tmps(es)
        psA = psring([0, 1, 2, 3])
        psB = psring([4, 5])
        psL = psring([6, 7])
        pending_ln = [None]

        def step_ln(drain=False):
            while pending_ln[0] is not None:
                try:
                    next(pending_ln[0])
                except StopIteration:
                    pending_ln[0] = None
                if not drain:
                    break

        def ln_store_gen_ffn(g):
            yield from layer_norm_gen(es, hr, rhr, layer, 2, psL, tm)
            sc.dma("sp", dst_v[:, :, gsl(g)], hr[:], reads=[rhr], writes=[r_dst[g]])
            if g + 1 < NG:
                sc.dma("sp", hr[:], src_v[:, :, gsl(g + 1)], reads=[r_src[g + 1]], writes=[rhr])
            yield

        for g in range(NG):
            gc = gsl(g)
            for fc in range(32):
                pt, rpt = psA.next()
                for k in range(8):
                    sc.op("pe", lambda e: e.matmul(pt[:, :], lhsT=w1[:, k, fc * 128:(fc + 1) * 128], rhs=hb[:, k, :], start=(k == 0), stop=(k == 7)),
                          reads=[r_w1[fc // 8], rhb], writes=[rpt], inc=(k == 7))
                sq, rsq = sq_ring.next()
                sc.op("act", lambda e: e.activation(out=sq[:], in_=pt[:, :], func=AF.Square), reads=[rpt], writes=[rsq])
                sc.op("dve", lambda e: e.scalar_tensor_tensor(out=a[:, fc, :], in0=pt[:, :], scalar=0.0, in1=sq[:], op0=ALU.is_gt, op1=ALU.mult),
                      reads=[rpt, rsq], writes=[ra[fc]])
                if fc >= 1:
                    step_ln()
            step_ln(drain=True)
            if g + 1 < NG:
                sc.dma("pool", hb[:], src_v[:, :, gsl(g + 1)], reads=[r_src[g + 1]], writes=[rhb])
            elif after_last_ffn1 is not None:
                after_last_ffn1()
            for oc in range(8):
                pt, rpt = psB.next()
                for fc in range(32):
                    sc.op("pe", lambda e: e.matmul(pt[:, :], lhsT=w2[:, fc, oc * 128:(oc + 1) * 128], rhs=a[:, fc, :], start=(fc == 0), stop=(fc == 31)),
                          reads=[r_w2[fc // 8], ra[fc]], writes=[rpt], inc=(fc == 31))
                sc.op("dve", lambda e: e.scalar_tensor_tensor(out=hr[:, oc, :], in0=hr[:, oc, :], scalar=ALPHA, in1=pt[:, :], op0=ALU.mult, op1=ALU.add),
                      reads=[rhr, rpt], writes=[rhr])
            pending_ln[0] = ln_store_gen_ffn(g)
        step_ln(drain=True)
        sc.barrier()
        es.close()

    hB_v = fm(hB)
    wi_v = w_in_o.rearrange("(k p) n -> p k n", p=128)

    def _load_wi():
        for i in (0, 2, 3, 1):
            sc.dma("pool", wsh[:, :, i * 1024:(i + 1) * 1024], wi_v[:, :, i * 1024:(i + 1) * 1024], writes=[r_wsh[i]])

    ffn_phase(0, hA_v, r_hA, hB_v, r_hB, pre=pre0, after_last_ffn1=_load_wi)
    es_w0.close()
    if stop_after <= 3:
        es_sh.close()
        return _finish(nc, sc, top)

    es = ExitStack()
    wi = wsh
    r_wi = r_wsh
    hb_ring = Ring([(sb(es, f"hb{i}", [128, 8, TG], BF16), Res(f"hb{i}")) for i in range(2)])
    hr = sb(es, "hr3", [128, 8, TG], F32)
    rhr = Res("hr3")
    hbs = {}
    hbs[0] = hb_ring.next()
    sc.dma("pool", hbs[0][0][:], hB_v[:, :, gsl(0)], reads=[r_hB[0]], writes=[hbs[0][1]])
    sc.dma("sp", hr[:], hB_v[:, :, gsl(0)], reads=[r_hB[0]], writes=[rhr])
    wo2 = sb(es, "wo2", [128, 8, D], BF16)
    r_wo2 = [Res("wo2_0"), Res("wo2_1")]
    load_w(wo2, w_out_o.rearrange("(k p) n -> p k n", p=128), r_wo2, 2, D)
    lbt = sb(es, "lbt", [128, 2, 8], F32)
    gn = sb(es, "gn", [128, 8], F32)
    oml = sb(es, "oml", [128, 8], F32)
    noml = sb(es, "noml", [128, 8], F32)
    r_hc = Res("hgc")
    sc.dma("sp", lbt[:], lb_in, writes=[r_hc])
    sc.dma("sp", gn[:], gn_in, writes=[r_hc])
    sc.op("dve", lambda e: e.tensor_tensor(out=oml[:], in0=lbt[:, 1, :], in1=lbt[:, 0, :], op=ALU.subtract), reads=[r_hc], writes=[r_hc])
    sc.op("act", lambda e: e.activation(out=oml[:], in_=oml[:], func=AF.Exp), reads=[r_hc], writes=[r_hc])
    sc.op("act", lambda e: e.activation(out=oml[:], in_=oml[:], func=AF.Ln, bias=1.0), reads=[r_hc], writes=[r_hc])
    sc.op("act", lambda e: e.activation(out=oml[:], in_=oml[:], func=AF.Exp, scale=-1.0), reads=[r_hc], writes=[r_hc])
    sc.op("dve", lambda e: e.tensor_scalar(out=noml[:], in0=oml[:], scalar1=-1.0, scalar2=None, op0=ALU.mult), reads=[r_hc], writes=[r_hc])
    st = sb(es, "hst", [128, 8, 128], F32)
    stb4 = sb(es, "hstb", [128, 8, 4, 128], BF16)
    r_st = [Res(f"st{h}") for h in range(8)]
    r_stb = [[Res(f"stb{h}_{i}") for i in range(4)] for h in range(8)]
    for h in range(8):
        sc.op("pool", lambda e: e.memset(st[:, h, :], 0.0), writes=[r_st[h]])
        sc.op("pool", lambda e: e.memset(stb4[:, h, 0, :], 0.0), writes=[r_stb[h][0]])
    ones_t = ones_f
    mask4 = sb(es, "mask4", [128, 4, 128], BF16)
    for i in range(4):
        sc.op("pool", lambda e: e.tensor_copy(out=mask4[:, i, :], in_=mask_b[:]), reads=[r_const], writes=[r_hc])
    NWAY = 3
    cbuf = []
    for i in range(NWAY):
        d = {}
        for nm in ("s1", "lg", "bc"):
            d[nm] = (sb(es, f"c{i}_{nm}", [128, TG], F32), Res(f"c{i}_{nm}"))
        d["eb"] = d["lg"]
        for nm in ("kt", "qt", "kd", "at4"):
            d[nm] = (sb(es, f"c{i}_{nm}", [128, TG], BF16), Res(f"c{i}_{nm}"))
        d["ktok4"] = d["kd"]
        d["sqo"] = d["at4"]
        d["banks"] = ((psum[2 * i], rps[2 * i]), (psum[2 * i + 1], rps[2 * i + 1]))
        cbuf.append(d)
    q_free = [True] * 8
    g_free = [True] * 8
    vtok_bufs = [(sb(es, f"vtok{i}", [128, 4, D], BF16), [Res(f"vtok{i}_{t}") for t in range(4)]) for i in range(2)]
    siluq = sb(es, "siluq", [128, 8, TG], BF16)
    r_sq = [Res(f"siluq{h}") for h in range(8)]
    sg = sb(es, "sg", [128, 8, TG], BF16)
    r_sg = [Res(f"sg{h}") for h in range(8)]
    onb = sb(es, "onb", [128, 8, TG], BF16)
    r_onb = [Res(f"onb{h}") for h in range(8)]
    tm = ln_tmps(es)
    psA = psring([6, 7])
    hC_v = fm(hC)

    def load_hb(g):
        if g not in hbs:
            hbs[g] = hb_ring.next()
            sc.dma("pool", hbs[g][0][:], hB_v[:, :, gsl(g)], reads=[r_hB[g]], writes=[hbs[g][1]])

    def session(g, col0, dst, rdst):
        hb, rhb = hbs[g]
        for h in range(8):
            pp, rpp = psA.next()
            c0 = col0 + h * 128
            for k in range(8):
                sc.op("pe", lambda e: e.matmul(pp[:, :], lhsT=wi[:, k, c0:c0 + 128], rhs=hb[:, k, :], start=(k == 0), stop=(k == 7)),
                      reads=[r_wi[col0 // 1024], rhb], writes=[rpp], inc=(k == 7))
            sc.op("act", lambda e: e.activation(out=dst[:, h, :], in_=pp[:, :], func=AF.Silu), reads=[rpp], writes=[rdst[h]])

    def vproj_gen(g):
        hb, rhb = hbs[g]
        vtok, rvt = vtok_bufs[g % 2]
        for tt in range(4):
            for half in range(2):
                pv, rpv = psA.next()
                c0 = 2048 + half * 512
                for k in range(8):
                    sc.op("pe", lambda e: e.matmul(pv[:, :], lhsT=hb[:, k, tt * 128:(tt + 1) * 128], rhs=wi[:, k, c0:c0 + 512], start=(k == 0), stop=(k == 7)),
                          reads=[r_wi[2], rhb], writes=[rpv], inc=(k == 7))
                if half == 0:
                    sc.op("act", lambda e: e.copy(out=vtok[:, tt, 0:512], in_=pv[:, :]), reads=[rpv], writes=[rvt[tt]])
                else:
                    sc.op("dve", lambda e: e.tensor_copy(out=vtok[:, tt, 512:1024], in_=pv[:, :]), reads=[rpv], writes=[rvt[tt]])
                yield

    def vproj(g):
        for _ in vproj_gen(g):
            pass

    def chain(g, h, slot):
        B = cbuf[slot]
        hb, rhb = hbs[g]
        vtok, rvt = vtok_bufs[g % 2]
        (b0, rb0), (b1, rb1) = B["banks"]
        (s1, rs1), (lg, rlg), (bc, rbc), (eb, reb) = B["s1"], B["lg"], B["bc"], B["eb"]
        (kt, rkt), (qt, rqt), (kd, rkd) = B["kt"], B["qt"], B["kd"]
        (ktok4, rk4), (at4, rat), (sqo, rsqo) = B["ktok4"], B["at4"], B["sqo"]
        hs = slice(h * 128, (h + 1) * 128)
        TS = [slice(tt * 128, (tt + 1) * 128) for tt in range(4)]
        for k in range(8):
            sc.op("pe", lambda e: e.matmul(b0[:, :], lhsT=wi[:, k, 1024 + h * 128:1024 + (h + 1) * 128], rhs=hb[:, k, :], start=(k == 0), stop=(k == 7)),
                  reads=[r_wi[1], rhb], writes=[rb0], inc=(k == 7))
        yield
        sc.op("act", lambda e: e.activation(out=s1[:], in_=b0[:, :], func=AF.Exp), reads=[rb0], writes=[rs1])
        yield
        sc.op("act", lambda e: e.activation(out=s1[:], in_=s1[:], func=AF.Ln, bias=1.0), reads=[rs1], writes=[rs1])
        yield
        sc.op("act", lambda e: e.activation(out=s1[:], in_=s1[:], func=AF.Exp, scale=-1.0), reads=[rs1], writes=[rs1])
        yield
        sc.op("act", lambda e: e.activation(out=lg[:], in_=s1[:], func=AF.Ln, scale=noml[:, h:h + 1], bias=1.0), reads=[rs1, r_hc], writes=[rlg])
        yield
        for tt in range(4):
            sc.op("dve", lambda e: e.tensor_tensor_scan(out=bc[:, TS[tt]], data0=ones_t[:], data1=lg[:, TS[tt]], initial=0.0, op0=ALU.mult, op1=ALU.add),
                  reads=[rlg, r_hc], writes=[rbc])
        yield
        sc.op("act", lambda e: e.activation(out=eb[:], in_=bc[:], func=AF.Exp), reads=[rbc], writes=[reb])
        yield
        sc.op("act", lambda e: e.activation(out=bc[:], in_=bc[:], func=AF.Exp, scale=-1.0), reads=[rbc], writes=[rbc])
        yield
        sc.op("dve", lambda e: e.scalar_tensor_tensor(out=kt[:], in0=s1[:], scalar=oml[:, h:h + 1], in1=bc[:], op0=ALU.mult, op1=ALU.mult), reads=[rs1, rbc, r_hc], writes=[rkt])
        yield
        sc.op("dve", lambda e: e.tensor_tensor(out=qt[:], in0=siluq[:, h, :], in1=eb[:], op=ALU.mult), reads=[r_sq[h], reb], writes=[rqt])
        q_free[h] = True
        yield
        eb_last = bass.AP(eb[:].tensor, 127, [[TG, 128], [128, 4], [0, 128]])
        sc.op("dve", lambda e: e.tensor_tensor(out=kd[:].rearrange("p (a b) -> p a b", a=4), in0=kt[:].rearrange("p (a b) -> p a b", a=4), in1=eb_last, op=ALU.mult),
              reads=[rkt, reb], writes=[rkd])
        yield
        for tt in range(4):
            sc.op("pe", lambda e: e.matmul(b0[:, TS[tt]], lhsT=kd[:, TS[tt]], rhs=ident_b[:], start=True, stop=True), reads=[rkd, r_const], writes=[rb0], inc=(tt == 3))
        yield
        sc.op("act", lambda e: e.copy(out=ktok4[:], in_=b0[:, :]), reads=[rb0], writes=[rk4])
        yield
        for tt in range(4):
            sc.op("pe", lambda e: e.matmul(b1[:, TS[tt]], lhsT=kt[:, TS[tt]], rhs=qt[:, TS[tt]], start=True, stop=True), reads=[rkt, rqt], writes=[rb1], inc=(tt == 3))
        yield
        sc.op("dve", lambda e: e.tensor_tensor(out=at4[:], in0=b1[:, :], in1=mask4[:].rearrange("p a b -> p (a b)"), op=ALU.mult), reads=[rb1, r_hc], writes=[rat])
        yield
        for tt in range(4):
            sc.op("pe", lambda e: e.matmul(b0[:, TS[tt]], lhsT=ktok4[:, TS[tt]], rhs=vtok[:, tt, hs], start=True, stop=True), reads=[rk4, rvt[tt]], writes=[rb0], inc=(tt == 3))
        yield
        for tt in range(4):
            last = tt * 128 + 127
            sc.op("dve", lambda e: e.scalar_tensor_tensor(out=st[:, h, :], in0=st[:, h, :], scalar=eb[:, last:last + 1], in1=b0[:, TS[tt]], op0=ALU.mult, op1=ALU.add),
                  reads=[r_st[h], reb, rb0], writes=[r_st[h]])
            yield
            if tt < 3:
                sc.op("dve", lambda e: e.tensor_copy(out=stb4[:, h, tt + 1, :], in_=st[:, h, :]), reads=[r_st[h]], writes=[r_stb[h][tt + 1]])
                yield
        for tt in range(4):
            sc.op("pe", lambda e: e.matmul(b1[:, TS[tt]], lhsT=vtok[:, tt, hs], rhs=at4[:, TS[tt]], start=True, stop=False), reads=[rvt[tt], rat], writes=[rb1], inc=False)
            sc.op("pe", lambda e: e.matmul(b1[:, TS[tt]], lhsT=stb4[:, h, tt, :], rhs=qt[:, TS[tt]], start=False, stop=True), reads=[r_stb[h][tt], rqt], writes=[rb1], inc=True)
        yield
        sc.op("dve", lambda e: e.tensor_copy(out=stb4[:, h, 0, :], in_=st[:, h, :]), reads=[r_st[h]], writes=[r_stb[h][0]])
        sc.op("act", lambda e: e.activation(out=sqo[:], in_=b1[:, :], func=AF.Square), reads=[rb1], writes=[rsqo])
        yield
        sc.op("pe", lambda e: e.matmul(b0[:, :], lhsT=ones_bf[:], rhs=sqo[:], start=True, stop=True), reads=[rsqo, r_const], writes=[rb0], inc=True)
        yield
        rn, rrn = s1, rs1
        sc.op("dve", lambda e: e.tensor_scalar(out=rn[:], in0=b0[:, :], scalar1=1.0 / 128, scalar2=EPS, op0=ALU.mult, op1=ALU.add), reads=[rb0], writes=[rrn])
        yield
        sc.op("act", lambda e: e.activation(out=rn[:], in_=rn[:], func=AF.Ln), reads=[rrn], writes=[rrn])
        yield
        sc.op("act", lambda e: e.activation(out=rn[:], in_=rn[:], func=AF.Exp, scale=-0.5), reads=[rrn], writes=[rrn])
        yield
        sc.op("dve", lambda e: e.scalar_tensor_tensor(out=rn[:], in0=b1[:, :], scalar=gn[:, h:h + 1], in1=rn[:], op0=ALU.mult, op1=ALU.mult), reads=[rb1, rrn, r_hc], writes=[rrn])
        yield
        sc.op("dve", lambda e: e.tensor_tensor(out=onb[:, h, :], in0=rn[:], in1=sg[:, h, :], op=ALU.mult), reads=[rrn, r_sg[h]], writes=[r_onb[h]])
        g_free[h] = True
        yield

    def session_item(g, col0, dst, rdst, h):
        hb, rhb = hbs[g]
        pp, rpp = psA.next()
        c0 = col0 + h * 128
        for k in range(8):
            sc.op("pe", lambda e: e.matmul(pp[:, :], lhsT=wi[:, k, c0:c0 + 128], rhs=hb[:, k, :], start=(k == 0), stop=(k == 7)),
                  reads=[r_wi[col0 // 1024], rhb], writes=[rpp], inc=(k == 7))
        sc.op("act", lambda e: e.activation(out=dst[:, h, :], in_=pp[:, :], func=AF.Silu), reads=[rpp], writes=[rdst[h]])

    def outproj_gen(g):
        for oc in range(8):
            pt, rpt = psA.next()
            for h in range(8):
                sc.op("pe", lambda e: e.matmul(pt[:, :], lhsT=wo2[:, h, oc * 128:(oc + 1) * 128], rhs=onb[:, h, :], start=(h == 0), stop=(h == 7)),
                      reads=[r_wo2[oc // 4], r_onb[h]], writes=[rpt], inc=(h == 7))
            sc.op("dve", lambda e: e.scalar_tensor_tensor(out=hr[:, oc, :], in0=hr[:, oc, :], scalar=ALPHA, in1=pt[:, :], op0=ALU.mult, op1=ALU.add),
                  reads=[rhr, rpt], writes=[rhr])
            yield

    def ln_store_gen(g):
        yield from layer_norm_gen(es, hr, rhr, 1, 1, psA, tm)
        sc.dma("sp", hC_v[:, :, gsl(g)], hr[:], reads=[rhr], writes=[r_hC[g]])
        if g + 1 < NG:
            sc.dma("sp", hr[:], hB_v[:, :, gsl(g + 1)], reads=[r_hB[g + 1]], writes=[rhr])
        yield

    state = {"chains_done": False}

    def bulk_gen(g):
        if g > 0:
            yield from outproj_gen(g - 1)
            yield from ln_store_gen(g - 1)
        if g + 1 < NG:
            yield from vproj_gen(g + 1)
            pending = [("q", h) for h in range(8)] + [("g", h) for h in range(8)]
            while pending:
                elig = [it for it in pending if (q_free[it[1]] if it[0] == "q" else g_free[it[1]])]
                if len(elig) >= 4 or (state["chains_done"] and elig):
                    for it in elig[:4]:
                        if it[0] == "q":
                            session_item(g + 1, 0, siluq, r_sq, it[1])
                        else:
                            session_item(g + 1, 3072, sg, r_sg, it[1])
                        pending.remove(it)
                        yield
                else:
                    yield

    for h in range(8):
        session_item(0, 0, siluq, r_sq, h)
    vproj(0)
    for h in range(8):
        session_item(0, 3072, sg, r_sg, h)
    for g in range(NG):
        if g + 1 < NG:
            load_hb(g + 1)
        for h in range(8):
            q_free[h] = False
            g_free[h] = False
        state["chains_done"] = False
        heads = list(range(8))
        slots = [None] * NWAY
        bulk = bulk_gen(g)
        bulk_alive = True
        rnd = 0
        while True:
            active = False
            for i in range(NWAY):
                if slots[i] is None and heads:
                    slots[i] = chain(g, heads.pop(0), i)
                if slots[i] is not None:
                    active = True
                    try:
                        next(slots[i])
                    except StopIteration:
                        slots[i] = None
            if not active and not heads:
                state["chains_done"] = True
            rnd += 1
            if bulk_alive and (rnd % 3 != 2 or state["chains_done"]):
                try:
                    next(bulk)
                except StopIteration:
                    bulk_alive = False
            if state["chains_done"] and not bulk_alive:
                break
    w1_v1 = w_ff1[1].rearrange("(k p) n -> p k n", p=128)
    for i in range(4):
        sc.dma("pool", wsh[:, :, i * 1024:(i + 1) * 1024], w1_v1[:, :, i * 1024:(i + 1) * 1024], writes=[r_wsh[i]])
    for _ in outproj_gen(NG - 1):
        pass
    for _ in ln_store_gen(NG - 1):
        pass
    sc.barrier()
    es.close()
    if stop_after <= 4:
        es_sh.close()
        return _finish(nc, sc, top)

    out_v = fm(out)
    r_out = [Res(f"out{g}") for g in range(NG)]
    ffn_phase(1, hC_v, r_hC, out_v, r_out, pre=(wsh, None, r_wsh, None))
    es_sh.close()
    return _finish(nc, sc, top)


def _finish(nc, sc, top):
    sc.finish()
    top.close()
    return nc


def _prep_shared(inputs):
    f = np.float32
    half = 16
    inv_freq = (10000.0 ** (-np.arange(half, dtype=np.float32) / half)).astype(f)
    invf = np.zeros((128, 1), f)
    invf[64:80, 0] = inv_freq
    invf[80:96, 0] = inv_freq
    pidx = np.arange(128)
    mask = (pidx[None, :] >= pidx[:, None]).astype(f)
    lnp = np.stack([inputs["ln1_g"], inputs["ln1_b"], inputs["ln2_g"], inputs["ln2_b"]], 0)
    lnp = np.ascontiguousarray(lnp.reshape(4, 2, 8, 128).transpose(3, 0, 1, 2)).astype(f)
    sh = {
        "w_in_e": np.ascontiguousarray(inputs["w_in_e"][0], f),
        "w_qb": np.ascontiguousarray(inputs["w_qb"][0], f),
        "w_kvb": np.ascontiguousarray(inputs["w_kvb"][0], f),
        "w_out_e": np.ascontiguousarray(inputs["w_out_e"][0], f),
        "sgu_wT": np.ascontiguousarray(np.transpose(inputs["sgu_w"][0], (2, 0, 1)), f),
        "sgu_b": np.ascontiguousarray(inputs["sgu_b"][0].reshape(1, 512), f),
        "sgu_g": np.ascontiguousarray(inputs["sgu_ln_g"][0].reshape(1, 512), f),
        "sgu_bb": np.ascontiguousarray(inputs["sgu_ln_b"][0].reshape(1, 512), f),
        "gq": np.ascontiguousarray(inputs["mla_gq"][0].reshape(2, 128).T, f),
        "gkv": np.ascontiguousarray(inputs["mla_gkv"][0].reshape(2, 128).T, f),
        "w_in_o": np.ascontiguousarray(inputs["w_in_o"][0], f),
        "w_out_o": np.ascontiguousarray(inputs["w_out_o"][0], f),
        "hg_lb": np.ascontiguousarray(inputs["hg_lb"].reshape(2, 8, 128).transpose(2, 0, 1), f),
        "hg_gn": np.ascontiguousarray(inputs["hg_gnorm"][0].reshape(8, 128).T, f),
        "lnp": lnp,
        "w_ff1": np.ascontiguousarray(inputs["w_ff1"], f),
        "w_ff2": np.ascontiguousarray(inputs["w_ff2"], f),
        "invf": invf,
        "mask_ge": mask,
        "ident": np.eye(128, dtype=f),
    }
    return sh


_NC_CACHE = {}


def kernel(**inputs):
    inputs = {k: np.asarray(v) for k, v in inputs.items()}
    x = inputs["x"]
    B, S, _ = x.shape
    if S not in _NC_CACHE:
        _NC_CACHE[S] = build(S)
    nc = _NC_CACHE[S]
    sh = _prep_shared(inputs)
    in_maps = []
    for b in range(B):
        m = dict(sh)
        m["xT"] = np.ascontiguousarray(x[b].T)
        m["pos"] = np.ascontiguousarray(inputs["positions"][b].reshape(1, S).astype(np.int32))
        in_maps.append(m)
    res = run_bass_kernel_spmd(nc, in_maps, core_ids=list(range(B)))
    outs = [np.asarray(r["out"]).T for r in res.results]
    return np.ascontiguousarray(np.stack(outs, 0).astype(np.float32))
```

```python
import bisect
import math
from contextlib import ExitStack

import numpy as np
import ml_dtypes
import concourse.bass as bass
import concourse.mybir as mybir
from concourse.bass_utils import run_bass_kernel_spmd

F32 = mybir.dt.float32
BF16 = mybir.dt.bfloat16
I32 = mybir.dt.int32
AF = mybir.ActivationFunctionType
ALU = mybir.AluOpType

D = 1024
DFF = 4096
DEPTH = 2
ALPHA = float((2 * DEPTH) ** 0.25)
EPS = 1e-5
MLA_SCALE = float(96 ** -0.5)
EVEN_IN = 1568
TG = 512
PI = math.pi
import os
VAR = int(os.environ.get('VAR', '0'))


class Tok:
    __slots__ = ("sem", "val", "step", "hv", "hc", "name")

    def __init__(self, nc, name, step):
        self.sem = nc.alloc_semaphore(name)
        self.val = 0
        self.step = step
        self.hv = []
        self.hc = []
        self.name = name


class Res:
    __slots__ = ("w", "r", "name", "excl")

    def __init__(self, name="", excl=False):
        self.w = None
        self.r = {}
        self.name = name
        self.excl = excl


class Eng:
    def __init__(self, nc, e, name, raw):
        self.e = e
        self.name = name
        self.raw = raw
        self.tok = Tok(nc, "t_" + name, 1)
        self.clock = {}
        self.dirty = False
        self.ring = []
        self.ri = 0


class Sched:
    def __init__(self, nc, nring=10):
        self.nc = nc
        self.engs = {
            "pe": Eng(nc, nc.tensor, "pe", False),
            "act": Eng(nc, nc.scalar, "act", True),
            "dve": Eng(nc, nc.vector, "dve", True),
            "pool": Eng(nc, nc.gpsimd, "pool", True),
            "sp": Eng(nc, nc.sync, "sp", False),
        }
        self.dtoks = []
        for q in ("sp", "pool"):
            for i in range(nring):
                t = Tok(nc, f"d_{q}{i}", 16)
                self.engs[q].ring.append(t)
                self.dtoks.append(t)
        self.nins = 0
        self.nwait = 0

    def _deps(self, E, reads, writes):
        deps = {}
        for r in reads:
            if r.w is not None:
                t, v = r.w
                if v > deps.get(t, 0):
                    deps[t] = v
            if r.excl:
                for t, v in r.r.items():
                    if t is not E.tok and v > deps.get(t, 0):
                        deps[t] = v
        for w in writes:
            if w.w is not None:
                t, v = w.w
                if v > deps.get(t, 0):
                    deps[t] = v
            for t, v in w.r.items():
                if v > deps.get(t, 0):
                    deps[t] = v
        waits = []
        own = None
        for t, v in deps.items():
            if t is E.tok:
                if E.raw and v > E.clock.get(t, 0):
                    own = (t, v)
                continue
            if E.clock.get(t, 0) >= v:
                continue
            waits.append((t, v))
        if own is not None:
            waits.append(own)
        return waits

    def _note(self, E, t, v):
        if E.clock.get(t, 0) < v:
            E.clock[t] = v
        i = bisect.bisect_right(t.hv, v) - 1
        if i >= 0:
            for t2, v2 in t.hc[i].items():
                if E.clock.get(t2, 0) < v2:
                    E.clock[t2] = v2
        E.dirty = True

    def _emit_waits(self, E, waits):
        last = None
        if waits:
            for t, v in waits[:-1]:
                E.e.wait_ge(t.sem, v)
                self.nwait += 1
            last = waits[-1]
            for t, v in waits:
                self._note(E, t, v)
        return last

    def op(self, eng, fn, reads=(), writes=(), inc=True):
        E = self.engs[eng]
        waits = self._deps(E, reads, writes)
        last = self._emit_waits(E, waits)
        ins = fn(E.e)
        self.nins += 1
        if last is not None:
            ins._wait_ge(last[0].sem, last[1])
        tok = E.tok
        if inc:
            ins.then_inc(tok.sem, 1)
            tok.val += 1
            cv = tok.val
        else:
            cv = tok.val + 1
        if E.dirty:
            tok.hv.append(cv)
            tok.hc.append(dict(E.clock))
            E.dirty = False
        for r in reads:
            if r.r.get(tok, 0) < cv:
                r.r[tok] = cv
        for w in writes:
            w.w = (tok, cv)
            w.r = {}
        return ins

    def dma(self, q, out, in_, reads=(), writes=()):
        E = self.engs[q]
        tok = E.ring[E.ri]
        E.ri = (E.ri + 1) % len(E.ring)
        waits = self._deps(E, reads, writes)
        if tok.val and E.clock.get(tok, 0) < tok.val:
            waits = [(t, v) for (t, v) in waits if t is not tok] + [(tok, tok.val)]
        last = self._emit_waits(E, waits)
        ins = E.e.dma_start(out=out, in_=in_)
        self.nins += 1
        if last is not None:
            ins._wait_ge(last[0].sem, last[1])
        ins.then_inc(tok.sem, 16)
        tok.val += 16
        cv = tok.val
        tok.hv.append(cv)
        tok.hc.append(dict(E.clock))
        for r in reads:
            if r.r.get(tok, 0) < cv:
                r.r[tok] = cv
        for w in writes:
            w.w = (tok, cv)
            w.r = {}
        return ins

    def barrier(self):
        toks = [E.tok for E in self.engs.values()] + self.dtoks
        for E in self.engs.values():
            for t in toks:
                if t is E.tok or t.val == 0:
                    continue
                if E.clock.get(t, 0) < t.val:
                    E.e.wait_ge(t.sem, t.val)
                    self._note(E, t, t.val)

    def finish(self):
        E = self.engs["sp"]
        for t in self.dtoks:
            if t.val and E.clock.get(t, 0) < t.val:
                E.e.wait_ge(t.sem, t.val)


class FreePool:
    def __init__(self, items):
        self.items = list(items)

    def avail(self, n=1):
        return len(self.items) >= n

    def get(self):
        return self.items.pop(0)

    def put(self, it):
        self.items.append(it)


class Ring:
    def __init__(self, items):
        self.items = items
        self.i = 0

    def next(self):
        it = self.items[self.i]
        self.i = (self.i + 1) % len(self.items)
        return it


def build(S_len=4096, stop_after=99, debug=False, cut=99):
    S = S_len
    NG = S // TG
    NT = S // 128
    nc = bass.Bass("TRN2", target_bir_lowering=False)
    sc = Sched(nc)

    def din(name, shape, dt=F32):
        return nc.dram_tensor(name, list(shape), dt, kind="ExternalInput").ap()

    xT = din("xT", [D, S])
    pos = din("pos", [1, S], I32)
    w_in_e = din("w_in_e", [D, EVEN_IN])
    w_qb = din("w_qb", [256, 768])
    w_kvb = din("w_kvb", [256, 1024])
    w_out_e = din("w_out_e", [D, D])
    sgu_wT = din("sgu_wT", [128, 4, 128])
    sgu_b = din("sgu_b", [1, 512])
    sgu_g = din("sgu_g", [1, 512])
    sgu_bb = din("sgu_bb", [1, 512])
    gq_in = din("gq", [128, 2])
    gkv_in = din("gkv", [128, 2])
    w_in_o = din("w_in_o", [D, 4096])
    w_out_o = din("w_out_o", [D, D])
    lb_in = din("hg_lb", [128, 2, 8])
    gn_in = din("hg_gn", [128, 8])
    lnp_in = din("lnp", [128, 4, 2, 8])
    w_ff1 = din("w_ff1", [2, D, DFF])
    w_ff2 = din("w_ff2", [2, DFF, D])
    invf_in = din("invf", [128, 1])
    mask_in = din("mask_ge", [128, 128])
    ident_in = din("ident", [128, 128])
    out = nc.dram_tensor("out", [D, S], F32, kind="ExternalOutput").ap()
    ikind = "ExternalOutput" if debug else "Internal"
    mixin = nc.dram_tensor("mixin", [D, S], BF16, kind=ikind).ap()
    hA = nc.dram_tensor("hA", [D, S], F32, kind=ikind).ap()
    hB = nc.dram_tensor("hB", [D, S], F32, kind=ikind).ap()
    hC = nc.dram_tensor("hC", [D, S], F32, kind=ikind).ap()
    r_mix = [Res(f"mix{g}") for g in range(NG)]
    r_hA = [Res(f"hA{g}") for g in range(NG)]
    r_hB = [Res(f"hB{g}") for g in range(NG)]
    r_hC = [Res(f"hC{g}") for g in range(NG)]

    def fm(ap):
        return ap.rearrange("(k p) s -> p k s", p=128)

    def gsl(g):
        return slice(g * TG, (g + 1) * TG)

    top = ExitStack()

    cnt = [0]

    def sb(es, name, shape, dt):
        cnt[0] += 1
        return es.enter_context(nc.sbuf_tensor(f"s{cnt[0]}_{name}", list(shape), dt))

    psum = [top.enter_context(nc.psum_tensor(f"ps{i}", [128, 512], F32)) for i in range(8)]
    rps = [Res(f"ps{i}", excl=True) for i in range(8)]

    def psring(idx):
        return Ring([(psum[i], rps[i]) for i in idx])

    ones_bf = sb(top, "ones_bf", [128, 128], BF16)
    ones_f = sb(top, "ones_f", [128, 128], F32)
    mask_f = sb(top, "mask_f", [128, 128], F32)
    mask_b = sb(top, "mask_b", [128, 128], BF16)
    ident_b = sb(top, "ident_b", [128, 128], BF16)
    lnp = sb(top, "lnp", [128, 4, 2, 8], F32)
    r_const = Res("const")
    sc.op("pool", lambda e: e.memset(ones_bf[:], 1.0), writes=[r_const])
    sc.op("pool", lambda e: e.memset(ones_f[:], 1.0), writes=[r_const])
    sc.dma("sp", mask_f[:], mask_in, writes=[r_const])
    sc.dma("pool", mask_b[:], mask_in, writes=[r_const])
    sc.dma("pool", ident_b[:], ident_in, writes=[r_const])
    sc.dma("sp", lnp[:], lnp_in, writes=[r_const])

    def layer_norm(es_tmp, buf, rbuf, layer, which, psr, tmps):
        for _ in layer_norm_gen(es_tmp, buf, rbuf, layer, which, psr, tmps):
            pass

    def layer_norm_gen(es_tmp, buf, rbuf, layer, which, psr, tmps):
        (rb_ring, rs_ring, mean, msq, rstd, mr) = tmps
        gi, bi = (0, 1) if which == 1 else (2, 3)
        p_sum, r_sum = psr.next()
        p_sq, r_sq = psr.next()
        def stats_mm(c, rb, rrb, rs, rrs):
            sc.op("pe", lambda e: e.matmul(p_sum[:, :], lhsT=ones_bf[:], rhs=rb[:], start=(c == 0), stop=(c == 7)),
                  reads=[rrb, r_const], writes=[r_sum], inc=True)
            sc.op("pe", lambda e: e.matmul(p_sq[:, :], lhsT=ones_bf[:], rhs=rs[:], start=(c == 0), stop=(c == 7)),
                  reads=[rrs, r_const], writes=[r_sq], inc=True)

        prev = None
        for c in range(8):
            if prev is not None:
                stats_mm(*prev)
            rb, rrb = rb_ring.next()
            rs, rrs = rs_ring.next()
            sc.op("act", lambda e: e.copy(out=rb[:], in_=buf[:, c, :]), reads=[rbuf], writes=[rrb])
            sc.op("act", lambda e: e.activation(out=rs[:], in_=buf[:, c, :], func=AF.Square), reads=[rbuf], writes=[rrs])
            prev = (c, rb, rrb, rs, rrs)
            yield
        stats_mm(*prev)
        yield
        (mean_t, r_mean), (msq_t, r_msq), (rstd_t, r_rstd), (mr_t, r_mr) = mean, msq, rstd, mr
        sc.op("act", lambda e: e.activation(out=mean_t[:], in_=p_sum[:, :], func=AF.Copy, scale=1.0 / D), reads=[r_sum], writes=[r_mean])
        sc.op("dve", lambda e: e.tensor_tensor(out=msq_t[:], in0=mean_t[:], in1=mean_t[:], op=ALU.mult), reads=[r_mean], writes=[r_msq])
        sc.op("dve", lambda e: e.scalar_tensor_tensor(out=msq_t[:], in0=p_sq[:, :], scalar=1.0 / D, in1=msq_t[:], op0=ALU.mult, op1=ALU.subtract),
              reads=[r_sq, r_msq], writes=[r_msq])
        sc.op("dve", lambda e: e.tensor_scalar(out=msq_t[:], in0=msq_t[:], scalar1=EPS, scalar2=None, op0=ALU.add), reads=[r_msq], writes=[r_msq])
        sc.op("act", lambda e: e.activation(out=rstd_t[:], in_=msq_t[:], func=AF.Ln), reads=[r_msq], writes=[r_rstd])
        sc.op("act", lambda e: e.activation(out=rstd_t[:], in_=rstd_t[:], func=AF.Exp, scale=-0.5), reads=[r_rstd], writes=[r_rstd])
        mr_t, r_mr = mean_t, r_mean
        sc.op("dve", lambda e: e.tensor_tensor(out=mr_t[:], in0=mean_t[:], in1=rstd_t[:], op=ALU.mult), reads=[r_mean, r_rstd], writes=[r_mr])
        yield
        for c in range(8):
            sc.op("dve", lambda e: e.tensor_tensor(out=buf[:, c, :], in0=buf[:, c, :], in1=rstd_t[:], op=ALU.mult), reads=[rbuf, r_rstd], writes=[rbuf])
            yield
            sc.op("dve", lambda e: e.tensor_tensor(out=buf[:, c, :], in0=buf[:, c, :], in1=mr_t[:], op=ALU.subtract), reads=[rbuf, r_mr], writes=[rbuf])
            sc.op("act", lambda e: e.activation(out=buf[:, c, :], in_=buf[:, c, :], func=AF.Identity,
                                                scale=lnp[:, gi, layer, c:c + 1], bias=lnp[:, bi, layer, c:c + 1]),
                  reads=[rbuf, r_const], writes=[rbuf])
            yield

    def ln_tmps(es):
        rb_ring = Ring([(sb(es, f"ln_rb{i}", [128, TG], BF16), Res(f"ln_rb{i}")) for i in range(2)])
        rs_ring = Ring([(sb(es, f"ln_rs{i}", [128, TG], BF16), Res(f"ln_rs{i}")) for i in range(2)])
        t = [(sb(es, f"ln_t{i}", [128, TG], F32), Res(f"ln_t{i}")) for i in range(3)]
        return (rb_ring, rs_ring, t[0], t[1], t[2], t[0])

    def load_w(dst, src_ap, rlist, nparts, axis_len, q="pool"):
        step = axis_len // nparts
        for i in range(nparts):
            sc.dma(q, dst[:, :, i * step:(i + 1) * step], src_ap[:, :, i * step:(i + 1) * step], writes=[rlist[i]])

    es01 = ExitStack()
    cqn = sb(es01, "cqn", [128, 2, S], BF16)
    ckvn = sb(es01, "ckvn", [128, 2, S], BF16)
    krot = sb(es01, "krot", [96, S], BF16)
    cosT = sb(es01, "cosT", [96, S], F32)
    sinT = sb(es01, "sinT", [96, S], F32)
    r_cqn = [Res(f"cqn{g}") for g in range(NG)]
    r_ckvn = [Res(f"ckvn{g}") for g in range(NG)]
    r_krot = [Res(f"krot{g}") for g in range(NG)]
    r_trig = Res("trig")

    es = ExitStack()
    w_in = sb(es, "w_in", [128, 8, EVEN_IN], BF16)
    w_kr = sb(es, "w_kr", [128, 8, 2, 96], BF16)
    wsg = sb(es, "wsg", [128, 4, 128], BF16)
    wsg_f = sb(es, "wsg_f", [128, 4, 128], F32)
    lnG = sb(es, "lnG", [128, 512], F32)
    lnB = sb(es, "lnB", [128, 512], F32)
    bsg = sb(es, "bsg", [1, 512], BF16)
    gq = sb(es, "gq", [128, 2], F32)
    gkv = sb(es, "gkv", [128, 2], F32)
    invf = sb(es, "invf", [128, 1], F32)
    r_win = [Res(f"w_in{i}") for i in range(2)]
    r_p0c = Res("p0c")
    w_in_v = w_in_e.rearrange("(k p) n -> p k n", p=128)
    sc.dma("pool", w_in[:, :, 0:544], w_in_v[:, :, 0:544], writes=[r_win[0]])
    sc.dma("pool", w_in[:, :, 544:EVEN_IN], w_in_v[:, :, 544:EVEN_IN], writes=[r_win[1]])
    sc.dma("sp", wsg_f[:], sgu_wT, writes=[r_p0c])
    sc.dma("sp", lnG[:], bass.AP(sgu_g.tensor, 0, [[0, 128], [1, 512]]), writes=[r_p0c])
    sc.dma("sp", lnB[:], bass.AP(sgu_bb.tensor, 0, [[0, 128], [1, 512]]), writes=[r_p0c])
    sc.dma("pool", bsg[:], sgu_b, writes=[r_p0c])
    sc.dma("sp", gq[:], gq_in, writes=[r_p0c])
    sc.dma("sp", gkv[:], gkv_in, writes=[r_p0c])
    sc.dma("sp", invf[:], invf_in, writes=[r_p0c])
    for g4 in range(4):
        sc.op("dve", lambda e: e.tensor_tensor(out=wsg[:, g4, :], in0=wsg_f[:, g4, :], in1=mask_f[:], op=ALU.mult), reads=[r_p0c, r_const], writes=[r_p0c])
    r_wkr = Res("w_kr")
    sc.op("pool", lambda e: e.memset(w_kr[:], 0.0), writes=[r_wkr])
    sc.op("pool", lambda e: e.tensor_copy(out=w_kr[:, :, 0, 64:96], in_=w_in[:, :, 512:544]), reads=[r_win[0]], writes=[r_wkr])
    sc.op("pool", lambda e: e.tensor_scalar(out=w_kr[:, :, 1, 64:80], in0=w_in[:, :, 528:544], scalar1=-1.0, scalar2=None, op0=ALU.mult), reads=[r_win[0]], writes=[r_wkr])
    sc.op("pool", lambda e: e.tensor_copy(out=w_kr[:, :, 1, 80:96], in_=w_in[:, :, 512:528]), reads=[r_win[0]], writes=[r_wkr])

    if stop_after == -3:
        sc.barrier()
        es.close()
        es01.close()
        return _finish(nc, sc, top)
    es_trig = ExitStack()
    posi = sb(es_trig, "posi", [96, S], I32)
    ang = sb(es_trig, "ang", [96, S], F32)
    kf = sb(es_trig, "kf", [96, S], F32)
    r_tr = Res("trtmp")
    P = slice(64, 96)
    sc.dma("sp", posi[P, :], bass.AP(pos.tensor, 0, [[0, 32], [1, S]]), writes=[r_tr])
    sc.op("dve", lambda e: e.tensor_copy(out=ang[P, :], in_=posi[P, :]), reads=[r_tr], writes=[r_tr])
    sc.op("dve", lambda e: e.tensor_scalar(out=ang[P, :], in0=ang[P, :], scalar1=invf[P, 0:1], scalar2=None, op0=ALU.mult), reads=[r_tr, r_p0c], writes=[r_tr])
    sc.op("dve", lambda e: e.tensor_scalar(out=kf[P, :], in0=ang[P, :], scalar1=float(1.0 / (2 * PI)), scalar2=None, op0=ALU.mult), reads=[r_tr], writes=[r_tr])
    sc.op("dve", lambda e: e.tensor_copy(out=posi[P, :], in_=kf[P, :]), reads=[r_tr], writes=[r_tr])
    sc.op("dve", lambda e: e.tensor_copy(out=kf[P, :], in_=posi[P, :]), reads=[r_tr], writes=[r_tr])
    C1 = 6.28125
    C2 = float(2 * PI - C1)
    sc.op("dve", lambda e: e.scalar_tensor_tensor(out=ang[P, :], in0=kf[P, :], scalar=-C1, in1=ang[P, :], op0=ALU.mult, op1=ALU.add), reads=[r_tr], writes=[r_tr])
    sc.op("dve", lambda e: e.scalar_tensor_tensor(out=ang[P, :], in0=kf[P, :], scalar=-C2, in1=ang[P, :], op0=ALU.mult, op1=ALU.add), reads=[r_tr], writes=[r_tr])

    def wrap(t):
        sc.op("dve", lambda e: e.tensor_scalar(out=kf[P, :], in0=t[P, :], scalar1=float(PI), scalar2=float(2 * PI), op0=ALU.is_gt, op1=ALU.mult), reads=[r_tr], writes=[r_tr])
        sc.op("dve", lambda e: e.tensor_tensor(out=t[P, :], in0=t[P, :], in1=kf[P, :], op=ALU.subtract), reads=[r_tr], writes=[r_tr])
        sc.op("dve", lambda e: e.tensor_scalar(out=kf[P, :], in0=t[P, :], scalar1=float(-PI), scalar2=float(2 * PI), op0=ALU.is_lt, op1=ALU.mult), reads=[r_tr], writes=[r_tr])
        sc.op("dve", lambda e: e.tensor_tensor(out=t[P, :], in0=t[P, :], in1=kf[P, :], op=ALU.add), reads=[r_tr], writes=[r_tr])

    wrap(ang)
    sc.op("act", lambda e: e.activation(out=sinT[P, :], in_=ang[P, :], func=AF.Sin), reads=[r_tr], writes=[r_trig])
    sc.op("dve", lambda e: e.tensor_scalar(out=ang[P, :], in0=ang[P, :], scalar1=float(PI / 2), scalar2=None, op0=ALU.add), reads=[r_tr], writes=[r_tr])
    wrap(ang)
    sc.op("act", lambda e: e.activation(out=cosT[P, :], in_=ang[P, :], func=AF.Sin), reads=[r_tr], writes=[r_trig])
    sc.barrier()
    es_trig.close()

    if stop_after == -2:
        sc.dma("sp", out[0:32, :], sinT[P, :], reads=[r_trig])
        sc.dma("sp", out[32:64, :], cosT[P, :], reads=[r_trig])
        sc.barrier()
        es.close()
        es01.close()
        return _finish(nc, sc, top)
    xb_ring = Ring([(sb(es, f"xb{i}", [128, 8, TG], BF16), Res(f"xb{i}")) for i in range(2)])
    cq_ring = Ring([(sb(es, f"cq{i}", [128, 2, TG], F32), Res(f"cq{i}")) for i in range(2)])
    sq_ring = Ring([(sb(es, f"sq{i}", [128, 2, TG], BF16), Res(f"sq{i}")) for i in range(2)])
    f_ring = Ring([(sb(es, f"ft{i}", [128, TG], F32), Res(f"ft{i}")) for i in range(12)])
    gu_ring = Ring([(sb(es, f"gu{i}", [128, 4, TG], F32), [Res(f"gu{i}_{j}") for j in range(4)]) for i in range(2)])
    bo_ring = Ring([(sb(es, f"bo{i}", [128, 4, TG], BF16), Res(f"bo{i}")) for i in range(2)])
    vnb_ring = Ring([(sb(es, f"vnb{i}", [128, TG], BF16), Res(f"vnb{i}")) for i in range(2)])
    st_ring = Ring([(sb(es, f"bst{i}", [128, 8], F32), Res(f"bst{i}")) for i in range(2)])
    psA = psring([0, 1, 2, 3])
    psB = psring([4, 5])
    psC = psring([6, 7])
    GC = float(math.sqrt(0.044715))
    GS = float(2.0 * math.sqrt(2.0 / PI))

    def gelu_from_psum(pt, rpt, dst_ap, rdst, rows=slice(0, 128)):
        t1, rt1 = f_ring.next()
        sc.op("act", lambda e: e.activation(out=t1[:], in_=pt[:, :], func=AF.Square, scale=GC), reads=[rpt], writes=[rt1])
        sc.op("dve", lambda e: e.scalar_tensor_tensor(out=t1[:], in0=t1[:], scalar=1.0, in1=pt[:, :], op0=ALU.add, op1=ALU.mult), reads=[rt1, rpt], writes=[rt1])
        sc.op("act", lambda e: e.activation(out=t1[:], in_=t1[:], func=AF.Exp, scale=-GS), reads=[rt1], writes=[rt1])
        sc.op("act", lambda e: e.activation(out=t1[:], in_=t1[:], func=AF.Ln, bias=1.0), reads=[rt1], writes=[rt1])
        sc.op("act", lambda e: e.activation(out=t1[:], in_=t1[:], func=AF.Exp, scale=-1.0), reads=[rt1], writes=[rt1])
        sc.op("dve", lambda e: e.tensor_tensor(out=dst_ap, in0=t1[:], in1=pt[:, :], op=ALU.mult), reads=[rt1, rpt], writes=[rdst])

    def rms_feat(cq_t, rcq, sq_t, rsq, gvec, dst, rdst, gcols):
        pss, rpss = psB.next()
        for j in range(2):
            sc.op("pe", lambda e: e.matmul(pss[:, :], lhsT=ones_bf[:], rhs=sq_t[:, j, :], start=(j == 0), stop=(j == 1)), reads=[rsq, r_const], writes=[rpss])
        t1, rt1 = f_ring.next()
        sc.op("dve", lambda e: e.tensor_scalar(out=t1[:], in0=pss[:, :], scalar1=1.0 / 256, scalar2=EPS, op0=ALU.mult, op1=ALU.add), reads=[rpss], writes=[rt1])
        sc.op("act", lambda e: e.activation(out=t1[:], in_=t1[:], func=AF.Ln), reads=[rt1], writes=[rt1])
        sc.op("act", lambda e: e.activation(out=t1[:], in_=t1[:], func=AF.Exp, scale=-0.5), reads=[rt1], writes=[rt1])
        for j in range(2):
            sc.op("dve", lambda e: e.scalar_tensor_tensor(out=dst[:, j, gcols], in0=cq_t[:, j, :], scalar=gvec[:, j:j + 1], in1=t1[:], op0=ALU.mult, op1=ALU.mult),
                  reads=[rcq, rt1, r_p0c], writes=[rdst])

    xT_v = fm(xT)
    mix_v = fm(mixin)
    gctx = {}
    fpool = FreePool(f_ring.items)
    cqpool = FreePool(cq_ring.items)
    sqpool = FreePool(sq_ring.items)
    stpool = FreePool(st_ring.items)
    vnbpool = FreePool(vnb_ring.items)
    pA = FreePool(psA.items)
    pB = FreePool(psB.items)
    pC = FreePool(psC.items)

    def g_setup(g):
        xb, rxb = xb_ring.next()
        sc.dma("pool", xb[:], xT_v[:, :, gsl(g)], writes=[rxb])
        gu, rgu = gu_ring.next()
        bo, rbo = bo_ring.next()
        gctx[g] = dict(xb=xb, rxb=rxb, gu=gu, rgu=rgu, bo=bo, rbo=rbo, u_done=0, v_done=0, done=0)

    def gelu_gen(pt, rpt, dst_ap, rdst):
        while not fpool.avail():
            yield
        t1, rt1 = it = fpool.get()
        sc.op("act", lambda e: e.activation(out=t1[:], in_=pt[:, :], func=AF.Square, scale=GC), reads=[rpt], writes=[rt1])
        yield
        sc.op("dve", lambda e: e.scalar_tensor_tensor(out=t1[:], in0=t1[:], scalar=1.0, in1=pt[:, :], op0=ALU.add, op1=ALU.mult), reads=[rt1, rpt], writes=[rt1])
        yield
        sc.op("act", lambda e: e.activation(out=t1[:], in_=t1[:], func=AF.Exp, scale=-GS), reads=[rt1], writes=[rt1])
        yield
        sc.op("act", lambda e: e.activation(out=t1[:], in_=t1[:], func=AF.Ln, bias=1.0), reads=[rt1], writes=[rt1])
        yield
        sc.op("act", lambda e: e.activation(out=t1[:], in_=t1[:], func=AF.Exp, scale=-1.0), reads=[rt1], writes=[rt1])
        yield
        sc.op("dve", lambda e: e.tensor_tensor(out=dst_ap, in0=t1[:], in1=pt[:, :], op=ALU.mult), reads=[rt1, rpt], writes=[rdst])
        fpool.put(it)
        yield

    def gen_cq(g, which):
        c = gctx[g]
        xb, rxb = c["xb"], c["rxb"]
        gc = gsl(g)
        while not (cqpool.avail() and sqpool.avail()):
            yield
        cq_t, rcq = icq = cqpool.get()
        sq_t, rsq = isq = sqpool.get()
        for j in range(2):
            col = (which * 2 + j) * 128
            while not pA.avail():
                yield
            pt, rpt = ipt = pA.get()
            for k in range(8):
                sc.op("pe", lambda e: e.matmul(pt[:, :], lhsT=w_in[:, k, col:col + 128], rhs=xb[:, k, :], start=(k == 0), stop=(k == 7)),
                      reads=[r_win[0], rxb], writes=[rpt], inc=(k == 7))
            yield
            sc.op("act", lambda e: e.activation(out=sq_t[:, j, :], in_=pt[:, :], func=AF.Square), reads=[rpt], writes=[rsq])
            yield
            sc.op("dve", lambda e: e.tensor_copy(out=cq_t[:, j, :], in_=pt[:, :]), reads=[rpt], writes=[rcq])
            pA.put(ipt)
            yield
        gvec, dst, rdst = (gq, cqn, r_cqn[g]) if which == 0 else (gkv, ckvn, r_ckvn[g])
        while not (pB.avail() and fpool.avail()):
            yield
        pss, rpss = ipss = pB.get()
        t1, rt1 = it1 = fpool.get()
        for j in range(2):
            sc.op("pe", lambda e: e.matmul(pss[:, :], lhsT=ones_bf[:], rhs=sq_t[:, j, :], start=(j == 0), stop=(j == 1)), reads=[rsq, r_const], writes=[rpss])
        sqpool.put(isq)
        yield
        sc.op("dve", lambda e: e.tensor_scalar(out=t1[:], in0=pss[:, :], scalar1=1.0 / 256, scalar2=EPS, op0=ALU.mult, op1=ALU.add), reads=[rpss], writes=[rt1])
        pB.put(ipss)
        yield
        sc.op("act", lambda e: e.activation(out=t1[:], in_=t1[:], func=AF.Ln), reads=[rt1], writes=[rt1])
        yield
        sc.op("act", lambda e: e.activation(out=t1[:], in_=t1[:], func=AF.Exp, scale=-0.5), reads=[rt1], writes=[rt1])
        yield
        for j in range(2):
            sc.op("dve", lambda e: e.scalar_tensor_tensor(out=dst[:, j, gc], in0=cq_t[:, j, :], scalar=gvec[:, j:j + 1], in1=t1[:], op0=ALU.mult, op1=ALU.mult),
                  reads=[rcq, rt1, r_p0c], writes=[rdst])
            yield
        cqpool.put(icq)
        fpool.put(it1)

    def gen_krope(g):
        c = gctx[g]
        xb, rxb = c["xb"], c["rxb"]
        gc = gsl(g)
        while not (pA.avail(2) and fpool.avail(2)):
            yield
        pa, rpa = ipa = pA.get()
        pb, rpb = ipb = pA.get()
        t1, rt1 = it1 = fpool.get()
        t2, rt2 = it2 = fpool.get()
        for (pp, rpp, v) in ((pa, rpa, 0), (pb, rpb, 1)):
            for k in range(8):
                sc.op("pe", lambda e: e.matmul(pp[0:96, :], lhsT=w_kr[:, k, v, :], rhs=xb[:, k, :], start=(k == 0), stop=(k == 7)),
                      reads=[r_wkr, rxb], writes=[rpp], inc=(k == 7))
            yield
        sc.op("dve", lambda e: e.tensor_tensor(out=t1[P, :], in0=pa[P, :], in1=cosT[P, gc], op=ALU.mult), reads=[rpa, r_trig], writes=[rt1])
        pA.put(ipa)
        yield
        sc.op("dve", lambda e: e.tensor_tensor(out=t2[P, :], in0=pb[P, :], in1=sinT[P, gc], op=ALU.mult), reads=[rpb, r_trig], writes=[rt2])
        pA.put(ipb)
        yield
        sc.op("dve", lambda e: e.tensor_tensor(out=krot[P, gc], in0=t1[P, :], in1=t2[P, :], op=ALU.add), reads=[rt1, rt2], writes=[r_krot[g]])
        fpool.put(it1)
        fpool.put(it2)
        yield

    def gen_u(g, j):
        c = gctx[g]
        xb, rxb, gu, rgu = c["xb"], c["rxb"], c["gu"], c["rgu"]
        col = 544 + j * 128
        while not pA.avail():
            yield
        pt, rpt = ipt = pA.get()
        for k in range(8):
            sc.op("pe", lambda e: e.matmul(pt[:, :], lhsT=w_in[:, k, col:col + 128], rhs=xb[:, k, :], start=(k == 0), stop=(k == 7)),
                  reads=[r_win[1], rxb], writes=[rpt], inc=(k == 7))
        yield
        yield from gelu_gen(pt, rpt, gu[:, j, :], rgu[j])
        pA.put(ipt)
        c["u_done"] += 1

    def gen_v(g, tt):
        c = gctx[g]
        xb, rxb, gu, rgu, bo, rbo = c["xb"], c["rxb"], c["gu"], c["rgu"], c["bo"], c["rbo"]
        while not (pA.avail() and fpool.avail(2)):
            yield
        pv, rpv = ipv = pA.get()
        gv, rgv = igv = fpool.get()
        for k in range(8):
            sc.op("pe", lambda e: e.matmul(pv[:, :], lhsT=xb[:, k, tt * 128:(tt + 1) * 128], rhs=w_in[:, k, 1056:1568], start=(k == 0), stop=(k == 7)),
                  reads=[r_win[1], rxb], writes=[rpv], inc=(k == 7))
        yield
        yield from gelu_gen(pv, rpv, gv[:], rgv)
        pA.put(ipv)
        while not stpool.avail():
            yield
        st, rst = ist = stpool.get()
        sc.op("dve", lambda e: e.bn_stats(out=st[:, 0:6], in_=gv[:]), reads=[rgv], writes=[rst])
        yield
        sc.op("dve", lambda e: e.bn_aggr(out=st[:, 6:8], in_=st[:, 0:6]), reads=[rst], writes=[rst])
        yield
        sc.op("dve", lambda e: e.tensor_scalar(out=st[:, 7:8], in0=st[:, 7:8], scalar1=EPS, scalar2=None, op0=ALU.add), reads=[rst], writes=[rst])
        yield
        sc.op("act", lambda e: e.activation(out=st[:, 7:8], in_=st[:, 7:8], func=AF.Ln), reads=[rst], writes=[rst])
        yield
        sc.op("act", lambda e: e.activation(out=st[:, 7:8], in_=st[:, 7:8], func=AF.Exp, scale=-0.5), reads=[rst], writes=[rst])
        yield
        sc.op("dve", lambda e: e.tensor_scalar(out=gv[:], in0=gv[:], scalar1=st[:, 6:7], scalar2=st[:, 7:8], op0=ALU.subtract, op1=ALU.mult), reads=[rgv, rst], writes=[rgv])
        stpool.put(ist)
        yield
        sc.op("dve", lambda e: e.tensor_tensor(out=gv[:], in0=gv[:], in1=lnG[:], op=ALU.mult), reads=[rgv, r_p0c], writes=[rgv])
        yield
        while not (vnbpool.avail() and pC.avail()):
            yield
        vnb, rvnb = ivnb = vnbpool.get()
        pm, rpm = ipm = pC.get()
        sc.op("dve", lambda e: e.tensor_tensor(out=vnb[:], in0=gv[:], in1=lnB[:], op=ALU.add), reads=[rgv, r_p0c], writes=[rvnb])
        fpool.put(igv)
        yield
        for g4 in range(4):
            cs = slice(g4 * 128, (g4 + 1) * 128)
            sc.op("pe", lambda e: e.matmul(pm[:, cs], lhsT=vnb[:, cs], rhs=wsg[:, g4, :], start=True, stop=False), reads=[rvnb, r_p0c], writes=[rpm], inc=False)
            sc.op("pe", lambda e: e.matmul(pm[:, cs], lhsT=ones_bf[0:1, 0:128], rhs=bsg[0:1, cs], start=False, stop=True), reads=[r_const, r_p0c], writes=[rpm], inc=(g4 == 3))
        vnbpool.put(ivnb)
        yield
        while c["u_done"] < 4:
            yield
        sc.op("dve", lambda e: e.tensor_tensor(out=bo[:, :, tt * 128:(tt + 1) * 128], in0=pm[:, :].rearrange("p (a b) -> p a b", a=4), in1=gu[:, :, tt * 128:(tt + 1) * 128], op=ALU.mult),
              reads=[rpm] + rgu, writes=[rbo])
        pC.put(ipm)
        c["v_done"] += 1
        if c["v_done"] == 4:
            sc.dma("sp", mix_v[:, 4:8, gsl(g)], bo[:], reads=[rbo], writes=[r_mix[g]])
        yield

    tasks = []
    for g in range(NG):
        tasks.append(("setup", g))
        tasks += [(gen_cq, g, 0), (gen_cq, g, 1), (gen_krope, g)]
        tasks += [(gen_u, g, j) for j in range(4)]
        tasks += [(gen_v, g, tt) for tt in range(4)]
    NW0 = 5
    slots0 = [None] * NW0
    while tasks or any(sl is not None for sl in slots0):
        for i in range(NW0):
            if slots0[i] is None and tasks:
                t = tasks.pop(0)
                if t[0] == "setup":
                    g_setup(t[1])
                    t = tasks.pop(0)
                slots0[i] = t[0](*t[1:])
            if slots0[i] is not None:
                try:
                    next(slots0[i])
                except StopIteration:
                    slots0[i] = None
    sc.barrier()
    es.close()
    if stop_after <= 0:
        es01.close()
        return _finish(nc, sc, top)

    es = ExitStack()
    wq = sb(es, "wq", [128, 2, 8 * 96 + 32], BF16)
    wqs = sb(es, "wqs", [128, 2, 8 * 96 + 32], BF16)
    wkv = sb(es, "wkv", [128, 2, 8, 128], BF16)
    vaug = sb(es, "vaug", [128, NT, 8, 128], BF16)
    r_wq = Res("wq")
    r_wqs = Res("wqs")
    r_wkv = Res("wkv")
    r_vaug = [Res(f"vaug{t}") for t in range(NT)]
    r_vones = Res("vones")
    sc.op("pool", lambda e: e.memset(wq[:, :, 768:800], 0.0), writes=[r_wq])
    sc.dma("pool", wq[:, :, 0:768], w_qb.rearrange("(k p) n -> p k n", p=128), writes=[r_wq])
    sc.dma("pool", wkv[:], w_kvb.rearrange("(k p) (h d) -> p k h d", p=128, h=8), writes=[r_wkv])
    sc.op("pool", lambda e: e.memset(wqs[:], 0.0), writes=[r_wqs])
    for k in range(2):
        wq4 = wq[:, k, 0:768].rearrange("p (h d) -> p h d", h=8)
        wqs4 = wqs[:, k, 0:768].rearrange("p (h d) -> p h d", h=8)
        sc.op("pool", lambda e: e.tensor_scalar(out=wqs4[:, :, 64:80], in0=wq4[:, :, 80:96], scalar1=-1.0, scalar2=None, op0=ALU.mult), reads=[r_wq], writes=[r_wqs])
        sc.op("pool", lambda e: e.tensor_copy(out=wqs4[:, :, 80:96], in_=wq4[:, :, 64:80]), reads=[r_wq], writes=[r_wqs])
    sc.op("pool", lambda e: e.memset(vaug[:, :, :, 64:128], 1.0), writes=[r_vones])
    psP = psring([0, 1])
    psS = psring([2, 3, 4, 5])
    psO = psring([6, 7])
    psBC = psP
    negm = sb(es, "negm", [128, 128], BF16)
    r_negm = Res("negm")
    sc.op("dve", lambda e: e.tensor_scalar(out=negm[:], in0=mask_f[:], scalar1=-1.0, scalar2=30000.0, op0=ALU.add, op1=ALU.mult), reads=[r_const], writes=[r_negm])
    for tt in range(NT):
        pv, rpv = psP.next()
        ts_ = slice(tt * 128, (tt + 1) * 128)
        for k in range(2):
            sc.op("pe", lambda e: e.matmul(pv[:, :].rearrange("p (h d) -> p h d", h=8), lhsT=ckvn[:, k, ts_], rhs=wkv[:, k, :, 64:128], start=(k == 0), stop=(k == 1)),
                  reads=[r_ckvn[tt // 4], r_wkv], writes=[rpv], inc=(k == 1))
        eng = "act" if tt % 2 == 0 else "dve"
        if eng == "act":
            sc.op("act", lambda e: e.copy(out=vaug[:, tt, :, 0:64], in_=pv[:, :].rearrange("p (h d) -> p h d", h=8)), reads=[rpv], writes=[r_vaug[tt]])
        else:
            sc.op("dve", lambda e: e.tensor_copy(out=vaug[:, tt, :, 0:64], in_=pv[:, :].rearrange("p (h d) -> p h d", h=8)), reads=[rpv], writes=[r_vaug[tt]])

    qT_ring = Ring([(sb(es, f"qT{i}", [128, S], BF16), [Res(f"qT{i}_{g}") for g in range(NG)]) for i in range(2)])
    kT_ring = Ring([(sb(es, f"kT{i}", [128, S], BF16), [Res(f"kT{i}_{g}") for g in range(NG)]) for i in range(2)])
    for (t_, rl_) in qT_ring.items + kT_ring.items:
        for g in range(NG):
            sc.op("pool", lambda e: e.memset(t_[96:128, gsl(g)], 0.0), writes=[rl_[g]])
    pT_ring = Ring([(sb(es, f"pT{i}", [128, TG], BF16), Res(f"pT{i}")) for i in range(6)])
    rt_ring = Ring([(sb(es, f"rt{i}", [96, TG], F32), Res(f"rt{i}")) for i in range(4)])
    on_ring = Ring([(sb(es, f"onum{i}", [64, TG], F32), Res(f"onum{i}")) for i in range(2)])
    rd_ring = Ring([(sb(es, f"rden{i}", [65, TG], F32), Res(f"rden{i}")) for i in range(2)])
    ao_ring = Ring([(sb(es, f"ao{i}", [64, TG], BF16), Res(f"ao{i}")) for i in range(2)])

    def proj_head(h):
        qT, rq = qT_ring.next()
        kT, rk = kT_ring.next()
        for g in range(NG):
            gc = gsl(g)
            pa, rpa = psP.next()
            pb, rpb = psP.next()
            for k in range(2):
                sc.op("pe", lambda e: e.matmul(pa[:, :], lhsT=wq[:, k, h * 96:h * 96 + 128], rhs=cqn[:, k, gc], start=(k == 0), stop=(k == 1)), reads=[r_wq, r_cqn[g]], writes=[rpa], inc=(k == 1))
            for k in range(2):
                sc.op("pe", lambda e: e.matmul(pb[:, :], lhsT=wqs[:, k, h * 96:h * 96 + 128], rhs=cqn[:, k, gc], start=(k == 0), stop=(k == 1)), reads=[r_wqs, r_cqn[g]], writes=[rpb], inc=(k == 1))
            sc.op("dve", lambda e: e.tensor_copy(out=qT[0:64, gc], in_=pa[0:64, :]), reads=[rpa], writes=[rq[g]])
            t1, rt1 = rt_ring.next()
            t2, rt2 = rt_ring.next()
            sc.op("dve", lambda e: e.tensor_tensor(out=t1[P, :], in0=pa[P, :], in1=cosT[P, gc], op=ALU.mult), reads=[rpa, r_trig], writes=[rt1])
            sc.op("dve", lambda e: e.tensor_tensor(out=t2[P, :], in0=pb[P, :], in1=sinT[P, gc], op=ALU.mult), reads=[rpb, r_trig], writes=[rt2])
            sc.op("pool", lambda e: e.tensor_tensor(out=qT[P, gc], in0=t1[P, :], in1=t2[P, :], op=ALU.add), reads=[rt1, rt2], writes=[rq[g]])
            pk, rpk = psP.next()
            for k in range(2):
                sc.op("pe", lambda e: e.matmul(pk[:, :], lhsT=wkv[:, k, h, :], rhs=ckvn[:, k, gc], start=(k == 0), stop=(k == 1)), reads=[r_wkv, r_ckvn[g]], writes=[rpk], inc=(k == 1))
            sc.op("dve", lambda e: e.tensor_copy(out=kT[0:64, gc], in_=pk[0:64, :]), reads=[rpk], writes=[rk[g]])
            sc.op("pool", lambda e: e.tensor_copy(out=kT[P, gc], in_=krot[P, gc]), reads=[r_krot[g]], writes=[rk[g]])
        return (qT, rq, kT, rk)

    fin_pending = []

    def attn_head(h, qT, rq, kT, rk):
        for gq_ in range(NG):
            po, rpo = psO.next()
            nj = 4 * gq_ + 4
            pend = []

            def pv_mm(item, po=po, rpo=rpo, nj=nj):
                j, pT, rpT, c0, ncols = item
                sc.op("pe", lambda e: e.matmul(po[:, c0:TG], lhsT=vaug[:, j, h, :], rhs=pT[:, 0:ncols], start=(j == 0), stop=(j == nj - 1), skip_group_check=True),
                      reads=[r_vaug[j], r_vones, rpT], writes=[rpo], inc=True)

            for j in range(nj):
                c0 = 0 if j < 4 * gq_ else (j - 4 * gq_) * 128
                ncols = TG - c0
                ps_, rps_ = psS.next()
                q0 = gq_ * TG + c0
                diag = j >= 4 * gq_
                sc.op("pe", lambda e: e.matmul(ps_[:, 0:ncols], lhsT=kT[:, j * 128:(j + 1) * 128], rhs=qT[:, q0:q0 + ncols], start=True, stop=not diag, skip_group_check=True),
                      reads=[rk[j // 4], rq[gq_]], writes=[rps_], inc=not diag)
                if diag:
                    sc.op("pe", lambda e: e.matmul(ps_[:, 0:128], lhsT=ident_b[:], rhs=negm[:], start=False, stop=True, skip_group_check=True),
                          reads=[r_const, r_negm], writes=[rps_], inc=True)
                pT, rpT = pT_ring.next()
                sc.op("act", lambda e: e.activation(out=pT[:, 0:ncols], in_=ps_[:, 0:ncols], func=AF.Exp, scale=MLA_SCALE), reads=[rps_], writes=[rpT])
                pend.append((j, pT, rpT, c0, ncols))
                if len(pend) > 3:
                    pv_mm(pend.pop(0))
            while pend:
                pv_mm(pend.pop(0))
            rd, rrd = rd_ring.next()
            sc.op("dve", lambda e: e.reciprocal(out=rd[0:64, :], in_=po[64:128, :]), reads=[rpo], writes=[rrd])
            ao, rao = ao_ring.next()
            sc.op("dve", lambda e: e.tensor_tensor(out=ao[:, :], in0=po[0:64, :], in1=rd[0:64, :], op=ALU.mult), reads=[rpo, rrd], writes=[rao])
            sc.dma("sp", mixin[h * 64:(h + 1) * 64, gsl(gq_)], ao[:, :], reads=[rao], writes=[r_mix[gq_]])

    cur = proj_head(0)
    for h in range(8):
        nxt = proj_head(h + 1) if h + 1 < 8 else None
        attn_head(h, *cur)
        cur = nxt
    while fin_pending:
        fin_pending.pop(0)()
    sc.barrier()
    es.close()
    es01.close()
    if stop_after <= 1:
        return _finish(nc, sc, top)

    def outproj_ln_phase(es, wo, r_wo, src_rhs_loader, resid_src_v, r_resid_src, dst_v, r_dst, layer):
        pass

    es_sh = ExitStack()
    wsh = sb(es_sh, "wsh", [128, 8, 4096], BF16)
    r_wsh = [Res(f"wsh{i}") for i in range(4)]
    es_w0 = ExitStack()
    es = ExitStack()
    wo_holder = []

    def _p2a_first():
        wo_ = sb(es, "wo", [128, 8, D], BF16)
        r_wo_ = [Res("wo0"), Res("wo1")]
        load_w(wo_, w_out_e.rearrange("(k p) n -> p k n", p=128), r_wo_, 2, D)
        wo_holder.append((wo_, r_wo_))

    _p2a_first()
    wo, r_wo = wo_holder[0]
    w1_v0 = w_ff1[0].rearrange("(k p) n -> p k n", p=128)
    for i in range(4):
        sc.dma("pool", wsh[:, :, i * 1024:(i + 1) * 1024], w1_v0[:, :, i * 1024:(i + 1) * 1024], writes=[r_wsh[i]])
    pre0 = (wsh, None, r_wsh, None)
    mi_ring = Ring([(sb(es, f"mi{i}", [128, 8, TG], BF16), Res(f"mi{i}")) for i in range(4)])
    xr_ring = Ring([(sb(es, f"xr{i}", [128, 8, TG], F32), Res(f"xr{i}")) for i in range(4)])
    tms = [ln_tmps(es), ln_tmps(es)]
    psA = psring([0, 1, 2, 3])
    psLs = [psring([4, 5]), psring([6, 7])]
    hA_v = fm(hA)
    def run_rr(gens):
        gens = list(gens)
        while gens:
            for gen in list(gens):
                try:
                    next(gen)
                except StopIteration:
                    gens.remove(gen)

    p2a_bufs = {}
    p2a_in = {}

    def p2a_load(g):
        if g >= NG or g in p2a_in:
            return
        mi, rmi = mi_ring.next()
        xr, rxr = xr_ring.next()
        p2a_in[g] = (mi, rmi, xr, rxr)
        sc.dma("sp", mi[:], mix_v[:, :, gsl(g)], reads=[r_mix[g]], writes=[rmi])
        sc.dma("sp", xr[:], xT_v[:, :, gsl(g)], writes=[rxr])

    def p2a_main(g):
        gc = gsl(g)
        p2a_load(g)
        mi, rmi, xr, rxr = p2a_in[g]
        p2a_bufs[g] = (xr, rxr)
        for oc in range(8):
            pt, rpt = psA.next()
            for k in range(8):
                sc.op("pe", lambda e: e.matmul(pt[:, :], lhsT=wo[:, k, oc * 128:(oc + 1) * 128], rhs=mi[:, k, :], start=(k == 0), stop=(k == 7)),
                      reads=[r_wo[oc // 4], rmi], writes=[rpt], inc=(k == 7))
            sc.op("dve", lambda e: e.scalar_tensor_tensor(out=xr[:, oc, :], in0=xr[:, oc, :], scalar=ALPHA, in1=pt[:, :], op0=ALU.mult, op1=ALU.add),
                  reads=[rxr, rpt], writes=[rxr])
            yield
            yield
            yield

    def p2a_ln(g, sl):
        xr, rxr = p2a_bufs[g]
        yield from layer_norm_gen(es, xr, rxr, 0, 1, psLs[sl], tms[sl])
        sc.dma("pool", hA_v[:, :, gsl(g)], xr[:], reads=[rxr], writes=[r_hA[g]])
        p2a_load(g + 4)
        yield

    for g in range(min(4, NG)):
        p2a_load(g)
    ln_gens = {}

    def step_lns():
        for sl_ in list(ln_gens):
            try:
                next(ln_gens[sl_])
            except StopIteration:
                del ln_gens[sl_]

    for g in range(NG):
        m = p2a_main(g)
        while True:
            try:
                next(m)
            except StopIteration:
                break
            step_lns()
        sl = g % 2
        while sl in ln_gens:
            step_lns()
        ln_gens[sl] = p2a_ln(g, sl)
    while ln_gens:
        step_lns()
    sc.barrier()
    es.close()
    if stop_after <= 2:
        es_w0.close()
        es_sh.close()
        return _finish(nc, sc, top)

    def ffn_weights(es, layer, first=None):
        w1 = sb(es, "w1", [128, 8, DFF], BF16)
        w2 = sb(es, "w2", [128, 32, D], BF16)
        r_w1 = [Res(f"w1_{i}") for i in range(4)]
        r_w2 = [Res(f"w2_{i}") for i in range(4)]
        w1_v = w_ff1[layer].rearrange("(k p) n -> p k n", p=128)
        w2_v = w_ff2[layer].rearrange("(k p) n -> p k n", p=128)
        sc.dma("pool", w1[:, :, 0:1024], w1_v[:, :, 0:1024], writes=[r_w1[0]])
        if first is not None:
            first()
        for i in range(1, 4):
            sc.dma("pool", w1[:, :, i * 1024:(i + 1) * 1024], w1_v[:, :, i * 1024:(i + 1) * 1024], writes=[r_w1[i]])
        for i in range(4):
            sc.dma("pool", w2[:, i * 8:(i + 1) * 8, :], w2_v[:, i * 8:(i + 1) * 8, :], writes=[r_w2[i]])
        return (w1, w2, r_w1, r_w2)

    def ffn_phase(layer, src_v, r_src, dst_v, r_dst, pre=None, after_last_ffn1=None):
        es = ExitStack()
        hb = hr = None
        rhb = Res("hb")
        rhr = Res("hr")

        def alloc_io():
            nonlocal hb, hr
            hb = sb(es, "hb", [128, 8, TG], BF16)
            hr = sb(es, "hr", [128, 8, TG], F32)

        def load_first():
            sc.dma("pool", hb[:], src_v[:, :, gsl(0)], reads=[r_src[0]], writes=[rhb])
            sc.dma("sp", hr[:], src_v[:, :, gsl(0)], reads=[r_src[0]], writes=[rhr])

        if pre is None:
            w1 = sb(es, "w1", [128, 8, DFF], BF16)
            w2 = sb(es, "w2", [128, 32, D], BF16)
            alloc_io()
            r_w1 = [Res(f"w1_{i}") for i in range(4)]
            r_w2 = [Res(f"w2_{i}") for i in range(4)]
            w1_v = w_ff1[layer].rearrange("(k p) n -> p k n", p=128)
            w2_v = w_ff2[layer].rearrange("(k p) n -> p k n", p=128)
            sc.dma("pool", w1[:, :, 0:1024], w1_v[:, :, 0:1024], writes=[r_w1[0]])
            load_first()
            for i in range(1, 4):
                sc.dma("pool", w1[:, :, i * 1024:(i + 1) * 1024], w1_v[:, :, i * 1024:(i + 1) * 1024], writes=[r_w1[i]])
            for i in range(4):
                sc.dma("pool", w2[:, i * 8:(i + 1) * 8, :], w2_v[:, i * 8:(i + 1) * 8, :], writes=[r_w2[i]])
        else:
            (w1, w2, r_w1, r_w2) = pre
            if w2 is None:
                w2 = sb(es, "w2", [128, 32, D], BF16)
                r_w2 = [Res(f"w2_{i}") for i in range(4)]
                alloc_io()
                load_first()
                w2_v = w_ff2[layer].rearrange("(k p) n -> p k n", p=128)
                for i in range(4):
                    sc.dma("pool", w2[:, i * 8:(i + 1) * 8, :], w2_v[:, i * 8:(i + 1) * 8, :], writes=[r_w2[i]])
            else:
                alloc_io()
                load_first()
        a = sb(es, "a_act", [128, 32, TG], BF16)
        ra = [Res(f"a{i}") for i in range(32)]
        sq_ring = Ring([(sb(es, f"fsq{i}", [128, TG], F32), Res(f"fsq{i}")) for i in range(2)])
        tm = ln_tmps(es)
        psA = psring([0, 1, 2, 3])
        psB = psring([4, 5])
        psL = psring([6, 7])
        pending_ln = [None]

        def step_ln(drain=False):
            while pending_ln[0] is not None:
                try:
                    next(pending_ln[0])
                except StopIteration:
                    pending_ln[0] = None
                if not drain:
                    break

        def ln_store_gen_ffn(g):
            yield from layer_norm_gen(es, hr, rhr, layer, 2, psL, tm)
            sc.dma("sp", dst_v[:, :, gsl(g)], hr[:], reads=[rhr], writes=[r_dst[g]])
            if g + 1 < NG:
                sc.dma("sp", hr[:], src_v[:, :, gsl(g + 1)], reads=[r_src[g + 1]], writes=[rhr])
            yield

        for g in range(NG):
            gc = gsl(g)
            for fc in range(32):
                pt, rpt = psA.next()
                for k in range(8):
                    sc.op("pe", lambda e: e.matmul(pt[:, :], lhsT=w1[:, k, fc * 128:(fc + 1) * 128], rhs=hb[:, k, :], start=(k == 0), stop=(k == 7)),
                          reads=[r_w1[fc // 8], rhb], writes=[rpt], inc=(k == 7))
                sq, rsq = sq_ring.next()
                sc.op("act", lambda e: e.activation(out=sq[:], in_=pt[:, :], func=AF.Square), reads=[rpt], writes=[rsq])
                sc.op("dve", lambda e: e.scalar_tensor_tensor(out=a[:, fc, :], in0=pt[:, :], scalar=0.0, in1=sq[:], op0=ALU.is_gt, op1=ALU.mult),
                      reads=[rpt, rsq], writes=[ra[fc]])
                if fc >= 1:
                    step_ln()
            step_ln(drain=True)
            if g + 1 < NG:
                sc.dma("pool", hb[:], src_v[:, :, gsl(g + 1)], reads=[r_src[g + 1]], writes=[rhb])
            elif after_last_ffn1 is not None:
                after_last_ffn1()
            for oc in range(8):
                pt, rpt = psB.next()
                for fc in range(32):
                    sc.op("pe", lambda e: e.matmul(pt[:, :], lhsT=w2[:, fc, oc * 128:(oc + 1) * 128], rhs=a[:, fc, :], start=(fc == 0), stop=(fc == 31)),
                          reads=[r_w2[fc // 8], ra[fc]], writes=[rpt], inc=(fc == 31))
                sc.op("dve", lambda e: e.scalar_tensor_tensor(out=hr[:, oc, :], in0=hr[:, oc, :], scalar=ALPHA, in1=pt[:, :], op0=ALU.mult, op1=ALU.add),
                      reads=[rhr, rpt], writes=[rhr])
            pending_ln[0] = ln_store_gen_ffn(g)
        step_ln(drain=True)
        sc.barrier()
        es.close()

    hB_v = fm(hB)
    wi_v = w_in_o.rearrange("(k p) n -> p k n", p=128)

    def _load_wi():
        for i in (0, 2, 3, 1):
            sc.dma("pool", wsh[:, :, i * 1024:(i + 1) * 1024], wi_v[:, :, i * 1024:(i + 1) * 1024], writes=[r_wsh[i]])

    ffn_phase(0, hA_v, r_hA, hB_v, r_hB, pre=pre0, after_last_ffn1=_load_wi)
    es_w0.close()
    if stop_after <= 3:
        es_sh.close()
        return _finish(nc, sc, top)

    es = ExitStack()
    wi = wsh
    r_wi = r_wsh
    hb_ring = Ring([(sb(es, f"hb{i}", [128, 8, TG], BF16), Res(f"hb{i}")) for i in range(2)])
    hr = sb(es, "hr3", [128, 8, TG], F32)
    rhr = Res("hr3")
    hbs = {}
    hbs[0] = hb_ring.next()
    sc.dma("pool", hbs[0][0][:], hB_v[:, :, gsl(0)], reads=[r_hB[0]], writes=[hbs[0][1]])
    sc.dma("sp", hr[:], hB_v[:, :, gsl(0)], reads=[r_hB[0]], writes=[rhr])
    wo2 = sb(es, "wo2", [128, 8, D], BF16)
    r_wo2 = [Res("wo2_0"), Res("wo2_1")]
    load_w(wo2, w_out_o.rearrange("(k p) n -> p k n", p=128), r_wo2, 2, D)
    lbt = sb(es, "lbt", [128, 2, 8], F32)
    gn = sb(es, "gn", [128, 8], F32)
    oml = sb(es, "oml", [128, 8], F32)
    noml = sb(es, "noml", [128, 8], F32)
    r_hc = Res("hgc")
    sc.dma("sp", lbt[:], lb_in, writes=[r_hc])
    sc.dma("sp", gn[:], gn_in, writes=[r_hc])
    sc.op("dve", lambda e: e.tensor_tensor(out=oml[:], in0=lbt[:, 1, :], in1=lbt[:, 0, :], op=ALU.subtract), reads=[r_hc], writes=[r_hc])
    sc.op("act", lambda e: e.activation(out=oml[:], in_=oml[:], func=AF.Exp), reads=[r_hc], writes=[r_hc])
    sc.op("act", lambda e: e.activation(out=oml[:], in_=oml[:], func=AF.Ln, bias=1.0), reads=[r_hc], writes=[r_hc])
    sc.op("act", lambda e: e.activation(out=oml[:], in_=oml[:], func=AF.Exp, scale=-1.0), reads=[r_hc], writes=[r_hc])
    sc.op("dve", lambda e: e.tensor_scalar(out=noml[:], in0=oml[:], scalar1=-1.0, scalar2=None, op0=ALU.mult), reads=[r_hc], writes=[r_hc])
    st = sb(es, "hst", [128, 8, 128], F32)
    stb4 = sb(es, "hstb", [128, 8, 4, 128], BF16)
    r_st = [Res(f"st{h}") for h in range(8)]
    r_stb = [[Res(f"stb{h}_{i}") for i in range(4)] for h in range(8)]
    for h in range(8):
        sc.op("pool", lambda e: e.memset(st[:, h, :], 0.0), writes=[r_st[h]])
        sc.op("pool", lambda e: e.memset(stb4[:, h, 0, :], 0.0), writes=[r_stb[h][0]])
    ones_t = ones_f
    mask4 = sb(es, "mask4", [128, 4, 128], BF16)
    for i in range(4):
        sc.op("pool", lambda e: e.tensor_copy(out=mask4[:, i, :], in_=mask_b[:]), reads=[r_const], writes=[r_hc])
    NWAY = 3
    cbuf = []
    for i in range(NWAY):
        d = {}
        for nm in ("s1", "lg", "bc"):
            d[nm] = (sb(es, f"c{i}_{nm}", [128, TG], F32), Res(f"c{i}_{nm}"))
        d["eb"] = d["lg"]
        for nm in ("kt", "qt", "kd", "at4"):
            d[nm] = (sb(es, f"c{i}_{nm}", [128, TG], BF16), Res(f"c{i}_{nm}"))
        d["ktok4"] = d["kd"]
        d["sqo"] = d["at4"]
        d["banks"] = ((psum[2 * i], rps[2 * i]), (psum[2 * i + 1], rps[2 * i + 1]))
        cbuf.append(d)
    q_free = [True] * 8
    g_free = [True] * 8
    vtok_bufs = [(sb(es, f"vtok{i}", [128, 4, D], BF16), [Res(f"vtok{i}_{t}") for t in range(4)]) for i in range(2)]
    siluq = sb(es, "siluq", [128, 8, TG], BF16)
    r_sq = [Res(f"siluq{h}") for h in range(8)]
    sg = sb(es, "sg", [128, 8, TG], BF16)
    r_sg = [Res(f"sg{h}") for h in range(8)]
    onb = sb(es, "onb", [128, 8, TG], BF16)
    r_onb = [Res(f"onb{h}") for h in range(8)]
    tm = ln_tmps(es)
    psA = psring([6, 7])
    hC_v = fm(hC)

    def load_hb(g):
        if g not in hbs:
            hbs[g] = hb_ring.next()
            sc.dma("pool", hbs[g][0][:], hB_v[:, :, gsl(g)], reads=[r_hB[g]], writes=[hbs[g][1]])

    def session(g, col0, dst, rdst):
        hb, rhb = hbs[g]
        for h in range(8):
            pp, rpp = psA.next()
            c0 = col0 + h * 128
            for k in range(8):
                sc.op("pe", lambda e: e.matmul(pp[:, :], lhsT=wi[:, k, c0:c0 + 128], rhs=hb[:, k, :], start=(k == 0), stop=(k == 7)),
                      reads=[r_wi[col0 // 1024], rhb], writes=[rpp], inc=(k == 7))
            sc.op("act", lambda e: e.activation(out=dst[:, h, :], in_=pp[:, :], func=AF.Silu), reads=[rpp], writes=[rdst[h]])

    def vproj_gen(g):
        hb, rhb = hbs[g]
        vtok, rvt = vtok_bufs[g % 2]
        for tt in range(4):
            for half in range(2):
                pv, rpv = psA.next()
                c0 = 2048 + half * 512
                for k in range(8):
                    sc.op("pe", lambda e: e.matmul(pv[:, :], lhsT=hb[:, k, tt * 128:(tt + 1) * 128], rhs=wi[:, k, c0:c0 + 512], start=(k == 0), stop=(k == 7)),
                          reads=[r_wi[2], rhb], writes=[rpv], inc=(k == 7))
                if half == 0:
                    sc.op("act", lambda e: e.copy(out=vtok[:, tt, 0:512], in_=pv[:, :]), reads=[rpv], writes=[rvt[tt]])
                else:
                    sc.op("dve", lambda e: e.tensor_copy(out=vtok[:, tt, 512:1024], in_=pv[:, :]), reads=[rpv], writes=[rvt[tt]])
                yield

    def vproj(g):
        for _ in vproj_gen(g):
            pass

    def chain(g, h, slot):
        B = cbuf[slot]
        hb, rhb = hbs[g]
        vtok, rvt = vtok_bufs[g % 2]
        (b0, rb0), (b1, rb1) = B["banks"]
        (s1, rs1), (lg, rlg), (bc, rbc), (eb, reb) = B["s1"], B["lg"], B["bc"], B["eb"]
        (kt, rkt), (qt, rqt), (kd, rkd) = B["kt"], B["qt"], B["kd"]
        (ktok4, rk4), (at4, rat), (sqo, rsqo) = B["ktok4"], B["at4"], B["sqo"]
        hs = slice(h * 128, (h + 1) * 128)
        TS = [slice(tt * 128, (tt + 1) * 128) for tt in range(4)]
        for k in range(8):
            sc.op("pe", lambda e: e.matmul(b0[:, :], lhsT=wi[:, k, 1024 + h * 128:1024 + (h + 1) * 128], rhs=hb[:, k, :], start=(k == 0), stop=(k == 7)),
                  reads=[r_wi[1], rhb], writes=[rb0], inc=(k == 7))
        yield
        sc.op("act", lambda e: e.activation(out=s1[:], in_=b0[:, :], func=AF.Exp), reads=[rb0], writes=[rs1])
        yield
        sc.op("act", lambda e: e.activation(out=s1[:], in_=s1[:], func=AF.Ln, bias=1.0), reads=[rs1], writes=[rs1])
        yield
        sc.op("act", lambda e: e.activation(out=s1[:], in_=s1[:], func=AF.Exp, scale=-1.0), reads=[rs1], writes=[rs1])
        yield
        sc.op("act", lambda e: e.activation(out=lg[:], in_=s1[:], func=AF.Ln, scale=noml[:, h:h + 1], bias=1.0), reads=[rs1, r_hc], writes=[rlg])
        yield
        for tt in range(4):
            sc.op("dve", lambda e: e.tensor_tensor_scan(out=bc[:, TS[tt]], data0=ones_t[:], data1=lg[:, TS[tt]], initial=0.0, op0=ALU.mult, op1=ALU.add),
                  reads=[rlg, r_hc], writes=[rbc])
        yield
        sc.op("act", lambda e: e.activation(out=eb[:], in_=bc[:], func=AF.Exp), reads=[rbc], writes=[reb])
        yield
        sc.op("act", lambda e: e.activation(out=bc[:], in_=bc[:], func=AF.Exp, scale=-1.0), reads=[rbc], writes=[rbc])
        yield
        sc.op("dve", lambda e: e.scalar_tensor_tensor(out=kt[:], in0=s1[:], scalar=oml[:, h:h + 1], in1=bc[:], op0=ALU.mult, op1=ALU.mult), reads=[rs1, rbc, r_hc], writes=[rkt])
        yield
        sc.op("dve", lambda e: e.tensor_tensor(out=qt[:], in0=siluq[:, h, :], in1=eb[:], op=ALU.mult), reads=[r_sq[h], reb], writes=[rqt])
        q_free[h] = True
        yield
        eb_last = bass.AP(eb[:].tensor, 127, [[TG, 128], [128, 4], [0, 128]])
        sc.op("dve", lambda e: e.tensor_tensor(out=kd[:].rearrange("p (a b) -> p a b", a=4), in0=kt[:].rearrange("p (a b) -> p a b", a=4), in1=eb_last, op=ALU.mult),
              reads=[rkt, reb], writes=[rkd])
        yield
        for tt in range(4):
            sc.op("pe", lambda e: e.matmul(b0[:, TS[tt]], lhsT=kd[:, TS[tt]], rhs=ident_b[:], start=True, stop=True), reads=[rkd, r_const], writes=[rb0], inc=(tt == 3))
        yield
        sc.op("act", lambda e: e.copy(out=ktok4[:], in_=b0[:, :]), reads=[rb0], writes=[rk4])
        yield
        for tt in range(4):
            sc.op("pe", lambda e: e.matmul(b1[:, TS[tt]], lhsT=kt[:, TS[tt]], rhs=qt[:, TS[tt]], start=True, stop=True), reads=[rkt, rqt], writes=[rb1], inc=(tt == 3))
        yield
        sc.op("dve", lambda e: e.tensor_tensor(out=at4[:], in0=b1[:, :], in1=mask4[:].rearrange("p a b -> p (a b)"), op=ALU.mult), reads=[rb1, r_hc], writes=[rat])
        yield
        for tt in range(4):
            sc.op("pe", lambda e: e.matmul(b0[:, TS[tt]], lhsT=ktok4[:, TS[tt]], rhs=vtok[:, tt, hs], start=True, stop=True), reads=[rk4, rvt[tt]], writes=[rb0], inc=(tt == 3))
        yield
        for tt in range(4):
            last = tt * 128 + 127
            sc.op("dve", lambda e: e.scalar_tensor_tensor(out=st[:, h, :], in0=st[:, h, :], scalar=eb[:, last:last + 1], in1=b0[:, TS[tt]], op0=ALU.mult, op1=ALU.add),
                  reads=[r_st[h], reb, rb0], writes=[r_st[h]])
            yield
            if tt < 3:
                sc.op("dve", lambda e: e.tensor_copy(out=stb4[:, h, tt + 1, :], in_=st[:, h, :]), reads=[r_st[h]], writes=[r_stb[h][tt + 1]])
                yield
        for tt in range(4):
            sc.op("pe", lambda e: e.matmul(b1[:, TS[tt]], lhsT=vtok[:, tt, hs], rhs=at4[:, TS[tt]], start=True, stop=False), reads=[rvt[tt], rat], writes=[rb1], inc=False)
            sc.op("pe", lambda e: e.matmul(b1[:, TS[tt]], lhsT=stb4[:, h, tt, :], rhs=qt[:, TS[tt]], start=False, stop=True), reads=[r_stb[h][tt], rqt], writes=[rb1], inc=True)
        yield
        sc.op("dve", lambda e: e.tensor_copy(out=stb4[:, h, 0, :], in_=st[:, h, :]), reads=[r_st[h]], writes=[r_stb[h][0]])
        sc.op("act", lambda e: e.activation(out=sqo[:], in_=b1[:, :], func=AF.Square), reads=[rb1], writes=[rsqo])
        yield
        sc.op("pe", lambda e: e.matmul(b0[:, :], lhsT=ones_bf[:], rhs=sqo[:], start=True, stop=True), reads=[rsqo, r_const], writes=[rb0], inc=True)
        yield
        rn, rrn = s1, rs1
        sc.op("dve", lambda e: e.tensor_scalar(out=rn[:], in0=b0[:, :], scalar1=1.0 / 128, scalar2=EPS, op0=ALU.mult, op1=ALU.add), reads=[rb0], writes=[rrn])
        yield
        sc.op("act", lambda e: e.activation(out=rn[:], in_=rn[:], func=AF.Ln), reads=[rrn], writes=[rrn])
        yield
        sc.op("act", lambda e: e.activation(out=rn[:], in_=rn[:], func=AF.Exp, scale=-0.5), reads=[rrn], writes=[rrn])
        yield
        sc.op("dve", lambda e: e.scalar_tensor_tensor(out=rn[:], in0=b1[:, :], scalar=gn[:, h:h + 1], in1=rn[:], op0=ALU.mult, op1=ALU.mult), reads=[rb1, rrn, r_hc], writes=[rrn])
        yield
        sc.op("dve", lambda e: e.tensor_tensor(out=onb[:, h, :], in0=rn[:], in1=sg[:, h, :], op=ALU.mult), reads=[rrn, r_sg[h]], writes=[r_onb[h]])
        g_free[h] = True
        yield

    def session_item(g, col0, dst, rdst, h):
        hb, rhb = hbs[g]
        pp, rpp = psA.next()
        c0 = col0 + h * 128
        for k in range(8):
            sc.op("pe", lambda e: e.matmul(pp[:, :], lhsT=wi[:, k, c0:c0 + 128], rhs=hb[:, k, :], start=(k == 0), stop=(k == 7)),
                  reads=[r_wi[col0 // 1024], rhb], writes=[rpp], inc=(k == 7))
        sc.op("act", lambda e: e.activation(out=dst[:, h, :], in_=pp[:, :], func=AF.Silu), reads=[rpp], writes=[rdst[h]])

    def outproj_gen(g):
        for oc in range(8):
            pt, rpt = psA.next()
            for h in range(8):
                sc.op("pe", lambda e: e.matmul(pt[:, :], lhsT=wo2[:, h, oc * 128:(oc + 1) * 128], rhs=onb[:, h, :], start=(h == 0), stop=(h == 7)),
                      reads=[r_wo2[oc // 4], r_onb[h]], writes=[rpt], inc=(h == 7))
            sc.op("dve", lambda e: e.scalar_tensor_tensor(out=hr[:, oc, :], in0=hr[:, oc, :], scalar=ALPHA, in1=pt[:, :], op0=ALU.mult, op1=ALU.add),
                  reads=[rhr, rpt], writes=[rhr])
            yield

    def ln_store_gen(g):
        yield from layer_norm_gen(es, hr, rhr, 1, 1, psA, tm)
        sc.dma("sp", hC_v[:, :, gsl(g)], hr[:], reads=[rhr], writes=[r_hC[g]])
        if g + 1 < NG:
            sc.dma("sp", hr[:], hB_v[:, :, gsl(g + 1)], reads=[r_hB[g + 1]], writes=[rhr])
        yield

    state = {"chains_done": False}

    def bulk_gen(g):
        if g > 0:
            yield from outproj_gen(g - 1)
            yield from ln_store_gen(g - 1)
        if g + 1 < NG:
            yield from vproj_gen(g + 1)
            pending = [("q", h) for h in range(8)] + [("g", h) for h in range(8)]
            while pending:
                elig = [it for it in pending if (q_free[it[1]] if it[0] == "q" else g_free[it[1]])]
                if len(elig) >= 4 or (state["chains_done"] and elig):
                    for it in elig[:4]:
                        if it[0] == "q":
                            session_item(g + 1, 0, siluq, r_sq, it[1])
                        else:
                            session_item(g + 1, 3072, sg, r_sg, it[1])
                        pending.remove(it)
                        yield
                else:
                    yield

    for h in range(8):
        session_item(0, 0, siluq, r_sq, h)
    vproj(0)
    for h in range(8):
        session_item(0, 3072, sg, r_sg, h)
    for g in range(NG):
        if g + 1 < NG:
            load_hb(g + 1)
        for h in range(8):
            q_free[h] = False
            g_free[h] = False
        state["chains_done"] = False
        heads = list(range(8))
        slots = [None] * NWAY
        bulk = bulk_gen(g)
        bulk_alive = True
        rnd = 0
        while True:
            active = False
            for i in range(NWAY):
                if slots[i] is None and heads:
                    slots[i] = chain(g, heads.pop(0), i)
                if slots[i] is not None:
                    active = True
                    try:
                        next(slots[i])
                    except StopIteration:
                        slots[i] = None
            if not active and not heads:
                state["chains_done"] = True
            rnd += 1
            if bulk_alive and (rnd % 3 != 2 or state["chains_done"]):
                try:
                    next(bulk)
                except StopIteration:
                    bulk_alive = False
            if state["chains_done"] and not bulk_alive:
                break
    w1_v1 = w_ff1[1].rearrange("(k p) n -> p k n", p=128)
    for i in range(4):
        sc.dma("pool", wsh[:, :, i * 1024:(i + 1) * 1024], w1_v1[:, :, i * 1024:(i + 1) * 1024], writes=[r_wsh[i]])
    for _ in outproj_gen(NG - 1):
        pass
    for _ in ln_store_gen(NG - 1):
        pass
    sc.barrier()
    es.close()
    if stop_after <= 4:
        es_sh.close()
        return _finish(nc, sc, top)

    out_v = fm(out)
    r_out = [Res(f"out{g}") for g in range(NG)]
    ffn_phase(1, hC_v, r_hC, out_v, r_out, pre=(wsh, None, r_wsh, None))
    es_sh.close()
    return _finish(nc, sc, top)


def _finish(nc, sc, top):
    sc.finish()
    top.close()
    return nc


def _prep_shared(inputs):
    f = np.float32
    half = 16
    inv_freq = (10000.0 ** (-np.arange(half, dtype=np.float32) / half)).astype(f)
    invf = np.zeros((128, 1), f)
    invf[64:80, 0] = inv_freq
    invf[80:96, 0] = inv_freq
    pidx = np.arange(128)
    mask = (pidx[None, :] >= pidx[:, None]).astype(f)
    lnp = np.stack([inputs["ln1_g"], inputs["ln1_b"], inputs["ln2_g"], inputs["ln2_b"]], 0)
    lnp = np.ascontiguousarray(lnp.reshape(4, 2, 8, 128).transpose(3, 0, 1, 2)).astype(f)
    sh = {
        "w_in_e": np.ascontiguousarray(inputs["w_in_e"][0], f),
        "w_qb": np.ascontiguousarray(inputs["w_qb"][0], f),
        "w_kvb": np.ascontiguousarray(inputs["w_kvb"][0], f),
        "w_out_e": np.ascontiguousarray(inputs["w_out_e"][0], f),
        "sgu_wT": np.ascontiguousarray(np.transpose(inputs["sgu_w"][0], (2, 0, 1)), f),
        "sgu_b": np.ascontiguousarray(inputs["sgu_b"][0].reshape(1, 512), f),
        "sgu_g": np.ascontiguousarray(inputs["sgu_ln_g"][0].reshape(1, 512), f),
        "sgu_bb": np.ascontiguousarray(inputs["sgu_ln_b"][0].reshape(1, 512), f),
        "gq": np.ascontiguousarray(inputs["mla_gq"][0].reshape(2, 128).T, f),
        "gkv": np.ascontiguousarray(inputs["mla_gkv"][0].reshape(2, 128).T, f),
        "w_in_o": np.ascontiguousarray(inputs["w_in_o"][0], f),
        "w_out_o": np.ascontiguousarray(inputs["w_out_o"][0], f),
        "hg_lb": np.ascontiguousarray(inputs["hg_lb"].reshape(2, 8, 128).transpose(2, 0, 1), f),
        "hg_gn": np.ascontiguousarray(inputs["hg_gnorm"][0].reshape(8, 128).T, f),
        "lnp": lnp,
        "w_ff1": np.ascontiguousarray(inputs["w_ff1"], f),
        "w_ff2": np.ascontiguousarray(inputs["w_ff2"], f),
        "invf": invf,
        "mask_ge": mask,
        "ident": np.eye(128, dtype=f),
    }
    return sh


_NC_CACHE = {}


def kernel(**inputs):
    inputs = {k: np.asarray(v) for k, v in inputs.items()}
    x = inputs["x"]
    B, S, _ = x.shape
    if S not in _NC_CACHE:
        _NC_CACHE[S] = build(S)
    nc = _NC_CACHE[S]
    sh = _prep_shared(inputs)
    in_maps = []
    for b in range(B):
        m = dict(sh)
        m["xT"] = np.ascontiguousarray(x[b].T)
        m["pos"] = np.ascontiguousarray(inputs["positions"][b].reshape(1, S).astype(np.int32))
        in_maps.append(m)
    res = run_bass_kernel_spmd(nc, in_maps, core_ids=list(range(B)))
    outs = [np.asarray(r["out"]).T for r in res.results]
    return np.ascontiguousarray(np.stack(outs, 0).astype(np.float32))
```
